# Optimizing a Trainium2 kernel written in Bass

```python
import math
import jax, jax.numpy as jnp
from jax import lax
import numpy as np

D_MODEL = 1024
BATCH = 16
SEQ = 2048
DEPTH = 4
DEC_BATCH = 2
DEC_SEQ = 16384
PAST_LEN = 128

BRANCH_WIDTH = D_MODEL // 2
N_BRANCHES = 3
N_POOL_GROUPS = 4
POOL_GROUP_WIDTH = BRANCH_WIDTH // N_POOL_GROUPS
POOL_WINDOWS = (2, 4, 8, 16)
RWKV_HEAD = 64
RWKV_HEADS = BRANCH_WIDTH // RWKV_HEAD
W_LORA = 64
A_LORA = 64
G_LORA = 128
RWKV_GN_EPS = 64e-5
ATT_HEAD = 64
N_Q_HEADS = BRANCH_WIDTH // ATT_HEAD
N_KV_HEADS = 2
GQA_GROUP = N_Q_HEADS // N_KV_HEADS
WINDOW = 128
BLOCK = 128
N_BUCKETS = 32
MAX_DISTANCE = 128
NEG_INF = -1e30
D_FF = 4 * D_MODEL
LN_EPS = 1e-5
DEEPNORM_ALPHA = (2 * DEPTH) ** 0.25
DEEPNORM_BETA = (8 * DEPTH) ** -0.25

OFF_POOL = 0
OFF_RWKV = OFF_POOL + BRANCH_WIDTH
RWKV_IN = 3 * BRANCH_WIDTH + 2 * W_LORA + 2 * A_LORA + G_LORA
OFF_ATT = OFF_RWKV + RWKV_IN
ATT_IN = N_Q_HEADS * ATT_HEAD + 2 * N_KV_HEADS * ATT_HEAD
OFF_GATE = OFF_ATT + ATT_IN
N_IN = OFF_GATE + N_BRANCHES * D_MODEL

kernel_name = 'hybrid_bidir_pool_rwkv7_swa_encoder'


def _layer_norm(x, g, b):
    xf = x.astype(jnp.float32)
    mu = xf.mean(-1, keepdims=True)
    var = jnp.square(xf - mu).mean(-1, keepdims=True)
    return ((xf - mu) * lax.rsqrt(var + LN_EPS) * g + b).astype(x.dtype)


def _multiscale_pool(u):
    B, S, _ = u.shape
    ug = u.reshape(B, S, N_POOL_GROUPS, POOL_GROUP_WIDTH)
    c = jnp.concatenate([jnp.zeros_like(ug[:, :1]), jnp.cumsum(ug, axis=1)], axis=1)
    t = np.arange(S)
    outs = []
    for gi, w in enumerate(POOL_WINDOWS):
        lo = np.clip(t - w // 2, 0, S - 1)
        hi = np.clip(t + w // 2 - 1, 0, S - 1)
        cnt = (hi - lo + 1).astype(np.float32)
        cg = c[:, :, gi]
        win_sum = jnp.take(cg, hi + 1, axis=1) - jnp.take(cg, lo, axis=1)
        outs.append(win_sum / cnt[None, :, None])
    return jnp.stack(outs, axis=2) - ug


def _centred_shift(u, mu):
    zeros = jnp.zeros_like(u[:, :1])
    prev = jnp.concatenate([zeros, u[:, :-1]], axis=1)
    nxt = jnp.concatenate([u[:, 1:], zeros], axis=1)
    return u + mu * (0.5 * (prev + nxt) - u)


def _rwkv7_scan(r, w, k, v, a, b):
    def step(state, inp):
        rt, wt, kt, vt, at, bt = inp
        sa = jnp.einsum('...ij,...j->...i', state, at)
        state = state * wt[..., None, :] + sa[..., :, None] * bt[..., None, :] + vt[..., :, None] * kt[..., None, :]
        return state, jnp.einsum('...ij,...j->...i', state, rt)
    s0 = jnp.zeros(r.shape[1:] + (RWKV_HEAD,), jnp.float32)
    _, ys = lax.scan(step, s0, (r, w, k, v, a, b))
    return ys


def _rwkv7_bidir(zb, mu, w0, w2, a0, a2, g2, k_k, k_a, r_k, gn_g, gn_b):
    B, S, _ = zb.shape
    C = BRANCH_WIDTH
    zb = _centred_shift(zb, mu)
    r = zb[..., 0:C]
    k = zb[..., C:2 * C]
    v = zb[..., 2 * C:3 * C]
    o = 3 * C
    wlo = jnp.tanh(zb[..., o:o + 2 * W_LORA]).reshape(B, S, 2, W_LORA)
    o += 2 * W_LORA
    alo = zb[..., o:o + 2 * A_LORA].reshape(B, S, 2, A_LORA)
    o += 2 * A_LORA
    glo = zb[..., o:o + G_LORA]
    w = -jax.nn.softplus(-(w0 + jnp.einsum('bsdr,drc->bsdc', wlo, w2))) - 0.5
    decay = jnp.exp(-jnp.exp(w))
    a = jax.nn.sigmoid(a0 + jnp.einsum('bsdr,drc->bsdc', alo, a2))
    g = jax.nn.sigmoid(glo) @ g2
    kk = (k * k_k).reshape(B, S, RWKV_HEADS, RWKV_HEAD)
    kk = kk / jnp.maximum(jnp.sqrt(jnp.sum(kk * kk, -1, keepdims=True)), 1e-12)
    kk = kk.reshape(B, S, 1, C)
    kd = k[:, :, None, :] * (1.0 + (a - 1.0) * k_a)
    a_in = jnp.broadcast_to(-kk, (B, S, 2, C))
    b_in = kk * a
    r2 = jnp.broadcast_to(r[:, :, None, :], (B, S, 2, C))
    v2 = jnp.broadcast_to(v[:, :, None, :], (B, S, 2, C))

    def to_time(t):
        t = t.reshape(B, S, 2, RWKV_HEADS, RWKV_HEAD)
        t = jnp.stack([t[:, :, 0], t[:, ::-1, 1]], axis=2)
        return jnp.transpose(t, (1, 2, 0, 3, 4))

    ys = _rwkv7_scan(to_time(r2), to_time(decay), to_time(kd), to_time(v2), to_time(a_in), to_time(b_in))
    y = ys[:, 0] + ys[::-1, 1]
    y = jnp.transpose(y, (1, 0, 2, 3))
    ym = y.mean(-1, keepdims=True)
    yv = jnp.square(y - ym).mean(-1, keepdims=True)
    yn = ((y - ym) * lax.rsqrt(yv + RWKV_GN_EPS)).reshape(B, S, C) * gn_g + gn_b
    kbar = 0.5 * (kd[:, :, 0] + kd[:, :, 1])
    bonus = jnp.sum((r * kbar).reshape(B, S, RWKV_HEADS, RWKV_HEAD) * r_k, -1, keepdims=True) \
        * v.reshape(B, S, RWKV_HEADS, RWKV_HEAD)
    return (yn + bonus.reshape(B, S, C)) * g


def _t5_bucket(rel):
    half = N_BUCKETS // 2
    max_exact = half // 2
    n = np.abs(rel)
    large = max_exact + (np.log(np.maximum(n, 1) / max_exact) / math.log(MAX_DISTANCE / max_exact)
                         * (half - max_exact)).astype(np.int32)
    large = np.minimum(large, half - 1)
    return (rel > 0).astype(np.int32) * half + np.where(n < max_exact, n, large)


def _windowed_gqa(q, k, v, rel_bias, sink):
    B, S = q.shape[:2]
    nb = S // BLOCK
    qpos = np.arange(BLOCK)[:, None]
    koff = np.arange(3 * BLOCK)[None, :] - BLOCK
    rel = koff - qpos
    bias = rel_bias[_t5_bucket(rel)]
    bias = jnp.transpose(bias, (2, 0, 1)).reshape(N_KV_HEADS, GQA_GROUP, BLOCK, 3 * BLOCK)
    kpos = np.arange(nb)[:, None] * BLOCK + koff
    mask = (np.abs(rel) <= WINDOW)[None] & ((kpos >= 0) & (kpos < S))[:, None, :]

    def band(t):
        tp = jnp.pad(t, ((0, 0), (BLOCK, BLOCK), (0, 0), (0, 0)))
        tb = tp.reshape(B, nb + 2, BLOCK, N_KV_HEADS, ATT_HEAD)
        return jnp.concatenate([tb[:, :-2], tb[:, 1:-1], tb[:, 2:]], axis=2)

    kb, vb = band(k), band(v)
    qb = q.reshape(B, nb, BLOCK, N_KV_HEADS, GQA_GROUP, ATT_HEAD)
    s = jnp.einsum('bnqhgd,bnkhd->bnhgqk', qb, kb) * (ATT_HEAD ** -0.5) + bias
    s = jnp.where(mask[None, :, None, None], s, NEG_INF)
    sk = sink.astype(jnp.float32).reshape(N_KV_HEADS, GQA_GROUP)[:, :, None, None]
    m = jnp.maximum(s.max(-1, keepdims=True), sk)
    p = jnp.exp(s - m)
    p = p / (p.sum(-1, keepdims=True) + jnp.exp(sk - m))
    o = jnp.einsum('bnhgqk,bnkhd->bnqhgd', p, vb)
    return o.reshape(B, S, N_Q_HEADS * ATT_HEAD)


def _mixer(x, rel_bias, w_in, pool_w, pool_scale, mu, w0, w2, a0, a2, g2, k_k, k_a, r_k, gn_g, gn_b,
           sink, w_branch, w_o):
    B, S, _ = x.shape
    z = (x @ w_in).astype(jnp.float32)
    pa = _multiscale_pool(z[..., OFF_POOL:OFF_RWKV])
    a_out = jnp.einsum('bsgc,gcd->bsgd', pa, pool_w).reshape(B, S, BRANCH_WIDTH) * pool_scale
    b_out = _rwkv7_bidir(z[..., OFF_RWKV:OFF_ATT], mu, w0, w2, a0, a2, g2, k_k, k_a, r_k, gn_g, gn_b)
    zq = z[..., OFF_ATT:OFF_ATT + N_Q_HEADS * ATT_HEAD].reshape(B, S, N_Q_HEADS, ATT_HEAD)
    ok = OFF_ATT + N_Q_HEADS * ATT_HEAD
    zk = z[..., ok:ok + N_KV_HEADS * ATT_HEAD].reshape(B, S, N_KV_HEADS, ATT_HEAD)
    ov = ok + N_KV_HEADS * ATT_HEAD
    zv = z[..., ov:ov + N_KV_HEADS * ATT_HEAD].reshape(B, S, N_KV_HEADS, ATT_HEAD)
    c_out = _windowed_gqa(zq, zk, zv, rel_bias, sink)
    gates = jax.nn.sigmoid(z[..., OFF_GATE:].reshape(B, S, N_BRANCHES, D_MODEL))
    merged = gates[:, :, 0] * (a_out @ w_branch[0])
    merged = merged + gates[:, :, 1] * (b_out @ w_branch[1])
    merged = merged + gates[:, :, 2] * (c_out @ w_branch[2])
    return (merged @ w_o).astype(x.dtype)


def _trunk(x, rel_bias, w_in, pool_w, pool_scale, rwkv_mu, rwkv_w0, rwkv_w2, rwkv_a0, rwkv_a2, rwkv_g2,
           rwkv_k_k, rwkv_k_a, rwkv_r_k, rwkv_gn_g, rwkv_gn_b, attn_sink, w_branch, w_o,
           ln1_g, ln1_b, w_ff1, w_ff2, ln2_g, ln2_b):
    for l in range(DEPTH):
        h = _mixer(x, rel_bias, w_in[l], pool_w[l], pool_scale[l], rwkv_mu[l], rwkv_w0[l], rwkv_w2[l],
                   rwkv_a0[l], rwkv_a2[l], rwkv_g2[l], rwkv_k_k[l], rwkv_k_a[l], rwkv_r_k[l],
                   rwkv_gn_g[l], rwkv_gn_b[l], attn_sink[l], w_branch[l], w_o[l])
        x = _layer_norm(DEEPNORM_ALPHA * x + h, ln1_g[l], ln1_b[l])
        f = jnp.square(jax.nn.relu(x @ w_ff1[l])) @ w_ff2[l]
        x = _layer_norm(DEEPNORM_ALPHA * x + f, ln2_g[l], ln2_b[l])
    return x


def setup_inputs(seed: int = 0) -> dict:
    key = jax.random.key(seed)
    ks = jax.random.split(key, 32)
    f32 = jnp.float32
    nrm = lambda k, shape, s: jax.random.normal(k, shape, f32) * s
    L, D, C = DEPTH, D_MODEL, BRANCH_WIDTH
    beta = DEEPNORM_BETA
    return {
        'x_prompt': nrm(ks[0], (BATCH, SEQ, D), 1.0),
        'x_sample': nrm(ks[1], (DEC_BATCH, DEC_SEQ, D), 1.0),
        'rel_bias': nrm(ks[2], (N_BUCKETS, N_Q_HEADS), 0.5),
        'w_in': nrm(ks[3], (L, D, N_IN), D ** -0.5),
        'pool_w': nrm(ks[4], (L, N_POOL_GROUPS, POOL_GROUP_WIDTH, POOL_GROUP_WIDTH), POOL_GROUP_WIDTH ** -0.5),
        'pool_scale': 1.0 + nrm(ks[5], (L, C), 0.1),
        'rwkv_mu': jax.random.uniform(ks[6], (L, RWKV_IN), f32),
        'rwkv_w0': jax.random.uniform(ks[7], (L, 2, C), f32, -6.0, -1.0),
        'rwkv_w2': nrm(ks[8], (L, 2, W_LORA, C), 0.1 * W_LORA ** -0.5),
        'rwkv_a0': nrm(ks[9], (L, 2, C), 0.1),
        'rwkv_a2': nrm(ks[10], (L, 2, A_LORA, C), 0.1 * A_LORA ** -0.5),
        'rwkv_g2': nrm(ks[11], (L, G_LORA, C), G_LORA ** -0.5),
        'rwkv_k_k': 0.85 + nrm(ks[12], (L, C), 0.05),
        'rwkv_k_a': 1.0 + nrm(ks[13], (L, C), 0.05),
        'rwkv_r_k': nrm(ks[14], (L, RWKV_HEADS, RWKV_HEAD), 0.1),
        'rwkv_gn_g': 1.0 + nrm(ks[15], (L, C), 0.05),
        'rwkv_gn_b': nrm(ks[16], (L, C), 0.02),
        'attn_sink': nrm(ks[17], (L, N_Q_HEADS), 0.5),
        'w_branch': nrm(ks[18], (L, N_BRANCHES, C, D), beta * C ** -0.5),
        'w_o': nrm(ks[19], (L, D, D), beta * D ** -0.5),
        'ln1_g': 1.0 + nrm(ks[20], (L, D), 0.05),
        'ln1_b': nrm(ks[21], (L, D), 0.02),
        'w_ff1': nrm(ks[22], (L, D, D_FF), beta * D ** -0.5),
        'w_ff2': nrm(ks[23], (L, D_FF, D), beta * D_FF ** -0.5),
        'ln2_g': 1.0 + nrm(ks[24], (L, D), 0.05),
        'ln2_b': nrm(ks[25], (L, D), 0.02),
    }


def reference(x_prompt, x_sample, rel_bias, w_in, pool_w, pool_scale, rwkv_mu, rwkv_w0, rwkv_w2, rwkv_a0,
              rwkv_a2, rwkv_g2, rwkv_k_k, rwkv_k_a, rwkv_r_k, rwkv_gn_g, rwkv_gn_b, attn_sink, w_branch, w_o,
              ln1_g, ln1_b, w_ff1, w_ff2, ln2_g, ln2_b):
    y_prompt = _trunk(x_prompt, rel_bias, w_in, pool_w, pool_scale, rwkv_mu, rwkv_w0, rwkv_w2, rwkv_a0,
                      rwkv_a2, rwkv_g2, rwkv_k_k, rwkv_k_a, rwkv_r_k, rwkv_gn_g, rwkv_gn_b, attn_sink,
                      w_branch, w_o, ln1_g, ln1_b, w_ff1, w_ff2, ln2_g, ln2_b)
    y_sample = _trunk(x_sample, rel_bias, w_in, pool_w, pool_scale, rwkv_mu, rwkv_w0, rwkv_w2, rwkv_a0,
                      rwkv_a2, rwkv_g2, rwkv_k_k, rwkv_k_a, rwkv_r_k, rwkv_gn_g, rwkv_gn_b, attn_sink,
                      w_branch, w_o, ln1_g, ln1_b, w_ff1, w_ff2, ln2_g, ln2_b)
    return (y_prompt, y_sample)
```

```python
import numpy as np
import contextlib, math, os
PHASES=os.environ.get('KPHASES','APTRC')
import concourse.bass as bass
import concourse.mybir as mybir
from concourse.bass_utils import run_bass_kernel_spmd
F32=mybir.dt.float32; BF16=mybir.dt.bfloat16
AF=mybir.ActivationFunctionType; ALU=mybir.AluOpType; AX=mybir.AxisListType
ENG=['pe','act','dve','pool','sp']
NROT=8
SAME_ENG_SYNC=True
D=1024; C=512; NIN=6272; DFF=4096
POOL_WINDOWS=(2,4,8,16)
ALPHA=(2*4)**0.25
LN_EPS=1e-5; GN_EPS=64e-5
NCOL=64

class Prog:
    def __init__(self, nc):
        self.nc=nc
        self.ops={e:[] for e in ENG}
        self.segs=[]
        self.lastw={}; self.readers={}
        self.dma_cnt={e:0 for e in ENG}
        self.last_rec={}
    def semid(self, rec):
        return ('d',rec['eng'],rec['dk']%NROT) if rec['dma'] else ('c',rec['eng'])
    def op(self, eng, fn, r=(), w=(), dma=False):
        rec=dict(eng=eng, fn=fn, waits=[], signal=bool(dma), dma=bool(dma))
        if dma:
            rec['dk']=self.dma_cnt[eng]; self.dma_cnt[eng]+=1
        def need(p):
            if p is None or p is rec: return
            if (not p['dma']) and (not dma) and p['eng']==eng and (eng=='pe' or not SAME_ENG_SYNC): return
            p['signal']=True; rec['waits'].append(p)
        for key in r: need(self.lastw.get(key))
        for key in w:
            need(self.lastw.get(key))
            for p in self.readers.get(key,{}).values(): need(p)
        sid=self.semid(rec)
        for key in r: self.readers.setdefault(key,{})[sid]=rec
        for key in w:
            self.lastw[key]=rec; self.readers[key]={}
        self.ops[eng].append(rec); self.last_rec[sid]=rec
        return rec
    def _setup(self):
        if getattr(self,'sems',None) is not None: return
        nc=self.nc
        self.stack=contextlib.ExitStack()
        self.sems={}
        for e in ENG:
            self.sems[('c',e)]=self.stack.enter_context(nc.semaphore(f"c_{e}"))
        for i in range(NROT):
            self.sems[('d','sp',i)]=self.stack.enter_context(nc.semaphore(f"d_sp_{i}"))
        self.emitted={e:0 for e in ENG}; self.ccnt={e:0 for e in ENG}; self.waited={e:{} for e in ENG}
    def barrier(self):
        self._setup()
        lasts=list(self.last_rec.values())
        for e in ENG:
            rec=dict(eng=e, fn=None, waits=[], signal=False, dma=False)
            for p in lasts:
                if p['eng']==e and not p['dma'] and e=='pe': continue
                p['signal']=True; rec['waits'].append(p)
            self.ops[e].append(rec)
        self.lastw={}; self.readers={}
        self.emit_segment()
    def emit_segment(self):
        nc=self.nc; sems=self.sems
        for e in ENG:
            for rec in self.ops[e][self.emitted[e]:]:
                if rec['dma']:
                    k=rec['dk']; rec['sv']=(('d',e,k%NROT),16*(k//NROT+1))
                elif rec['signal']:
                    self.ccnt[e]+=1; rec['sv']=(('c',e),self.ccnt[e])
        nosync=bool(os.environ.get('KNOSYNC'))
        with nc.Block() as block:
            def replay(ename):
                def f(e):
                    wd=self.waited[ename]
                    for rec in self.ops[ename][self.emitted[ename]:]:
                        ws={}
                        for p in rec['waits']:
                            s,v=p['sv']
                            if v>ws.get(s,0): ws[s]=v
                        for s,v in ws.items():
                            if wd.get(s,0)>=v: continue
                            if not nosync: e.wait_ge(sems[s],v)
                            wd[s]=v
                        if rec['fn'] is None: continue
                        ins=rec['fn'](e)
                        if rec['signal'] and not nosync:
                            ins.then_inc(sems[rec['sv'][0]],16 if rec['dma'] else 1)
                        rec['fn']=None
                return f
            block.tensor(replay('pe')); block.scalar(replay('act')); block.vector(replay('dve'))
            block.gpsimd(replay('pool')); block.sync(replay('sp'))
        for e in ENG: self.emitted[e]=len(self.ops[e])
    def run(self):
        self._setup()
        lasts=list(self.last_rec.values())
        rec=dict(eng='sp', fn=None, waits=[], signal=False, dma=False)
        for p in lasts:
            p['signal']=True; rec['waits'].append(p)
        self.ops['sp'].append(rec)
        self.emit_segment()
        self.stack.close()

class Arena:
    def __init__(self, nc, words):
        self.nc=nc
        self.t=nc.alloc_sbuf_tensor("arena",[128,words],F32); self.words=words; self.top=0; self.n=0
        self.base=nc.lookup_mloc(self.t).addr
        self.gen=0; self.cache={}
    def _get(self, off_words, n, dt):
        key=(self.gen,off_words,n,str(dt))
        t=self.cache.get(key)
        if t is None:
            t=self.nc.alloc_sbuf_tensor_at(f"a{len(self.cache)}",[128,n],dt,offset=self.base+4*off_words)
            self.cache[key]=t
        return t
    def f32(self, n0):
        n=(n0+7)//8*8
        assert self.top+n<=self.words, (self.top,n,self.words)
        t=self._get(self.top,n,F32); self.top+=n
        return t[:,0:n0]
    def bf(self, n):
        w=(n+15)//16*8
        assert self.top+w<=self.words, (self.top,w,self.words)
        t=self._get(self.top,2*w,BF16); self.top+=w
        return t[:,0:n]

def _t5_bucket(rel):
    half=16; max_exact=8
    n=np.abs(rel)
    large=max_exact+(np.log(np.maximum(n,1)/max_exact)/math.log(128/max_exact)*(half-max_exact)).astype(np.int32)
    large=np.minimum(large,half-1)
    return (rel>0).astype(np.int32)*half+np.where(n<max_exact,n,large)

def host_consts(rel_bias):
    c={}
    c['ident']=np.eye(128,dtype=np.float32)
    p=np.arange(128)[:,None]; f=np.arange(128)[None,:]
    same=(p//64)==(f//64)
    MUs=((p<f)&same).astype(np.float32); MUi=((p<=f)&same).astype(np.float32)
    MLs=((p>f)&same).astype(np.float32); MLi=((p>=f)&same).astype(np.float32)
    c['maskf']=np.concatenate([MUs,MUi,MUs,MUi],1); c['maskb']=np.concatenate([MLs,MLi,MLs,MLi],1)
    rs=np.ones((128,512),np.float32); rs[:,::64]=0.0; c['restart']=rs
    c['bones']=same.astype(np.float32)
    k=np.arange(128)[:,None]; q=np.arange(128)[None,:]
    ab=np.zeros((128,3,2,4,128),np.float32)
    for kbi,kb in enumerate((-1,0,1)):
        rel=kb*128+k-q
        bk=_t5_bucket(rel); ok=np.abs(rel)<=128
        for kvh in range(2):
            for g in range(4):
                ab[:,kbi,kvh,g,:]=np.where(ok,rel_bias[bk,kvh*4+g],np.float32(-30000.0))
    c['attb']=ab.reshape(128,3072)
    op=np.zeros((128,2,128),np.float32); op[:,0,0:64]=1; op[:,1,64:128]=1; c['onespad']=op.reshape(128,256)
    pe=np.ones((128,4,2,8),np.float32)
    for g,w in enumerate(POOL_WINDOWS):
        h=w//2
        for t in range(h): pe[:,g,0,t]=w/(t+h)
        for i in range(h-1): pe[:,g,1,7-i]=w/(i+1+h)
    c['pedge']=pe.reshape(128,64)
    return c

def host_weights(inp, depth):
    L=depth; w={}
    w_in=inp['w_in'][:L]
    cols=list(range(0,C+1920))
    oq=C+1920
    for j in range(4):
        cols+=list(range(oq+j*64,oq+j*64+64))+list(range(oq+(4+j)*64,oq+(4+j)*64+64))
    cols+=list(range(oq+512,NIN))
    wp=w_in[:,:,cols]
    w['win']=np.ascontiguousarray(wp.reshape(L,8,128,49,128).transpose(0,3,2,1,4).reshape(L,49,128,1024))
    w['poolw']=np.ascontiguousarray(inp['pool_w'][:L].transpose(0,2,1,3).reshape(L,128,512))
    w['w2']=np.ascontiguousarray(inp['rwkv_w2'][:L].reshape(L,128,512))
    w['a2']=np.ascontiguousarray(inp['rwkv_a2'][:L].reshape(L,128,512))
    w['g2']=np.ascontiguousarray(inp['rwkv_g2'][:L])
    wb=inp['w_branch'][:L].copy()
    perm=[kvh*256+g*64+d for g in range(4) for kvh in range(2) for d in range(64)]
    wb[:,2]=wb[:,2][:,perm]
    w['wbr']=np.ascontiguousarray(wb.reshape(L,3,4,128,1024).transpose(0,3,1,2,4).reshape(L,128,12*1024))
    w['wo']=np.ascontiguousarray(inp['w_o'][:L].reshape(L,8,128,1024).transpose(0,2,1,3).reshape(L,128,8*1024))
    w['ff1']=np.ascontiguousarray(inp['w_ff1'][:L].reshape(L,8,128,32,128).transpose(0,3,2,1,4).reshape(L,32,128,1024))
    w['ff2']=np.ascontiguousarray(inp['w_ff2'][:L].reshape(L,32,128,1024).transpose(0,2,1,3).reshape(L,128,32*1024))
    pc=np.zeros((L,128,NCOL),np.float32)
    pc[:,:,0:15]=inp['rwkv_mu'][:L].reshape(L,15,128).transpose(0,2,1)
    pc[:,:,15:19]=inp['pool_scale'][:L].reshape(L,4,128).transpose(0,2,1)
    pc[:,:,19:27]=inp['rwkv_w0'][:L].reshape(L,8,128).transpose(0,2,1)
    pc[:,:,27:35]=inp['rwkv_a0'][:L].reshape(L,8,128).transpose(0,2,1)
    pc[:,:,35:39]=inp['rwkv_k_k'][:L].reshape(L,4,128).transpose(0,2,1)
    pc[:,:,39:43]=inp['rwkv_k_a'][:L].reshape(L,4,128).transpose(0,2,1)
    pc[:,:,43:47]=inp['rwkv_r_k'][:L].reshape(L,4,128).transpose(0,2,1)
    pc[:,:,47:51]=inp['rwkv_gn_g'][:L].reshape(L,4,128).transpose(0,2,1)
    pc[:,:,51:55]=inp['rwkv_gn_b'][:L].reshape(L,4,128).transpose(0,2,1)
    sk=inp['attn_sink'][:L].reshape(L,2,1,4)
    pc[:,:,55:59]=np.broadcast_to(sk,(L,2,64,4)).reshape(L,128,4)
    w['pcol']=pc
    w['lnrow']=np.ascontiguousarray(np.broadcast_to(
        np.stack([inp['ln1_g'][:L],inp['ln1_b'][:L],inp['ln2_g'][:L],inp['ln2_b'][:L]],1)[:,:,None,:],(L,4,128,1024)))
    return w

WSHAPES=lambda L:{'win':[L,49,128,1024],'poolw':[L,128,512],'w2':[L,128,512],'a2':[L,128,512],'g2':[L,128,512],
        'wbr':[L,128,12288],'wo':[L,128,8192],'ff1':[L,32,128,1024],'ff2':[L,128,32768],'pcol':[L,128,NCOL],'lnrow':[L,4,128,1024]}
CSHAPES={'ident':[128,128],'maskf':[128,512],'maskb':[128,512],'restart':[128,512],'bones':[128,128],'attb':[128,3072],
         'onespad':[128,256],'pedge':[128,64]}
BFW=['win','poolw','w2','a2','g2','wbr','wo','ff1','ff2']

def build(job_S, depth, debug=False):
    IK='ExternalOutput' if debug else 'Internal'
    nc=bass.Bass("TRN2", target_bir_lowering=False)
    P=Prog(nc); L=depth
    SMAX=max(job_S)
    din={}
    for n,s in WSHAPES(L).items(): din[n]=nc.dram_tensor(n,s,F32,kind="ExternalInput").ap()
    for n,s in CSHAPES.items(): din[n]=nc.dram_tensor(n,s,F32,kind="ExternalInput").ap()
    xin=[nc.dram_tensor(f"x{j}",[S,D],F32,kind="ExternalInput").ap() for j,S in enumerate(job_S)]
    yout=[nc.dram_tensor(f"y{j}",[S,D],F32,kind="ExternalOutput").ap() for j,S in enumerate(job_S)]
    wb={n:nc.dram_tensor(n+"_b",WSHAPES(L)[n],BF16,kind="Internal").ap() for n in BFW}
    ZP=nc.dram_tensor("ZP",[4,128,SMAX],F32,kind=IK).ap()
    ZR=nc.dram_tensor("ZR",[15,128,SMAX],F32,kind=IK).ap()
    QT=nc.dram_tensor("QT",[128,4,SMAX],BF16,kind=IK).ap()
    KT=nc.dram_tensor("KT",[128,SMAX],BF16,kind=IK).ap()
    VTOK=nc.dram_tensor("VTOK",[SMAX,128],BF16,kind=IK).ap()
    GT=nc.dram_tensor("GT",[24,128,SMAX],BF16,kind=IK).ap()
    AOT=nc.dram_tensor("AOT",[128,4,SMAX],BF16,kind=IK).ap()
    BOT=nc.dram_tensor("BOT",[128,4,SMAX],BF16,kind=IK).ap()
    COT=nc.dram_tensor("COT",[128,4,SMAX],BF16,kind=IK).ap()
    YB=nc.dram_tensor("YB",[4,128,SMAX],F32,kind=IK).ap()
    XTOK=nc.dram_tensor("XTOK",[SMAX,D],F32,kind=IK).ap()
    A=Arena(nc,49152)
    PB=[None]*8; PBB=[None]*8; pbcm=[]
    pbk=lambda i:('pb',i)
    class RR:
        def __init__(s,ids): s.ids=ids; s.i=0
        def next(s):
            i=s.ids[s.i%len(s.ids)]; s.i+=1; return i
    mmrr=RR([0,1,2]); tprr=RR([3,4])
    def TPB(i): return PB[i][:, :].bitcast(BF16)
    def new_psum(tag):
        for cm in reversed(pbcm): cm.__exit__(None,None,None)
        pbcm.clear()
        for i in range(8):
            cm=nc.psum_tensor(f"pb{i}_{tag}",[128,512],F32); PB[i]=cm.__enter__(); pbcm.append(cm)
    new_psum('init')
    def dma(out,in_,r,w): P.op('sp',lambda e:e.dma_start(out=out,in_=in_),r=r,w=w,dma=True)
    ccnt=[0]
    def ev_copy(out,in_,r,w,eng=None):
        if eng is None:
            eng='act' if ccnt[0]%2==0 else 'dve'; ccnt[0]+=1
        if eng=='act': P.op('act',lambda e:e.activation(out=out,in_=in_,func=AF.Copy),r=r,w=w)
        elif eng=='dve': P.op('dve',lambda e:e.tensor_copy(out=out,in_=in_),r=r,w=w)
        else: P.op('pool',lambda e:e.tensor_copy(out=out,in_=in_),r=r,w=w)
    def tt(eng,out,a,b,op,r,w): P.op(eng,lambda e:e.tensor_tensor(out=out,in0=a,in1=b,op=op),r=r,w=w)
    def ts(eng,out,a,s1,s2,op0,op1,r,w): P.op(eng,lambda e:e.tensor_scalar(out=out,in0=a,scalar1=s1,scalar2=s2,op0=op0,op1=op1),r=r,w=w)
    def stt(eng,out,a,s,b,op0,op1,r,w): P.op(eng,lambda e:e.scalar_tensor_tensor(out=out,in0=a,scalar=s,in1=b,op0=op0,op1=op1),r=r,w=w)
    def act(out,in_,func,r,w,bias=None,scale=None):
        kw={}
        if bias is not None: kw['bias']=bias
        if scale is not None: kw['scale']=scale
        P.op('act',lambda e:e.activation(out=out,in_=in_,func=func,**kw),r=r,w=w)
    def red(out,in_,r,w): P.op('dve',lambda e:e.tensor_reduce(out=out,in_=in_,axis=AX.X,op=ALU.add),r=r,w=w)
    def recip(out,in_,r,w): P.op('dve',lambda e:e.reciprocal(out=out,in_=in_),r=r,w=w)
    def mset(ap,val,w): P.op('pool',lambda e:e.memset(ap,val),w=w)
    def mm(out,lhsT,rhs,start,stop,r,w): P.op('pe',lambda e:e.matmul(out,lhsT,rhs,start=start,stop=stop),r=r,w=w)
    def tr(out,in_,idn,r,w): P.op('pe',lambda e:e.transpose(out,in_,idn),r=r,w=w)

    identf=A.f32(128); maskf=A.f32(512); maskb=A.f32(512); restart=A.f32(512); bonesf=A.f32(128)
    attb=A.f32(3072); pedge=A.f32(64); tmpc=A.f32(256)
    identb=A.bf(128); bonesb=A.bf(128); onespad=A.bf(256)
    pcol=A.f32(NCOL); omm=A.f32(16); hmu=A.f32(16); omka=A.f32(4); rkh=A.f32(4); esink=A.f32(512); esk4=A.f32(4)
    lnrow=[A.f32(1024) for _ in range(4)]
    for n,t in [('ident',identf),('maskf',maskf),('maskb',maskb),('restart',restart),('bones',bonesf),('attb',attb),('pedge',pedge),('onespad',tmpc)]:
        dma(t,din[n][:,:],[],[('c',n)])
    ev_copy(identb,identf,[('c','ident')],[('c','identb')],'dve')
    ev_copy(bonesb,bonesf,[('c','bones')],[('c','bonesb')],'dve')
    ev_copy(onespad,tmpc,[('c','onespad')],[('c','onespadb')],'dve')
    P.barrier()
    base_top=A.top
    def cast_rows(src,dst,ncols):
        CH=4096
        for c0 in range(0,ncols,CH):
            cw=min(CH,ncols-c0)
            i=castrr.next()
            dma(cst[i][:,0:cw],src[:,c0:c0+cw],[],[('cst',i)])
            ev_copy(csb[i][:,0:cw],cst[i][:,0:cw],[('cst',i)],[('csb',i)])
            dma(dst[:,c0:c0+cw],csb[i][:,0:cw],[('csb',i)],[('wb',)])
    cst=[A.f32(4096) for _ in range(3)]; csb=[A.bf(4096) for _ in range(3)]; castrr=RR([0,1,2])
    for l in range(L):
        for n in BFW:
            s=WSHAPES(L)[n]
            if len(s)==4:
                for c in range(s[1]): cast_rows(din[n][l,c],wb[n][l,c],s[3])
            else:
                cast_rows(din[n][l],wb[n][l],s[2])
    P.barrier()

    for j,S in enumerate(job_S):
        NT=S//128
        for l in range(L):
            xsrc=xin[j] if l==0 else XTOK
            xdst=yout[j] if l==L-1 else XTOK
            A.top=base_top; A.gen+=1; new_psum(f'{j}_{l}')
            dma(pcol,din['pcol'][l],[],[('pcol',)])
            for i in range(4): dma(lnrow[i],din['lnrow'][l,i],[],[('lnrow',i)])
            ts('dve',omm[:,0:15],pcol[:,0:15],-1.0,1.0,ALU.mult,ALU.add,[('pcol',)],[('omm',)])
            ts('dve',hmu[:,0:15],pcol[:,0:15],0.5,0.0,ALU.mult,ALU.add,[('pcol',)],[('hmu',)])
            ts('dve',omka,pcol[:,39:43],-1.0,1.0,ALU.mult,ALU.add,[('pcol',)],[('omka',)])
            ts('dve',rkh,pcol[:,43:47],0.5,0.0,ALU.mult,ALU.add,[('pcol',)],[('rkh',)])
            act(esk4,pcol[:,55:59],AF.Exp,[('pcol',)],[('esk4',)])
            for g in range(4):
                ts('dve',esink[:,g*128:(g+1)*128],identf,0.0,esk4[:,g:g+1],ALU.mult,ALU.add,[('esk4',),('c','ident')],[('esink',)])
            P.barrier()
            ltop=A.top
            w0c=lambda d,cc: pcol[:,19+d*4+cc:20+d*4+cc]
            a0c=lambda d,cc: pcol[:,27+d*4+cc:28+d*4+cc]

            TA=min(S,2048)
            for sg in (range(S//TA) if 'A' in PHASES else []):
                A.top=ltop; t0=sg*TA
                XT=A.bf(8*TA).rearrange("p (k t) -> p k t",k=8)
                xs=[A.f32(1024) for _ in range(2)]; xb=[A.bf(1024) for _ in range(2)]
                wch=[A.bf(1024) for _ in range(3)]
                stg=[]; stgb=[]
                for _ in range(2):
                    o=A.top; stg.append(A.f32(TA)); o2=A.top; A.top=o; stgb.append(A.bf(TA)); A.top=o2
                for ti in range(TA//128):
                    i=ti%2
                    dma(xs[i],xsrc[t0+ti*128:t0+(ti+1)*128,:],[('xsrc',)],[('xs',i)])
                    ev_copy(xb[i],xs[i],[('xs',i)],[('xb',i)],'act')
                    pi=tprr.next()
                    for k in range(8):
                        tr(TPB(pi)[:,k*128:(k+1)*128],xb[i][:,k*128:(k+1)*128],identb,[('xb',i),('c','identb')],[pbk(pi)])
                    ev_copy(XT[:,:,ti*128:(ti+1)*128],TPB(pi).rearrange("p (k t) -> p k t",k=8),[pbk(pi)],[('XT',)],'dve')
                wi=0
                for c in range(49):
                    if c==24: continue
                    i=wi%3; si=wi%2; wi+=1
                    dma(wch[i],wb['win'][l,c],[('wb',)],[('wch',i)])
                    isbf = c>=19
                    for g in range(TA//512):
                        pi=mmrr.next()
                        for k in range(8):
                            mm(PB[pi][:,:],wch[i][:,k*128:(k+1)*128],XT[:,k,g*512:(g+1)*512],k==0,k==7,[('wch',i),('XT',)],[pbk(pi)])
                        if c>=25:
                            act(stgb[si][:,g*512:(g+1)*512],PB[pi][:,:],AF.Sigmoid,[pbk(pi)],[('stg',si)])
                        elif isbf:
                            ev_copy(stgb[si][:,g*512:(g+1)*512],PB[pi][:,:],[pbk(pi)],[('stg',si)])
                        else:
                            ev_copy(stg[si][:,g*512:(g+1)*512],PB[pi][:,:],[pbk(pi)],[('stg',si)])
                    if c<4: dst=ZP[c][:,t0:t0+TA]; src=stg[si]; key=('ZP',)
                    elif c<19: dst=ZR[c-4][:,t0:t0+TA]; src=stg[si]; key=('ZR',)
                    elif c<23: dst=QT[:,c-19,t0:t0+TA]; src=stgb[si][:,0:TA]; key=('QT',)
                    elif c==23: dst=KT[:,t0:t0+TA]; src=stgb[si][:,0:TA]; key=('KT',)
                    else: dst=GT[c-25][:,t0:t0+TA]; src=stgb[si][:,0:TA]; key=('GT',)
                    dma(dst,src,[('stg',si)],[key])
                dma(wch[0],wb['win'][l,24],[('wb',)],[('wch',0)])
                vst=stgb[0][:,0:TA].rearrange("p (n c) -> p n c",c=128)
                for ti in range(TA//128):
                    pi=mmrr.next()
                    for k in range(8):
                        mm(PB[pi][:,0:128],XT[:,k,ti*128:(ti+1)*128],wch[0][:,k*128:(k+1)*128],k==0,k==7,[('wch',0),('XT',)],[pbk(pi)])
                    ev_copy(vst[:,ti,:],PB[pi][:,0:128],[pbk(pi)],[('stg',0)])
                for n0 in range(0,TA//128,4):
                    dma(VTOK[t0+n0*128:t0+(n0+4)*128,:].rearrange("(n p) c -> p n c",p=128),vst[:,n0:n0+4,:],[('stg',0)],[('VTOK',)])
                P.barrier()

            A.top=ltop
            plw=A.bf(512)
            dma(plw,wb['poolw'][l],[('wb',)],[('plw',)])
            ptop=A.top
            TP=min(S,2048)
            for sg in (range(S//TP) if 'P' in PHASES else []):
                t0=sg*TP; n=TP+16
                for g in range(4):
                    A.top=ptop
                    u=A.f32(n); p0=A.f32(n); p1=A.f32(n); pab=A.bf(TP); ostg=A.bf(TP)
                    lo=max(t0-8,0); hi=min(t0+TP+8,S)
                    if t0-8<0: P.op('pool',lambda e,u=u:e.memset(u[:,0:8],0.0),w=[('u',)])
                    if t0+TP+8>S: P.op('pool',lambda e,u=u,n=n:e.memset(u[:,n-8:n],0.0),w=[('u',)])
                    dma(u[:,lo-(t0-8):hi-(t0-8)],ZP[g][:,lo:hi],[('ZP',)],[('u',)])
                    tt('pool',p0[:,1:n],u[:,0:n-1],u[:,1:n],ALU.add,[('u',)],[('p0',)])
                    cur,nxt,ck,nk=p0,p1,('p0',),('p1',)
                    for k in range(1,g+1):
                        d_=2**(k-1)
                        tt('pool',nxt[:,d_:n-d_],cur[:,0:n-2*d_],cur[:,2*d_:n],ALU.add,[ck],[nk])
                        cur,nxt,ck,nk=nxt,cur,nk,ck
                    wdw=POOL_WINDOWS[g]
                    act(nxt[:,8:8+TP],cur[:,8:8+TP],AF.Copy,[ck],[nk],scale=1.0/wdw)
                    if sg==0:
                        tt('dve',nxt[:,8:16],nxt[:,8:16],pedge[:,g*16:g*16+8],ALU.mult,[nk,('c','pedge')],[nk])
                    if sg==S//TP-1:
                        tt('dve',nxt[:,TP:TP+8],nxt[:,TP:TP+8],pedge[:,g*16+8:g*16+16],ALU.mult,[nk,('c','pedge')],[nk])
                    tt('dve',pab,nxt[:,8:8+TP],u[:,8:8+TP],ALU.subtract,[nk,('u',)],[('pab',)])
                    for tg in range(TP//512):
                        pi=mmrr.next()
                        mm(PB[pi][:,:],plw[:,g*128:(g+1)*128],pab[:,tg*512:(tg+1)*512],True,True,[('plw',),('pab',)],[pbk(pi)])
                        act(ostg[:,tg*512:(tg+1)*512],PB[pi][:,:],AF.Copy,[pbk(pi)],[('ostg',)],scale=pcol[:,15+g:16+g])
                    dma(AOT[:,g,t0:t0+TP],ostg,[('ostg',)],[('AOT',)])
            P.barrier()

            TQ=min(S,2048)
            attb5=attb.rearrange("p (a b c) -> p a b c",a=3,b=2)
            for sg in (range(S//TQ) if 'T' in PHASES else []):
                A.top=ltop; t0=sg*TQ; nqb=TQ//128; nkb=nqb+2
                q=A.bf(4*TQ).rearrange("p (g t) -> p g t",g=4)
                kt=A.bf(TQ+256)
                vp=[A.bf(nkb*128).rearrange("p (n c) -> p n c",c=128) for _ in range(2)]
                cstg=A.bf(4*TQ).rearrange("p (g t) -> p g t",g=4)
                sb=[A.f32(512) for _ in range(2)]; pT=[A.bf(512) for _ in range(2)]
                den=A.f32(512); rec=A.f32(512)
                dma(q,QT[:,:,t0:t0+TQ],[('QT',)],[('q',)])
                lo=max(t0-128,0); hi=min(t0+TQ+128,S)
                dma(kt[:,lo-(t0-128):hi-(t0-128)],KT[:,lo:hi],[('KT',)],[('kt',)])
                for kvh in range(2):
                    P.op('pool',lambda e,v=vp[kvh]:e.memset(v,0.0),w=[('vp',kvh)])
                    b0=(lo-(t0-128))//128; nb=(hi-lo)//128
                    for n0 in range(0,nb,4):
                        n1=min(n0+4,nb)
                        dma(vp[kvh][:,b0+n0:b0+n1,kvh*64:(kvh+1)*64],VTOK[lo+n0*128:lo+n1*128,kvh*64:(kvh+1)*64].rearrange("(n p) c -> p n c",p=128),[('VTOK',)],[('vp',kvh)])
                it=0
                for qb in range(nqb):
                    gq=sg*nqb+qb
                    lst=[(kvh,kbi) for kvh in range(2) for kbi in range(3) if 0<=gq+kbi-1<NT]
                    for idx,(kvh,kbi) in enumerate(lst):
                        pr=slice(kvh*64,(kvh+1)*64)
                        pi=mmrr.next(); si=it%2; it+=1
                        kb=qb+kbi
                        mm(PB[pi][:,:],kt[pr,kb*128:(kb+1)*128],q[pr,:,qb*128:(qb+1)*128],True,True,[('kt',),('q',)],[pbk(pi)])
                        stt('dve',sb[si],PB[pi][:,:],0.125,attb5[:,kbi,kvh,:],ALU.mult,ALU.add,[pbk(pi),('c','attb')],[('sb',si)])
                        act(pT[si],sb[si],AF.Exp,[('sb',si)],[('pT',si)])
                        mm(PB[5][:,:],vp[kvh][:,kb,:],pT[si],idx==0,idx==len(lst)-1,[('vp',kvh),('pT',si)],[pbk(5)])
                        mm(PB[6][:,:],onespad[:,kvh*128:(kvh+1)*128],pT[si],idx==0,idx==len(lst)-1,[('c','onespadb'),('pT',si)],[pbk(6)])
                    tt('dve',den,PB[6][:,:],esink,ALU.add,[pbk(6),('esink',)],[('den',)])
                    recip(rec,den,[('den',)],[('rec',)])
                    tt('dve',cstg[:,:,qb*128:(qb+1)*128],PB[5][:,:].rearrange("p (g t) -> p g t",g=4),rec.rearrange("p (g t) -> p g t",g=4),ALU.mult,[pbk(5),('rec',)],[('cstg',)])
                dma(COT[:,:,t0:t0+TQ],cstg,[('cstg',)],[('COT',)])
                P.barrier()

            TB=256; NBLK=S//TB; NTB=TB//128; NCK=TB//64
            NEG=-math.exp(-0.5)
            for d in ((1,0) if 'R' in PHASES else ()):
                A.top=ltop
                w2s=A.bf(512); a2s=A.bf(512); g2s=A.bf(512)
                dma(w2s,wb['w2'][l],[('wb',)],[('w2s',)]); dma(a2s,wb['a2'][l],[('wb',)],[('a2s',)]); dma(g2s,wb['g2'][l],[('wb',)],[('g2s',)])
                S32=[A.f32(128) for _ in range(4)]; Sbf=[A.bf(128) for _ in range(4)]
                for cc in range(4):
                    P.op('pool',lambda e,t=S32[cc]:e.memset(t,0.0),w=[('S32',cc)])
                    P.op('pool',lambda e,t=Sbf[cc]:e.memset(t,0.0),w=[('Sbf',cc)])
                rtop=A.top
                maskA=maskf if d==0 else maskb
                maskN=maskb[:,0:128] if d==0 else maskf[:,0:128]
                blks=range(NBLK) if d==0 else range(NBLK-1,-1,-1)
                for blk in blks:
                    A.top=rtop; t0=blk*TB
                    zraw=A.f32(TB+2); t1=A.f32(TB); t2=A.f32(TB)
                    th=A.bf(TB); al=A.bf(TB); gs=A.bf(TB)
                    zr=[A.f32(TB) for _ in range(4)]; zk=[A.f32(TB) for _ in range(4)]; zv=[A.f32(TB) for _ in range(4)]
                    def shift(c,out,okey):
                        lo=max(t0-1,0); hi=min(t0+TB+1,S)
                        if t0-1<0: mset(zraw[:,0:1],0.0,[('zraw',)])
                        if t0+TB+1>S: mset(zraw[:,TB+1:TB+2],0.0,[('zraw',)])
                        dma(zraw[:,lo-(t0-1):hi-(t0-1)],ZR[c][:,lo:hi],[('ZR',)],[('zraw',)])
                        tt('pool',t1,zraw[:,0:TB],zraw[:,2:TB+2],ALU.add,[('zraw',)],[('t1',)])
                        act(t2,zraw[:,1:TB+1],AF.Copy,[('zraw',),('omm',)],[('t2',)],scale=omm[:,c:c+1])
                        stt('dve',out,t1,hmu[:,c:c+1],t2,ALU.mult,ALU.add,[('t1',),('t2',),('hmu',)],[okey])
                    for cc in range(4):
                        shift(cc,zr[cc],('zr',cc)); shift(4+cc,zk[cc],('zk',cc)); shift(8+cc,zv[cc],('zv',cc))
                    zw=A.f32(TB); za=A.f32(TB); zg=A.f32(TB)
                    shift(12,zw,('zw',)); shift(13,za,('za',))
                    act(th,zw,AF.Tanh,[('zw',)],[('th',)])
                    ev_copy(al,za,[('za',)],[('al',)],'act')
                    if d==0:
                        shift(14,zg,('zg',))
                        act(gs,zg,AF.Sigmoid,[('zg',)],[('gs',)])
                    btop=A.top
                    AR=[A.bf(NTB*2*128).rearrange("p (n a t) -> p n a t",n=NTB,a=2) for _ in range(4)]
                    BTt=[A.bf(TB) for _ in range(4)]; KTt=[A.bf(TB) for _ in range(4)]
                    BH=[A.bf(TB).rearrange("p (n c) -> p n c",c=128) for _ in range(4)]
                    KH=[A.bf(TB).rearrange("p (n c) -> p n c",c=128) for _ in range(4)]
                    VT=[A.bf(TB).rearrange("p (n c) -> p n c",c=128) for _ in range(4)]
                    gL=[A.f32(NCK) for _ in range(4)]
                    YT=[A.f32(TB) for _ in range(4)]
                    if d==0:
                        bonus=[A.f32(TB) for _ in range(4)]; gg=[A.f32(TB) for _ in range(4)]
                    AM=[[[A.bf(512) for _ in range(2)] for _ in range(NTB)] for _ in range(4)]
                    TT=[[[A.bf(128) for _ in range(2)] for _ in range(NTB)] for _ in range(4)]
                    pp0=A.bf(256); pp1=A.bf(256); tt0=A.bf(128); tt1=A.bf(128); Xsb=A.bf(128); Usb=A.bf(128)
                    pyb=A.f32(TB); pyc=A.f32(TB); psq=A.f32(TB); pob=A.bf(TB)
                    etop=A.top
                    for cc in range(4):
                        A.top=etop
                        lw=A.f32(TB); alp=A.f32(TB); kk=A.f32(TB); tA=A.f32(TB); tB=A.f32(TB); kd=A.f32(TB); bb=A.f32(TB)
                        cumf=A.f32(TB); cbx=A.f32(TB); cumb=A.f32(TB); cfx=A.f32(TB)
                        e1=A.f32(TB); e2=A.f32(TB); e3=A.f32(TB); e4=A.f32(TB)
                        bhf=A.bf(TB); khf=A.bf(TB); vbf=A.bf(TB)
                        K=lambda n:('r1',n)
                        dr=slice(d*64,(d+1)*64)
                        pi=mmrr.next()
                        mm(PB[pi][:,0:TB],w2s[dr,cc*128:(cc+1)*128],th[dr,:],True,True,[('w2s',),('th',)],[pbk(pi)])
                        act(lw,PB[pi][:,0:TB],AF.Sigmoid,[pbk(pi),('pcol',)],[K('lw')],bias=w0c(d,cc))
                        ts('dve',lw,lw,NEG,0.0,ALU.mult,ALU.add,[K('lw')],[K('lw')])
                        pi=mmrr.next()
                        mm(PB[pi][:,0:TB],a2s[dr,cc*128:(cc+1)*128],al[dr,:],True,True,[('a2s',),('al',)],[pbk(pi)])
                        act(alp,PB[pi][:,0:TB],AF.Sigmoid,[pbk(pi),('pcol',)],[K('alp')],bias=a0c(d,cc))
                        act(tA,zk[cc],AF.Copy,[('zk',cc),('pcol',)],[K('tA')],scale=pcol[:,35+cc:36+cc])
                        tt('pool',tB,tA,tA,ALU.mult,[K('tA')],[K('tB')])
                        pi=mmrr.next()
                        mm(PB[pi][:,0:TB],bonesf,tB,True,True,[('c','bones'),K('tB')],[pbk(pi)])
                        act(tB,PB[pi][:,0:TB],AF.Sqrt,[pbk(pi)],[K('tB')])
                        ts('dve',tB,tB,1e-12,0.0,ALU.max,ALU.add,[K('tB')],[K('tB')])
                        P.op('dve',lambda e,tB=tB:e.reciprocal(out=tB,in_=tB),r=[K('tB')],w=[K('tB')])
                        tt('pool',kk,tA,tB,ALU.mult,[K('tA'),K('tB')],[K('kk')])
                        ts('dve',tA,alp,pcol[:,39+cc:40+cc],omka[:,cc:cc+1],ALU.mult,ALU.add,[K('alp'),('pcol',),('omka',)],[K('tA')])
                        tt('pool',kd,zk[cc],tA,ALU.mult,[('zk',cc),K('tA')],[K('kd')])
                        tt('pool',bb,kk,alp,ALU.mult,[K('kk'),K('alp')],[K('bb')])
                        P.op('dve',lambda e,cumf=cumf,lw=lw:e.tensor_tensor_scan(out=cumf,data0=restart[:,0:TB],data1=lw,initial=0.0,op0=ALU.mult,op1=ALU.add),r=[('c','restart'),K('lw')],w=[K('cumf')])
                        c3=lambda t:t.rearrange("p (c l) -> p c l",l=64)
                        tt('dve',c3(cbx),c3(cumf)[:,:,63:64].to_broadcast([128,NCK,64]),c3(cumf),ALU.subtract,[K('cumf')],[K('cbx')])
                        tt('pool',cumb,cbx,lw,ALU.add,[K('cbx'),K('lw')],[K('cumb')])
                        tt('pool',cfx,cumf,lw,ALU.subtract,[K('cumf'),K('lw')],[K('cfx')])
                        act(gL[cc],c3(cumf)[:,:,63],AF.Exp,[K('cumf')],[('gL',cc)])
                        ci,cx,co=(cumf,cfx,cbx) if d==0 else (cumb,cbx,cfx)
                        kci,kcx,kco=(K('cumf'),K('cfx'),K('cbx')) if d==0 else (K('cumb'),K('cbx'),K('cfx'))
                        act(e1,ci,AF.Exp,[kci],[K('e1')])
                        act(e2,ci,AF.Exp,[kci],[K('e2')],scale=-1.0)
                        act(e3,cx,AF.Exp,[kcx],[K('e3')])
                        act(e4,co,AF.Exp,[kco],[K('e4')])
                        v4=lambda t:t.rearrange("p (n t) -> p n t",t=128)
                        stt('dve',AR[cc][:,:,0,:],v4(kk),-1.0,v4(e3),ALU.mult,ALU.mult,[K('kk'),K('e3')],[('AR',cc)])
                        tt('pool',AR[cc][:,:,1,:],v4(zr[cc]),v4(e1),ALU.mult,[('zr',cc),K('e1')],[('AR',cc)])
                        tt('dve',BTt[cc],bb,e2,ALU.mult,[K('bb'),K('e2')],[('BTt',cc)])
                        tt('pool',KTt[cc],kd,e2,ALU.mult,[K('kd'),K('e2')],[('KTt',cc)])
                        tt('dve',bhf,bb,e4,ALU.mult,[K('bb'),K('e4')],[K('bhf')])
                        tt('pool',khf,kd,e4,ALU.mult,[K('kd'),K('e4')],[K('khf')])
                        ev_copy(vbf,zv[cc],[('zv',cc)],[K('vbf')],'act')
                        for src,skey,dstt,dkey in ((bhf,K('bhf'),BH[cc],('BH',cc)),(khf,K('khf'),KH[cc],('KH',cc)),(vbf,K('vbf'),VT[cc],('VT',cc))):
                            pi=tprr.next()
                            for ti in range(NTB):
                                tr(TPB(pi)[:,ti*128:(ti+1)*128],src[:,ti*128:(ti+1)*128],identb,[skey,('c','identb')],[pbk(pi)])
                            ev_copy(dstt,TPB(pi)[:,0:TB].rearrange("p (n c) -> p n c",c=128),[pbk(pi)],[dkey])
                        if d==0:
                            orr=slice(64,128)
                            pi=mmrr.next()
                            mm(PB[pi][:,0:TB],a2s[orr,cc*128:(cc+1)*128],al[orr,:],True,True,[('a2s',),('al',)],[pbk(pi)])
                            act(tA,PB[pi][:,0:TB],AF.Sigmoid,[pbk(pi),('pcol',)],[K('tA')],bias=a0c(1,cc))
                            ts('dve',tA,tA,pcol[:,39+cc:40+cc],omka[:,cc:cc+1],ALU.mult,ALU.add,[K('tA'),('pcol',),('omka',)],[K('tA')])
                            tt('pool',tA,zk[cc],tA,ALU.mult,[('zk',cc),K('tA')],[K('tA')])
                            tt('pool',tA,tA,kd,ALU.add,[K('tA'),K('kd')],[K('tA')])
                            stt('dve',bhf,zr[cc],rkh[:,cc:cc+1],tA,ALU.mult,ALU.mult,[('zr',cc),('rkh',),K('tA'),K('bhf')],[K('bhf')])
                            pi=mmrr.next()
                            mm(PB[pi][:,0:TB],bonesb,bhf,True,True,[('c','bonesb'),K('bhf')],[pbk(pi)])
                            tt('dve',bonus[cc],PB[pi][:,0:TB],zv[cc],ALU.mult,[pbk(pi),('zv',cc)],[('bonus',cc)])
                            pi=mmrr.next()
                            mm(PB[pi][:,0:TB],g2s[:,cc*128:(cc+1)*128],gs,True,True,[('g2s',),('gs',)],[pbk(pi)])
                            ev_copy(gg[cc],PB[pi][:,0:TB],[pbk(pi)],[('gg',cc)],'act')
                        for ti in range(NTB):
                            for hh in range(2):
                                pr=slice(hh*64,(hh+1)*64)
                                am=AM[cc][ti][hh]; tto=TT[cc][ti][hh]
                                kam=('AM',cc,ti,hh); ktt=('TT',cc,ti,hh)
                                pi=mmrr.next()
                                mm(PB[pi][:,0:256],BTt[cc][pr,ti*128:(ti+1)*128],AR[cc][pr,ti,:,:],True,True,[('BTt',cc),('AR',cc)],[pbk(pi)])
                                mm(PB[pi][:,256:512],KTt[cc][pr,ti*128:(ti+1)*128],AR[cc][pr,ti,:,:],True,True,[('KTt',cc),('AR',cc)],[pbk(pi)])
                                tt('dve',am,PB[pi][:,:],maskA,ALU.mult,[pbk(pi),('c','maskf'),('c','maskb')],[kam])
                                pi=mmrr.next()
                                mm(PB[pi][:,0:128],AR[cc][pr,ti,0,:],BTt[cc][pr,ti*128:(ti+1)*128],True,True,[('BTt',cc),('AR',cc)],[pbk(pi)])
                                Pk=[pp0,pp1]; ci_=0
                                tt('dve',Pk[0][:,0:128],PB[pi][:,0:128],maskN,ALU.mult,[pbk(pi),('c','maskf'),('c','maskb')],[('pp',0)])
                                ev_copy(Pk[0][:,128:256],am[:,0:128],[kam],[('pp',0)],'pool')
                                tcur,tnext=tt0,tt1; kcur,knext=('ttx',0),('ttx',1)
                                tt('pool',tcur,identb,am[:,0:128],ALU.add,[('c','identb'),kam],[kcur])
                                for k in range(1,6):
                                    src=Pk[ci_]; dst=Pk[1-ci_]; ks=('pp',ci_); kd_=('pp',1-ci_)
                                    pi=mmrr.next()
                                    mm(PB[pi][:,0:128],src[:,128:256],src[:,0:128],True,True,[ks],[pbk(pi)])
                                    if k<5:
                                        mm(PB[pi][:,128:256],src[:,0:128],src[:,128:256],True,True,[ks],[pbk(pi)])
                                        ev_copy(dst[:,0:256],PB[pi][:,0:256],[pbk(pi)],[kd_],'act')
                                    else:
                                        ev_copy(dst[:,0:128],PB[pi][:,0:128],[pbk(pi)],[kd_],'act')
                                    pi=mmrr.next()
                                    mm(PB[pi][:,0:128],dst[:,0:128],tcur,True,True,[kd_,kcur],[pbk(pi)])
                                    outt=tto if k==5 else tnext
                                    okey=ktt if k==5 else knext
                                    tt('dve',outt,PB[pi][:,0:128],tcur,ALU.add,[pbk(pi),kcur],[okey])
                                    tcur,tnext=tnext,tcur; kcur,knext=knext,kcur
                                    ci_=1-ci_
                    chunks=range(NCK) if d==0 else range(NCK-1,-1,-1)
                    for cc in range(4):
                        for ch in chunks:
                            ti=ch//2; half=ch%2
                            tb=slice(half*64,(half+1)*64); cs=slice(half*64,(half+1)*64)
                            kS=('Sbf',cc)
                            mm(PB[5][tb,0:128],AR[cc][:,ti,0,cs],Sbf[cc],True,False,[('AR',cc),kS],[pbk(5)])
                            for hh in range(2):
                                hs=slice(hh*64,(hh+1)*64)
                                mm(PB[5][tb,hs],AM[cc][ti][hh][tb,256+half*64:256+half*64+64],VT[cc][tb,ti,hs],False,hh==1,[('AM',cc,ti,hh),('VT',cc)],[pbk(5)])
                            ev_copy(Xsb[tb,:],PB[5][tb,0:128],[pbk(5)],[('Xsb',)],'act')
                            for hh in range(2):
                                hs=slice(hh*64,(hh+1)*64)
                                mm(PB[6][tb,hs],TT[cc][ti][hh][tb,cs],Xsb[tb,hs],True,True,[('TT',cc,ti,hh),('Xsb',)],[pbk(6)])
                            ev_copy(Usb[tb,:],PB[6][tb,0:128],[pbk(6)],[('Usb',)],'dve')
                            mm(PB[7][:,0:64],Sbf[cc],AR[cc][:,ti,1,cs],True,False,[kS,('AR',cc)],[pbk(7)])
                            for hh in range(2):
                                hs=slice(hh*64,(hh+1)*64)
                                mm(PB[7][hs,0:64],Usb[tb,hs],AM[cc][ti][hh][tb,128+half*64:128+half*64+64],False,False,[('Usb',),('AM',cc,ti,hh)],[pbk(7)])
                                mm(PB[7][hs,0:64],VT[cc][tb,ti,hs],AM[cc][ti][hh][tb,384+half*64:384+half*64+64],False,hh==1,[('VT',cc),('AM',cc,ti,hh)],[pbk(7)])
                            ev_copy(YT[cc][:,ch*64:(ch+1)*64],PB[7][:,0:64],[pbk(7)],[('YT',cc)],'act')
                            mm(PB[4][:,0:128],BH[cc][tb,ti,:],Usb[tb,:],True,False,[('BH',cc),('Usb',)],[pbk(4)])
                            mm(PB[4][:,0:128],KH[cc][tb,ti,:],VT[cc][tb,ti,:],False,True,[('KH',cc),('VT',cc)],[pbk(4)])
                            for hh in range(2):
                                hs=slice(hh*64,(hh+1)*64)
                                stt('dve',S32[cc][hs,hs],S32[cc][hs,hs],gL[cc][hs,ch:ch+1],PB[4][hs,hs],ALU.mult,ALU.add,[('S32',cc),('gL',cc),pbk(4)],[('S32',cc)])
                            ev_copy(Sbf[cc],S32[cc],[('S32',cc)],[kS],'act')
                    for cc in range(4):
                        if d==1:
                            dma(YB[cc][:,t0:t0+TB],YT[cc],[('YT',cc)],[('YB',)])
                        else:
                            yb,yc,sq,ob=pyb,pyc,psq,pob
                            K=lambda n:('po',n)
                            dma(yb,YB[cc][:,t0:t0+TB],[('YB',)],[K('yb')])
                            tt('pool',yb,yb,YT[cc],ALU.add,[K('yb'),('YT',cc)],[K('yb')])
                            pi=mmrr.next()
                            mm(PB[pi][:,0:TB],bonesf,yb,True,True,[('c','bones'),K('yb')],[pbk(pi)])
                            stt('dve',yc,PB[pi][:,0:TB],-1.0/64,yb,ALU.mult,ALU.add,[pbk(pi),K('yb')],[K('yc')])
                            tt('pool',sq,yc,yc,ALU.mult,[K('yc')],[K('sq')])
                            pi=mmrr.next()
                            mm(PB[pi][:,0:TB],bonesf,sq,True,True,[('c','bones'),K('sq')],[pbk(pi)])
                            act(sq,PB[pi][:,0:TB],AF.Sqrt,[pbk(pi)],[K('sq')],bias=GN_EPS,scale=1.0/64)
                            P.op('dve',lambda e,sq=sq:e.reciprocal(out=sq,in_=sq),r=[K('sq')],w=[K('sq')])
                            tt('pool',yc,yc,sq,ALU.mult,[K('yc'),K('sq')],[K('yc')])
                            ts('dve',yc,yc,pcol[:,47+cc:48+cc],pcol[:,51+cc:52+cc],ALU.mult,ALU.add,[K('yc'),('pcol',)],[K('yc')])
                            tt('pool',yc,yc,bonus[cc],ALU.add,[K('yc'),('bonus',cc)],[K('yc')])
                            tt('dve',ob,yc,gg[cc],ALU.mult,[K('yc'),('gg',cc)],[K('ob')])
                            dma(BOT[:,cc,t0:t0+TB],ob,[K('ob')],[('BOT',)])
                    P.barrier()

            TC=512
            for cg in (range(S//TC) if 'C' in PHASES else []):
                A.top=ltop; t0=cg*TC
                X1=A.f32(4*1024).rearrange("p (n d) -> p n d",n=4)
                X1T=A.bf(8*TC).rearrange("p (k t) -> p k t",k=8)
                c1top=A.top
                MT=A.bf(8*TC).rearrange("p (k t) -> p k t",k=8)
                XO=[A.bf(4*TC).rearrange("p (k t) -> p k t",k=4) for _ in range(3)]
                wbrs=A.bf(12288).rearrange("p (b k d) -> p b k d",b=3,k=4)
                wos=A.bf(8192).rearrange("p (k d) -> p k d",k=8)
                gt=[A.bf(TC) for _ in range(3)]
                acc=A.f32(TC); tmp=A.f32(TC)
                xs=[A.f32(1024) for _ in range(2)]; ypre=A.f32(1024); xbb=A.bf(1024)
                st1=A.f32(4); junk=A.f32(1024)
                for b,src in enumerate((AOT,BOT,COT)):
                    dma(XO[b],src[:,:,t0:t0+TC],[('AOT',),('BOT',),('COT',)],[('XO',b)])
                dma(wbrs.rearrange("p b k d -> p (b k d)"),wb['wbr'][l],[('wb',)],[('wbrs',)])
                dma(wos.rearrange("p k d -> p (k d)"),wb['wo'][l],[('wb',)],[('wos',)])
                for m in range(8):
                    for b in range(3):
                        dma(gt[b],GT[b*8+m][:,t0:t0+TC],[('GT',)],[('gt',b)])
                        pi=mmrr.next()
                        for k in range(4):
                            mm(PB[pi][:,:],wbrs[:,b,k,m*128:(m+1)*128],XO[b][:,k,:],k==0,k==3,[('wbrs',),('XO',b)],[pbk(pi)])
                        if b==0:
                            tt('dve',acc,PB[pi][:,:],gt[b],ALU.mult,[pbk(pi),('gt',b)],[('acc',)])
                        else:
                            tt('dve',tmp,PB[pi][:,:],gt[b],ALU.mult,[pbk(pi),('gt',b)],[('tmp',)])
                            if b==1: tt('pool',acc,acc,tmp,ALU.add,[('acc',),('tmp',)],[('acc',)])
                            else: tt('pool',MT[:,m,:],acc,tmp,ALU.add,[('acc',),('tmp',)],[('MT',)])
                def layer_norm(gi,out_ap,okey):
                    red(st1[:,0:1],ypre,[('ypre',)],[('st1',)])
                    ts('dve',st1[:,1:2],st1[:,0:1],-1.0/1024,0.0,ALU.mult,ALU.add,[('st1',)],[('st1',)])
                    ts('dve',ypre,ypre,st1[:,1:2],0.0,ALU.add,ALU.add,[('ypre',),('st1',)],[('ypre',)])
                    tt('pool',junk,ypre,ypre,ALU.mult,[('ypre',)],[('junk',)])
                    red(st1[:,2:3],junk,[('junk',)],[('st1',)])
                    act(st1[:,3:4],st1[:,2:3],AF.Sqrt,[('st1',)],[('st1',)],bias=LN_EPS,scale=1.0/1024)
                    recip(st1[:,3:4],st1[:,3:4],[('st1',)],[('st1',)])
                    stt('dve',ypre,ypre,st1[:,3:4],lnrow[gi],ALU.mult,ALU.mult,[('ypre',),('st1',),('lnrow',gi)],[('ypre',)])
                    tt('pool',out_ap,ypre,lnrow[gi+1],ALU.add,[('ypre',),('lnrow',gi+1)],[okey])
                for ti in range(TC//128):
                    i=ti%2
                    dma(xs[i],xsrc[t0+ti*128:t0+(ti+1)*128,:],[('xsrc',)],[('xs',i)])
                    for half in range(2):
                        pi=mmrr.next()
                        for k in range(8):
                            mm(PB[pi][:,:],MT[:,k,ti*128:(ti+1)*128],wos[:,k,half*512:(half+1)*512],k==0,k==7,[('MT',),('wos',)],[pbk(pi)])
                        stt('dve',ypre[:,half*512:(half+1)*512],xs[i][:,half*512:(half+1)*512],ALPHA,PB[pi][:,:],ALU.mult,ALU.add,[('xs',i),pbk(pi)],[('ypre',)])
                    layer_norm(0,X1[:,ti,:],('X1',))
                    ev_copy(xbb,X1[:,ti,:],[('X1',)],[('xbb',)],'act')
                    pi=tprr.next()
                    for k in range(8):
                        tr(TPB(pi)[:,k*128:(k+1)*128],xbb[:,k*128:(k+1)*128],identb,[('xbb',),('c','identb')],[pbk(pi)])
                    ev_copy(X1T[:,:,ti*128:(ti+1)*128],TPB(pi).rearrange("p (k t) -> p k t",k=8),[pbk(pi)],[('X1T',)],'dve')
                P.barrier()
                A.top=c1top
                H=A.bf(32*TC).rearrange("p (k t) -> p k t",k=32)
                w2f=A.bf(32768).rearrange("p (k d) -> p k d",k=32)
                w1c=[A.bf(1024) for _ in range(3)]
                rl=[A.f32(TC) for _ in range(2)]
                ypre=A.f32(1024); st1=A.f32(4); junk=A.f32(1024); ot=[A.f32(1024) for _ in range(2)]
                dma(w2f.rearrange("p k d -> p (k d)"),wb['ff2'][l],[('wb',)],[('w2f',)])
                for m in range(32):
                    i=m%3; ri=m%2
                    dma(w1c[i],wb['ff1'][l,m],[('wb',)],[('w1c',i)])
                    pi=mmrr.next()
                    for k in range(8):
                        mm(PB[pi][:,:],w1c[i][:,k*128:(k+1)*128],X1T[:,k,:],k==0,k==7,[('w1c',i),('X1T',)],[pbk(pi)])
                    act(rl[ri],PB[pi][:,:],AF.Relu,[pbk(pi)],[('rl',ri)])
                    tt('pool' if m%2 else 'dve',H[:,m,:],rl[ri],rl[ri],ALU.mult,[('rl',ri)],[('H',)])
                for ti in range(TC//128):
                    for half in range(2):
                        pi=mmrr.next()
                        for k in range(32):
                            mm(PB[pi][:,:],H[:,k,ti*128:(ti+1)*128],w2f[:,k,half*512:(half+1)*512],k==0,k==31,[('H',),('w2f',)],[pbk(pi)])
                        stt('dve',ypre[:,half*512:(half+1)*512],X1[:,ti,half*512:(half+1)*512],ALPHA,PB[pi][:,:],ALU.mult,ALU.add,[('X1',),pbk(pi)],[('ypre',)])
                    layer_norm(2,ot[ti%2],('ot',ti%2))
                    dma(xdst[t0+ti*128:t0+(ti+1)*128,:],ot[ti%2],[('ot',ti%2)],[('xdst',)])
                P.barrier()
    P.run()
    for cm in reversed(pbcm): cm.__exit__(None,None,None)
    return nc

N_CORES=8
_CACHE={}
def _run(job_S, depth, per_core_x, inp, debug=False):
    key=(tuple(job_S),depth)
    nc=build(job_S,depth,debug)
    consts=host_consts(np.asarray(inp['rel_bias'],np.float32))
    wts=host_weights({k:np.asarray(v,np.float32) for k,v in inp.items() if k not in ('x_prompt','x_sample')},depth)
    in_maps=[]
    for c in range(len(per_core_x)):
        m={}
        m.update(consts); m.update(wts)
        for j,x in enumerate(per_core_x[c]): m[f"x{j}"]=np.ascontiguousarray(x,dtype=np.float32)
        in_maps.append(m)
    res=run_bass_kernel_spmd(nc,in_maps,core_ids=list(range(len(per_core_x))))
    return res.results

def kernel(**inputs):
    xp=np.asarray(inputs['x_prompt'],np.float32); xs=np.asarray(inputs['x_sample'],np.float32)
    job_S=[2048,2048,16384]
    per_core=[]
    zero=np.zeros((16384,D),np.float32)
    for c in range(N_CORES):
        per_core.append([xp[2*c],xp[2*c+1],xs[c] if c<2 else zero])
    res=_run(job_S,4,per_core,inputs)
    yp=np.stack([res[c][f"y{j}"] for c in range(N_CORES) for j in range(2)],0).astype(np.float32)
    ys=np.stack([res[c]["y2"] for c in range(2)],0).astype(np.float32)
    return (yp,ys)
```

```python
import numpy as np
import contextlib, math, os
PHASES=os.environ.get('KPHASES','APTRC')
import concourse.bass as bass
import concourse.mybir as mybir
from concourse.bass_utils import run_bass_kernel_spmd
F32=mybir.dt.float32; BF16=mybir.dt.bfloat16
AF=mybir.ActivationFunctionType; ALU=mybir.AluOpType; AX=mybir.AxisListType
ENG=['pe','act','dve','pool','sp']
NROT=8
SAME_ENG_SYNC=True
D=1024; C=512; NIN=6272; DFF=4096
POOL_WINDOWS=(2,4,8,16)
ALPHA=(2*4)**0.25
LN_EPS=1e-5; GN_EPS=64e-5
NCOL=64

class Prog:
    def __init__(self, nc):
        self.nc=nc
        self.ops={e:[] for e in ENG}
        self.segs=[]
        self.lastw={}; self.readers={}
        self.dma_cnt={e:0 for e in ENG}
        self.last_rec={}
    def semid(self, rec):
        return ('d',rec['eng'],rec['dk']%NROT) if rec['dma'] else ('c',rec['eng'])
    def op(self, eng, fn, r=(), w=(), dma=False):
        rec=dict(eng=eng, fn=fn, waits=[], signal=bool(dma), dma=bool(dma))
        if dma:
            rec['dk']=self.dma_cnt[eng]; self.dma_cnt[eng]+=1
        def need(p):
            if p is None or p is rec: return
            if (not p['dma']) and (not dma) and p['eng']==eng and (eng=='pe' or not SAME_ENG_SYNC): return
            p['signal']=True; rec['waits'].append(p)
        for key in r: need(self.lastw.get(key))
        for key in w:
            need(self.lastw.get(key))
            for p in self.readers.get(key,{}).values(): need(p)
        sid=self.semid(rec)
        for key in r: self.readers.setdefault(key,{})[sid]=rec
        for key in w:
            self.lastw[key]=rec; self.readers[key]={}
        self.ops[eng].append(rec); self.last_rec[sid]=rec
        return rec
    def _setup(self):
        if getattr(self,'sems',None) is not None: return
        nc=self.nc
        self.stack=contextlib.ExitStack()
        self.sems={}
        for e in ENG:
            self.sems[('c',e)]=self.stack.enter_context(nc.semaphore(f"c_{e}"))
        for i in range(NROT):
            self.sems[('d','sp',i)]=self.stack.enter_context(nc.semaphore(f"d_sp_{i}"))
        self.emitted={e:0 for e in ENG}; self.ccnt={e:0 for e in ENG}; self.waited={e:{} for e in ENG}
    def barrier(self):
        self._setup()
        lasts=list(self.last_rec.values())
        for e in ENG:
            rec=dict(eng=e, fn=None, waits=[], signal=False, dma=False)
            for p in lasts:
                if p['eng']==e and not p['dma'] and e=='pe': continue
                p['signal']=True; rec['waits'].append(p)
            self.ops[e].append(rec)
        self.lastw={}; self.readers={}
        self.emit_segment()
    def emit_segment(self):
        nc=self.nc; sems=self.sems
        for e in ENG:
            for rec in self.ops[e][self.emitted[e]:]:
                if rec['dma']:
                    k=rec['dk']; rec['sv']=(('d',e,k%NROT),16*(k//NROT+1))
                elif rec['signal']:
                    self.ccnt[e]+=1; rec['sv']=(('c',e),self.ccnt[e])
        nosync=bool(os.environ.get('KNOSYNC'))
        with nc.Block() as block:
            def replay(ename):
                def f(e):
                    wd=self.waited[ename]
                    for rec in self.ops[ename][self.emitted[ename]:]:
                        ws={}
                        for p in rec['waits']:
                            s,v=p['sv']
                            if v>ws.get(s,0): ws[s]=v
                        for s,v in ws.items():
                            if wd.get(s,0)>=v: continue
                            if not nosync: e.wait_ge(sems[s],v)
                            wd[s]=v
                        if rec['fn'] is None: continue
                        ins=rec['fn'](e)
                        if rec['signal'] and not nosync:
                            ins.then_inc(sems[rec['sv'][0]],16 if rec['dma'] else 1)
                        rec['fn']=None
                return f
            block.tensor(replay('pe')); block.scalar(replay('act')); block.vector(replay('dve'))
            block.gpsimd(replay('pool')); block.sync(replay('sp'))
        for e in ENG: self.emitted[e]=len(self.ops[e])
    def run(self):
        self._setup()
        lasts=list(self.last_rec.values())
        rec=dict(eng='sp', fn=None, waits=[], signal=False, dma=False)
        for p in lasts:
            p['signal']=True; rec['waits'].append(p)
        self.ops['sp'].append(rec)
        self.emit_segment()
        self.stack.close()

class Arena:
    def __init__(self, nc, words):
        self.nc=nc
        self.t=nc.alloc_sbuf_tensor("arena",[128,words],F32); self.words=words; self.top=0; self.n=0
        self.base=nc.lookup_mloc(self.t).addr
        self.gen=0; self.cache={}
    def _get(self, off_words, n, dt):
        key=(self.gen,off_words,n,str(dt))
        t=self.cache.get(key)
        if t is None:
            t=self.nc.alloc_sbuf_tensor_at(f"a{len(self.cache)}",[128,n],dt,offset=self.base+4*off_words)
            self.cache[key]=t
        return t
    def f32(self, n0):
        n=(n0+7)//8*8
        assert self.top+n<=self.words, (self.top,n,self.words)
        t=self._get(self.top,n,F32); self.top+=n
        return t[:,0:n0]
    def bf(self, n):
        w=(n+15)//16*8
        assert self.top+w<=self.words, (self.top,w,self.words)
        t=self._get(self.top,2*w,BF16); self.top+=w
        return t[:,0:n]

def _t5_bucket(rel):
    half=16; max_exact=8
    n=np.abs(rel)
    large=max_exact+(np.log(np.maximum(n,1)/max_exact)/math.log(128/max_exact)*(half-max_exact)).astype(np.int32)
    large=np.minimum(large,half-1)
    return (rel>0).astype(np.int32)*half+np.where(n<max_exact,n,large)

def host_consts(rel_bias):
    c={}
    c['ident']=np.eye(128,dtype=np.float32)
    p=np.arange(128)[:,None]; f=np.arange(128)[None,:]
    same=(p//64)==(f//64)
    MUs=((p<f)&same).astype(np.float32); MUi=((p<=f)&same).astype(np.float32)
    MLs=((p>f)&same).astype(np.float32); MLi=((p>=f)&same).astype(np.float32)
    c['maskf']=np.concatenate([MUs,MUi,MUs,MUi],1); c['maskb']=np.concatenate([MLs,MLi,MLs,MLi],1)
    rs=np.ones((128,512),np.float32); rs[:,::64]=0.0; c['restart']=rs
    c['bones']=same.astype(np.float32)
    k=np.arange(128)[:,None]; q=np.arange(128)[None,:]
    ab=np.zeros((128,3,2,4,128),np.float32)
    for kbi,kb in enumerate((-1,0,1)):
        rel=kb*128+k-q
        bk=_t5_bucket(rel); ok=np.abs(rel)<=128
        for kvh in range(2):
            for g in range(4):
                ab[:,kbi,kvh,g,:]=np.where(ok,rel_bias[bk,kvh*4+g],np.float32(-30000.0))
    c['attb']=ab.reshape(128,3072)
    op=np.zeros((128,2,128),np.float32); op[:,0,0:64]=1; op[:,1,64:128]=1; c['onespad']=op.reshape(128,256)
    pe=np.ones((128,4,2,8),np.float32)
    for g,w in enumerate(POOL_WINDOWS):
        h=w//2
        for t in range(h): pe[:,g,0,t]=w/(t+h)
        for i in range(h-1): pe[:,g,1,7-i]=w/(i+1+h)
    c['pedge']=pe.reshape(128,64)
    return c

def host_weights(inp, depth):
    L=depth; w={}
    w_in=inp['w_in'][:L]
    cols=list(range(0,C+1920))
    oq=C+1920
    for j in range(4):
        cols+=list(range(oq+j*64,oq+j*64+64))+list(range(oq+(4+j)*64,oq+(4+j)*64+64))
    cols+=list(range(oq+512,NIN))
    wp=w_in[:,:,cols]
    w['win']=np.ascontiguousarray(wp.reshape(L,8,128,49,128).transpose(0,3,2,1,4).reshape(L,49,128,1024))
    w['poolw']=np.ascontiguousarray(inp['pool_w'][:L].transpose(0,2,1,3).reshape(L,128,512))
    w['w2']=np.ascontiguousarray(inp['rwkv_w2'][:L].reshape(L,128,512))
    w['a2']=np.ascontiguousarray(inp['rwkv_a2'][:L].reshape(L,128,512))
    w['g2']=np.ascontiguousarray(inp['rwkv_g2'][:L])
    wb=inp['w_branch'][:L].copy()
    perm=[kvh*256+g*64+d for g in range(4) for kvh in range(2) for d in range(64)]
    wb[:,2]=wb[:,2][:,perm]
    w['wbr']=np.ascontiguousarray(wb.reshape(L,3,4,128,1024).transpose(0,3,1,2,4).reshape(L,128,12*1024))
    w['wo']=np.ascontiguousarray(inp['w_o'][:L].reshape(L,8,128,1024).transpose(0,2,1,3).reshape(L,128,8*1024))
    w['ff1']=np.ascontiguousarray(inp['w_ff1'][:L].reshape(L,8,128,32,128).transpose(0,3,2,1,4).reshape(L,32,128,1024))
    w['ff2']=np.ascontiguousarray(inp['w_ff2'][:L].reshape(L,32,128,1024).transpose(0,2,1,3).reshape(L,128,32*1024))
    pc=np.zeros((L,128,NCOL),np.float32)
    pc[:,:,0:15]=inp['rwkv_mu'][:L].reshape(L,15,128).transpose(0,2,1)
    pc[:,:,15:19]=inp['pool_scale'][:L].reshape(L,4,128).transpose(0,2,1)
    pc[:,:,19:27]=inp['rwkv_w0'][:L].reshape(L,8,128).transpose(0,2,1)
    pc[:,:,27:35]=inp['rwkv_a0'][:L].reshape(L,8,128).transpose(0,2,1)
    pc[:,:,35:39]=inp['rwkv_k_k'][:L].reshape(L,4,128).transpose(0,2,1)
    pc[:,:,39:43]=inp['rwkv_k_a'][:L].reshape(L,4,128).transpose(0,2,1)
    pc[:,:,43:47]=inp['rwkv_r_k'][:L].reshape(L,4,128).transpose(0,2,1)
    pc[:,:,47:51]=inp['rwkv_gn_g'][:L].reshape(L,4,128).transpose(0,2,1)
    pc[:,:,51:55]=inp['rwkv_gn_b'][:L].reshape(L,4,128).transpose(0,2,1)
    sk=inp['attn_sink'][:L].reshape(L,2,1,4)
    pc[:,:,55:59]=np.broadcast_to(sk,(L,2,64,4)).reshape(L,128,4)
    w['pcol']=pc
    w['lnrow']=np.ascontiguousarray(np.broadcast_to(
        np.stack([inp['ln1_g'][:L],inp['ln1_b'][:L],inp['ln2_g'][:L],inp['ln2_b'][:L]],1)[:,:,None,:],(L,4,128,1024)))
    return w

WSHAPES=lambda L:{'win':[L,49,128,1024],'poolw':[L,128,512],'w2':[L,128,512],'a2':[L,128,512],'g2':[L,128,512],
        'wbr':[L,128,12288],'wo':[L,128,8192],'ff1':[L,32,128,1024],'ff2':[L,128,32768],'pcol':[L,128,NCOL],'lnrow':[L,4,128,1024]}
CSHAPES={'ident':[128,128],'maskf':[128,512],'maskb':[128,512],'restart':[128,512],'bones':[128,128],'attb':[128,3072],
         'onespad':[128,256],'pedge':[128,64]}
BFW=['win','poolw','w2','a2','g2','wbr','wo','ff1','ff2']

def build(job_S, depth, debug=False):
    IK='ExternalOutput' if debug else 'Internal'
    nc=bass.Bass("TRN2", target_bir_lowering=False)
    P=Prog(nc); L=depth
    SMAX=max(job_S)
    din={}
    for n,s in WSHAPES(L).items(): din[n]=nc.dram_tensor(n,s,F32,kind="ExternalInput").ap()
    for n,s in CSHAPES.items(): din[n]=nc.dram_tensor(n,s,F32,kind="ExternalInput").ap()
    xin=[nc.dram_tensor(f"x{j}",[S,D],F32,kind="ExternalInput").ap() for j,S in enumerate(job_S)]
    yout=[nc.dram_tensor(f"y{j}",[S,D],F32,kind="ExternalOutput").ap() for j,S in enumerate(job_S)]
    wb={n:nc.dram_tensor(n+"_b",WSHAPES(L)[n],BF16,kind="Internal").ap() for n in BFW}
    ZP=nc.dram_tensor("ZP",[4,128,SMAX],F32,kind=IK).ap()
    ZR=nc.dram_tensor("ZR",[15,128,SMAX],F32,kind=IK).ap()
    QT=nc.dram_tensor("QT",[128,4,SMAX],BF16,kind=IK).ap()
    KT=nc.dram_tensor("KT",[128,SMAX],BF16,kind=IK).ap()
    VTOK=nc.dram_tensor("VTOK",[SMAX,128],BF16,kind=IK).ap()
    GT=nc.dram_tensor("GT",[24,128,SMAX],BF16,kind=IK).ap()
    AOT=nc.dram_tensor("AOT",[128,4,SMAX],BF16,kind=IK).ap()
    BOT=nc.dram_tensor("BOT",[128,4,SMAX],BF16,kind=IK).ap()
    COT=nc.dram_tensor("COT",[128,4,SMAX],BF16,kind=IK).ap()
    YB=nc.dram_tensor("YB",[4,128,SMAX],F32,kind=IK).ap()
    XTOK=nc.dram_tensor("XTOK",[SMAX,D],F32,kind=IK).ap()
    A=Arena(nc,49152)
    PB=[None]*8; PBB=[None]*8; pbcm=[]
    pbk=lambda i:('pb',i)
    class RR:
        def __init__(s,ids): s.ids=ids; s.i=0
        def next(s):
            i=s.ids[s.i%len(s.ids)]; s.i+=1; return i
    mmrr=RR([0,1,2]); tprr=RR([3,4])
    def TPB(i): return PB[i][:, :].bitcast(BF16)
    def new_psum(tag):
        for cm in reversed(pbcm): cm.__exit__(None,None,None)
        pbcm.clear()
        for i in range(8):
            cm=nc.psum_tensor(f"pb{i}_{tag}",[128,512],F32); PB[i]=cm.__enter__(); pbcm.append(cm)
    new_psum('init')
    def dma(out,in_,r,w): P.op('sp',lambda e:e.dma_start(out=out,in_=in_),r=r,w=w,dma=True)
    ccnt=[0]
    def ev_copy(out,in_,r,w,eng=None):
        if eng is None:
            eng='act' if ccnt[0]%2==0 else 'dve'; ccnt[0]+=1
        if eng=='act': P.op('act',lambda e:e.activation(out=out,in_=in_,func=AF.Copy),r=r,w=w)
        elif eng=='dve': P.op('dve',lambda e:e.tensor_copy(out=out,in_=in_),r=r,w=w)
        else: P.op('pool',lambda e:e.tensor_copy(out=out,in_=in_),r=r,w=w)
    def tt(eng,out,a,b,op,r,w): P.op(eng,lambda e:e.tensor_tensor(out=out,in0=a,in1=b,op=op),r=r,w=w)
    def ts(eng,out,a,s1,s2,op0,op1,r,w): P.op(eng,lambda e:e.tensor_scalar(out=out,in0=a,scalar1=s1,scalar2=s2,op0=op0,op1=op1),r=r,w=w)
    def stt(eng,out,a,s,b,op0,op1,r,w): P.op(eng,lambda e:e.scalar_tensor_tensor(out=out,in0=a,scalar=s,in1=b,op0=op0,op1=op1),r=r,w=w)
    def act(out,in_,func,r,w,bias=None,scale=None):
        kw={}
        if bias is not None: kw['bias']=bias
        if scale is not None: kw['scale']=scale
        P.op('act',lambda e:e.activation(out=out,in_=in_,func=func,**kw),r=r,w=w)
    def red(out,in_,r,w): P.op('dve',lambda e:e.tensor_reduce(out=out,in_=in_,axis=AX.X,op=ALU.add),r=r,w=w)
    def recip(out,in_,r,w): P.op('dve',lambda e:e.reciprocal(out=out,in_=in_),r=r,w=w)
    def mset(ap,val,w): P.op('pool',lambda e:e.memset(ap,val),w=w)
    def mm(out,lhsT,rhs,start,stop,r,w): P.op('pe',lambda e:e.matmul(out,lhsT,rhs,start=start,stop=stop),r=r,w=w)
    def tr(out,in_,idn,r,w): P.op('pe',lambda e:e.transpose(out,in_,idn),r=r,w=w)

    identf=A.f32(128); maskf=A.f32(512); maskb=A.f32(512); restart=A.f32(512); bonesf=A.f32(128)
    attb=A.f32(3072); pedge=A.f32(64); tmpc=A.f32(256)
    identb=A.bf(128); bonesb=A.bf(128); onespad=A.bf(256)
    pcol=A.f32(NCOL); omm=A.f32(16); hmu=A.f32(16); omka=A.f32(4); rkh=A.f32(4); esink=A.f32(512); esk4=A.f32(4)
    lnrow=[A.f32(1024) for _ in range(4)]
    for n,t in [('ident',identf),('maskf',maskf),('maskb',maskb),('restart',restart),('bones',bonesf),('attb',attb),('pedge',pedge),('onespad',tmpc)]:
        dma(t,din[n][:,:],[],[('c',n)])
    ev_copy(identb,identf,[('c','ident')],[('c','identb')],'dve')
    ev_copy(bonesb,bonesf,[('c','bones')],[('c','bonesb')],'dve')
    ev_copy(onespad,tmpc,[('c','onespad')],[('c','onespadb')],'dve')
    P.barrier()
    base_top=A.top
    def cast_rows(src,dst,ncols):
        CH=4096
        for c0 in range(0,ncols,CH):
            cw=min(CH,ncols-c0)
            i=castrr.next()
            dma(cst[i][:,0:cw],src[:,c0:c0+cw],[],[('cst',i)])
            ev_copy(csb[i][:,0:cw],cst[i][:,0:cw],[('cst',i)],[('csb',i)])
            dma(dst[:,c0:c0+cw],csb[i][:,0:cw],[('csb',i)],[('wb',)])
    cst=[A.f32(4096) for _ in range(3)]; csb=[A.bf(4096) for _ in range(3)]; castrr=RR([0,1,2])
    for l in range(L):
        for n in BFW:
            s=WSHAPES(L)[n]
            if len(s)==4:
                for c in range(s[1]): cast_rows(din[n][l,c],wb[n][l,c],s[3])
            else:
                cast_rows(din[n][l],wb[n][l],s[2])
    P.barrier()

    for j,S in enumerate(job_S):
        NT=S//128
        for l in range(L):
            xsrc=xin[j] if l==0 else XTOK
            xdst=yout[j] if l==L-1 else XTOK
            A.top=base_top; A.gen+=1; new_psum(f'{j}_{l}')
            dma(pcol,din['pcol'][l],[],[('pcol',)])
            for i in range(4): dma(lnrow[i],din['lnrow'][l,i],[],[('lnrow',i)])
            ts('dve',omm[:,0:15],pcol[:,0:15],-1.0,1.0,ALU.mult,ALU.add,[('pcol',)],[('omm',)])
            ts('dve',hmu[:,0:15],pcol[:,0:15],0.5,0.0,ALU.mult,ALU.add,[('pcol',)],[('hmu',)])
            ts('dve',omka,pcol[:,39:43],-1.0,1.0,ALU.mult,ALU.add,[('pcol',)],[('omka',)])
            ts('dve',rkh,pcol[:,43:47],0.5,0.0,ALU.mult,ALU.add,[('pcol',)],[('rkh',)])
            act(esk4,pcol[:,55:59],AF.Exp,[('pcol',)],[('esk4',)])
            for g in range(4):
                ts('dve',esink[:,g*128:(g+1)*128],identf,0.0,esk4[:,g:g+1],ALU.mult,ALU.add,[('esk4',),('c','ident')],[('esink',)])
            P.barrier()
            ltop=A.top
            w0c=lambda d,cc: pcol[:,19+d*4+cc:20+d*4+cc]
            a0c=lambda d,cc: pcol[:,27+d*4+cc:28+d*4+cc]

            TA=min(S,2048)
            for sg in (range(S//TA) if 'A' in PHASES else []):
                A.top=ltop; t0=sg*TA
                XT=A.bf(8*TA).rearrange("p (k t) -> p k t",k=8)
                xs=[A.f32(1024) for _ in range(2)]; xb=[A.bf(1024) for _ in range(2)]
                wch=[A.bf(1024) for _ in range(3)]
                stg=[]; stgb=[]
                for _ in range(2):
                    o=A.top; stg.append(A.f32(TA)); o2=A.top; A.top=o; stgb.append(A.bf(TA)); A.top=o2
                for ti in range(TA//128):
                    i=ti%2
                    dma(xs[i],xsrc[t0+ti*128:t0+(ti+1)*128,:],[('xsrc',)],[('xs',i)])
                    ev_copy(xb[i],xs[i],[('xs',i)],[('xb',i)],'act')
                    pi=tprr.next()
                    for k in range(8):
                        tr(TPB(pi)[:,k*128:(k+1)*128],xb[i][:,k*128:(k+1)*128],identb,[('xb',i),('c','identb')],[pbk(pi)])
                    ev_copy(XT[:,:,ti*128:(ti+1)*128],TPB(pi).rearrange("p (k t) -> p k t",k=8),[pbk(pi)],[('XT',)],'dve')
                wi=0
                for c in range(49):
                    if c==24: continue
                    i=wi%3; si=wi%2; wi+=1
                    dma(wch[i],wb['win'][l,c],[('wb',)],[('wch',i)])
                    isbf = c>=19
                    for g in range(TA//512):
                        pi=mmrr.next()
                        for k in range(8):
                            mm(PB[pi][:,:],wch[i][:,k*128:(k+1)*128],XT[:,k,g*512:(g+1)*512],k==0,k==7,[('wch',i),('XT',)],[pbk(pi)])
                        if c>=25:
                            act(stgb[si][:,g*512:(g+1)*512],PB[pi][:,:],AF.Sigmoid,[pbk(pi)],[('stg',si)])
                        elif isbf:
                            ev_copy(stgb[si][:,g*512:(g+1)*512],PB[pi][:,:],[pbk(pi)],[('stg',si)])
                        else:
                            ev_copy(stg[si][:,g*512:(g+1)*512],PB[pi][:,:],[pbk(pi)],[('stg',si)])
                    if c<4: dst=ZP[c][:,t0:t0+TA]; src=stg[si]; key=('ZP',)
                    elif c<19: dst=ZR[c-4][:,t0:t0+TA]; src=stg[si]; key=('ZR',)
                    elif c<23: dst=QT[:,c-19,t0:t0+TA]; src=stgb[si][:,0:TA]; key=('QT',)
                    elif c==23: dst=KT[:,t0:t0+TA]; src=stgb[si][:,0:TA]; key=('KT',)
                    else: dst=GT[c-25][:,t0:t0+TA]; src=stgb[si][:,0:TA]; key=('GT',)
                    dma(dst,src,[('stg',si)],[key])
                dma(wch[0],wb['win'][l,24],[('wb',)],[('wch',0)])
                vst=stgb[0][:,0:TA].rearrange("p (n c) -> p n c",c=128)
                for ti in range(TA//128):
                    pi=mmrr.next()
                    for k in range(8):
                        mm(PB[pi][:,0:128],XT[:,k,ti*128:(ti+1)*128],wch[0][:,k*128:(k+1)*128],k==0,k==7,[('wch',0),('XT',)],[pbk(pi)])
                    ev_copy(vst[:,ti,:],PB[pi][:,0:128],[pbk(pi)],[('stg',0)])
                for n0 in range(0,TA//128,4):
                    dma(VTOK[t0+n0*128:t0+(n0+4)*128,:].rearrange("(n p) c -> p n c",p=128),vst[:,n0:n0+4,:],[('stg',0)],[('VTOK',)])
                P.barrier()

            A.top=ltop
            plw=A.bf(512)
            dma(plw,wb['poolw'][l],[('wb',)],[('plw',)])
            ptop=A.top
            TP=min(S,2048)
            for sg in (range(S//TP) if 'P' in PHASES else []):
                t0=sg*TP; n=TP+16
                for g in range(4):
                    A.top=ptop
                    u=A.f32(n); p0=A.f32(n); p1=A.f32(n); pab=A.bf(TP); ostg=A.bf(TP)
                    lo=max(t0-8,0); hi=min(t0+TP+8,S)
                    if t0-8<0: P.op('pool',lambda e,u=u:e.memset(u[:,0:8],0.0),w=[('u',)])
                    if t0+TP+8>S: P.op('pool',lambda e,u=u,n=n:e.memset(u[:,n-8:n],0.0),w=[('u',)])
                    dma(u[:,lo-(t0-8):hi-(t0-8)],ZP[g][:,lo:hi],[('ZP',)],[('u',)])
                    tt('pool',p0[:,1:n],u[:,0:n-1],u[:,1:n],ALU.add,[('u',)],[('p0',)])
                    cur,nxt,ck,nk=p0,p1,('p0',),('p1',)
                    for k in range(1,g+1):
                        d_=2**(k-1)
                        tt('pool',nxt[:,d_:n-d_],cur[:,0:n-2*d_],cur[:,2*d_:n],ALU.add,[ck],[nk])
                        cur,nxt,ck,nk=nxt,cur,nk,ck
                    wdw=POOL_WINDOWS[g]
                    act(nxt[:,8:8+TP],cur[:,8:8+TP],AF.Copy,[ck],[nk],scale=1.0/wdw)
                    if sg==0:
                        tt('dve',nxt[:,8:16],nxt[:,8:16],pedge[:,g*16:g*16+8],ALU.mult,[nk,('c','pedge')],[nk])
                    if sg==S//TP-1:
                        tt('dve',nxt[:,TP:TP+8],nxt[:,TP:TP+8],pedge[:,g*16+8:g*16+16],ALU.mult,[nk,('c','pedge')],[nk])
                    tt('dve',pab,nxt[:,8:8+TP],u[:,8:8+TP],ALU.subtract,[nk,('u',)],[('pab',)])
                    for tg in range(TP//512):
                        pi=mmrr.next()
                        mm(PB[pi][:,:],plw[:,g*128:(g+1)*128],pab[:,tg*512:(tg+1)*512],True,True,[('plw',),('pab',)],[pbk(pi)])
                        act(ostg[:,tg*512:(tg+1)*512],PB[pi][:,:],AF.Copy,[pbk(pi)],[('ostg',)],scale=pcol[:,15+g:16+g])
                    dma(AOT[:,g,t0:t0+TP],ostg,[('ostg',)],[('AOT',)])
            P.barrier()

            TQ=min(S,2048)
            attb5=attb.rearrange("p (a b c) -> p a b c",a=3,b=2)
            for sg in (range(S//TQ) if 'T' in PHASES else []):
                A.top=ltop; t0=sg*TQ; nqb=TQ//128; nkb=nqb+2
                q=A.bf(4*TQ).rearrange("p (g t) -> p g t",g=4)
                kt=A.bf(TQ+256)
                vp=[A.bf(nkb*128).rearrange("p (n c) -> p n c",c=128) for _ in range(2)]
                cstg=A.bf(4*TQ).rearrange("p (g t) -> p g t",g=4)
                sb=[A.f32(512) for _ in range(2)]; pT=[A.bf(512) for _ in range(2)]
                den=A.f32(512); rec=A.f32(512)
                dma(q,QT[:,:,t0:t0+TQ],[('QT',)],[('q',)])
                lo=max(t0-128,0); hi=min(t0+TQ+128,S)
                dma(kt[:,lo-(t0-128):hi-(t0-128)],KT[:,lo:hi],[('KT',)],[('kt',)])
                for kvh in range(2):
                    P.op('pool',lambda e,v=vp[kvh]:e.memset(v,0.0),w=[('vp',kvh)])
                    b0=(lo-(t0-128))//128; nb=(hi-lo)//128
                    for n0 in range(0,nb,4):
                        n1=min(n0+4,nb)
                        dma(vp[kvh][:,b0+n0:b0+n1,kvh*64:(kvh+1)*64],VTOK[lo+n0*128:lo+n1*128,kvh*64:(kvh+1)*64].rearrange("(n p) c -> p n c",p=128),[('VTOK',)],[('vp',kvh)])
                it=0
                for qb in range(nqb):
                    gq=sg*nqb+qb
                    lst=[(kvh,kbi) for kvh in range(2) for kbi in range(3) if 0<=gq+kbi-1<NT]
                    for idx,(kvh,kbi) in enumerate(lst):
                        pr=slice(kvh*64,(kvh+1)*64)
                        pi=mmrr.next(); si=it%2; it+=1
                        kb=qb+kbi
                        mm(PB[pi][:,:],kt[pr,kb*128:(kb+1)*128],q[pr,:,qb*128:(qb+1)*128],True,True,[('kt',),('q',)],[pbk(pi)])
                        stt('dve',sb[si],PB[pi][:,:],0.125,attb5[:,kbi,kvh,:],ALU.mult,ALU.add,[pbk(pi),('c','attb')],[('sb',si)])
                        act(pT[si],sb[si],AF.Exp,[('sb',si)],[('pT',si)])
                        mm(PB[5][:,:],vp[kvh][:,kb,:],pT[si],idx==0,idx==len(lst)-1,[('vp',kvh),('pT',si)],[pbk(5)])
                        mm(PB[6][:,:],onespad[:,kvh*128:(kvh+1)*128],pT[si],idx==0,idx==len(lst)-1,[('c','onespadb'),('pT',si)],[pbk(6)])
                    tt('dve',den,PB[6][:,:],esink,ALU.add,[pbk(6),('esink',)],[('den',)])
                    recip(rec,den,[('den',)],[('rec',)])
                    tt('dve',cstg[:,:,qb*128:(qb+1)*128],PB[5][:,:].rearrange("p (g t) -> p g t",g=4),rec.rearrange("p (g t) -> p g t",g=4),ALU.mult,[pbk(5),('rec',)],[('cstg',)])
                dma(COT[:,:,t0:t0+TQ],cstg,[('cstg',)],[('COT',)])
                P.barrier()

            TB=256; NBLK=S//TB; NTB=TB//128; NCK=TB//64
            NEG=-math.exp(-0.5)
            for d in ((1,0) if 'R' in PHASES else ()):
                A.top=ltop
                w2s=A.bf(512); a2s=A.bf(512); g2s=A.bf(512)
                dma(w2s,wb['w2'][l],[('wb',)],[('w2s',)]); dma(a2s,wb['a2'][l],[('wb',)],[('a2s',)]); dma(g2s,wb['g2'][l],[('wb',)],[('g2s',)])
                S32=[A.f32(128) for _ in range(4)]; Sbf=[A.bf(128) for _ in range(4)]
                for cc in range(4):
                    P.op('pool',lambda e,t=S32[cc]:e.memset(t,0.0),w=[('S32',cc)])
                    P.op('pool',lambda e,t=Sbf[cc]:e.memset(t,0.0),w=[('Sbf',cc)])
                rtop=A.top
                maskA=maskf if d==0 else maskb
                maskN=maskb[:,0:128] if d==0 else maskf[:,0:128]
                blks=range(NBLK) if d==0 else range(NBLK-1,-1,-1)
                for blk in blks:
                    A.top=rtop; t0=blk*TB
                    zraw=A.f32(TB+2); t1=A.f32(TB); t2=A.f32(TB)
                    th=A.bf(TB); al=A.bf(TB); gs=A.bf(TB)
                    zr=[A.f32(TB) for _ in range(4)]; zk=[A.f32(TB) for _ in range(4)]; zv=[A.f32(TB) for _ in range(4)]
                    def shift(c,out,okey):
                        lo=max(t0-1,0); hi=min(t0+TB+1,S)
                        if t0-1<0: mset(zraw[:,0:1],0.0,[('zraw',)])
                        if t0+TB+1>S: mset(zraw[:,TB+1:TB+2],0.0,[('zraw',)])
                        dma(zraw[:,lo-(t0-1):hi-(t0-1)],ZR[c][:,lo:hi],[('ZR',)],[('zraw',)])
                        tt('pool',t1,zraw[:,0:TB],zraw[:,2:TB+2],ALU.add,[('zraw',)],[('t1',)])
                        act(t2,zraw[:,1:TB+1],AF.Copy,[('zraw',),('omm',)],[('t2',)],scale=omm[:,c:c+1])
                        stt('dve',out,t1,hmu[:,c:c+1],t2,ALU.mult,ALU.add,[('t1',),('t2',),('hmu',)],[okey])
                    for cc in range(4):
                        shift(cc,zr[cc],('zr',cc)); shift(4+cc,zk[cc],('zk',cc)); shift(8+cc,zv[cc],('zv',cc))
                    zw=A.f32(TB); za=A.f32(TB); zg=A.f32(TB)
                    shift(12,zw,('zw',)); shift(13,za,('za',))
                    act(th,zw,AF.Tanh,[('zw',)],[('th',)])
                    ev_copy(al,za,[('za',)],[('al',)],'act')
                    if d==0:
                        shift(14,zg,('zg',))
                        act(gs,zg,AF.Sigmoid,[('zg',)],[('gs',)])
                    btop=A.top
                    AR=[A.bf(NTB*2*128).rearrange("p (n a t) -> p n a t",n=NTB,a=2) for _ in range(4)]
                    BTt=[A.bf(TB) for _ in range(4)]; KTt=[A.bf(TB) for _ in range(4)]
                    BH=[A.bf(TB).rearrange("p (n c) -> p n c",c=128) for _ in range(4)]
                    KH=[A.bf(TB).rearrange("p (n c) -> p n c",c=128) for _ in range(4)]
                    VT=[A.bf(TB).rearrange("p (n c) -> p n c",c=128) for _ in range(4)]
                    gL=[A.f32(NCK) for _ in range(4)]
                    YT=[A.f32(TB) for _ in range(4)]
                    if d==0:
                        bonus=[A.f32(TB) for _ in range(4)]; gg=[A.f32(TB) for _ in range(4)]
                    AM=[[[A.bf(512) for _ in range(2)] for _ in range(NTB)] for _ in range(4)]
                    TT=[[[A.bf(128) for _ in range(2)] for _ in range(NTB)] for _ in range(4)]
                    ppb=[[A.bf(256),A.bf(256)] for _ in range(2*NTB)]; ttb=[[A.bf(128),A.bf(128)] for _ in range(2*NTB)]
                    Xsb=[A.bf(128) for _ in range(4)]; Usb=[A.bf(128) for _ in range(4)]
                    pyb=A.f32(TB); pyc=A.f32(TB); psq=A.f32(TB); pob=A.bf(TB)
                    etop=A.top
                    for cc in range(4):
                        A.top=etop
                        lw=A.f32(TB); alp=A.f32(TB); kk=A.f32(TB); tA=A.f32(TB); tB=A.f32(TB); kd=A.f32(TB); bb=A.f32(TB)
                        cumf=A.f32(TB); cbx=A.f32(TB); cumb=A.f32(TB); cfx=A.f32(TB)
                        e1=A.f32(TB); e2=A.f32(TB); e3=A.f32(TB); e4=A.f32(TB)
                        bhf=A.bf(TB); khf=A.bf(TB); vbf=A.bf(TB)
                        K=lambda n:('r1',n)
                        dr=slice(d*64,(d+1)*64)
                        pi=mmrr.next()
                        mm(PB[pi][:,0:TB],w2s[dr,cc*128:(cc+1)*128],th[dr,:],True,True,[('w2s',),('th',)],[pbk(pi)])
                        act(lw,PB[pi][:,0:TB],AF.Sigmoid,[pbk(pi),('pcol',)],[K('lw')],bias=w0c(d,cc))
                        ts('dve',lw,lw,NEG,0.0,ALU.mult,ALU.add,[K('lw')],[K('lw')])
                        pi=mmrr.next()
                        mm(PB[pi][:,0:TB],a2s[dr,cc*128:(cc+1)*128],al[dr,:],True,True,[('a2s',),('al',)],[pbk(pi)])
                        act(alp,PB[pi][:,0:TB],AF.Sigmoid,[pbk(pi),('pcol',)],[K('alp')],bias=a0c(d,cc))
                        act(tA,zk[cc],AF.Copy,[('zk',cc),('pcol',)],[K('tA')],scale=pcol[:,35+cc:36+cc])
                        tt('pool',tB,tA,tA,ALU.mult,[K('tA')],[K('tB')])
                        pi=mmrr.next()
                        mm(PB[pi][:,0:TB],bonesf,tB,True,True,[('c','bones'),K('tB')],[pbk(pi)])
                        act(tB,PB[pi][:,0:TB],AF.Sqrt,[pbk(pi)],[K('tB')])
                        ts('dve',tB,tB,1e-12,0.0,ALU.max,ALU.add,[K('tB')],[K('tB')])
                        P.op('dve',lambda e,tB=tB:e.reciprocal(out=tB,in_=tB),r=[K('tB')],w=[K('tB')])
                        tt('pool',kk,tA,tB,ALU.mult,[K('tA'),K('tB')],[K('kk')])
                        ts('dve',tA,alp,pcol[:,39+cc:40+cc],omka[:,cc:cc+1],ALU.mult,ALU.add,[K('alp'),('pcol',),('omka',)],[K('tA')])
                        tt('pool',kd,zk[cc],tA,ALU.mult,[('zk',cc),K('tA')],[K('kd')])
                        tt('pool',bb,kk,alp,ALU.mult,[K('kk'),K('alp')],[K('bb')])
                        P.op('dve',lambda e,cumf=cumf,lw=lw:e.tensor_tensor_scan(out=cumf,data0=restart[:,0:TB],data1=lw,initial=0.0,op0=ALU.mult,op1=ALU.add),r=[('c','restart'),K('lw')],w=[K('cumf')])
                        c3=lambda t:t.rearrange("p (c l) -> p c l",l=64)
                        tt('dve',c3(cbx),c3(cumf)[:,:,63:64].to_broadcast([128,NCK,64]),c3(cumf),ALU.subtract,[K('cumf')],[K('cbx')])
                        tt('pool',cumb,cbx,lw,ALU.add,[K('cbx'),K('lw')],[K('cumb')])
                        tt('pool',cfx,cumf,lw,ALU.subtract,[K('cumf'),K('lw')],[K('cfx')])
                        act(gL[cc],c3(cumf)[:,:,63],AF.Exp,[K('cumf')],[('gL',cc)])
                        ci,cx,co=(cumf,cfx,cbx) if d==0 else (cumb,cbx,cfx)
                        kci,kcx,kco=(K('cumf'),K('cfx'),K('cbx')) if d==0 else (K('cumb'),K('cbx'),K('cfx'))
                        act(e1,ci,AF.Exp,[kci],[K('e1')])
                        act(e2,ci,AF.Exp,[kci],[K('e2')],scale=-1.0)
                        act(e3,cx,AF.Exp,[kcx],[K('e3')])
                        act(e4,co,AF.Exp,[kco],[K('e4')])
                        v4=lambda t:t.rearrange("p (n t) -> p n t",t=128)
                        stt('dve',AR[cc][:,:,0,:],v4(kk),-1.0,v4(e3),ALU.mult,ALU.mult,[K('kk'),K('e3')],[('AR',cc)])
                        tt('pool',AR[cc][:,:,1,:],v4(zr[cc]),v4(e1),ALU.mult,[('zr',cc),K('e1')],[('AR',cc)])
                        tt('dve',BTt[cc],bb,e2,ALU.mult,[K('bb'),K('e2')],[('BTt',cc)])
                        tt('pool',KTt[cc],kd,e2,ALU.mult,[K('kd'),K('e2')],[('KTt',cc)])
                        tt('dve',bhf,bb,e4,ALU.mult,[K('bb'),K('e4')],[K('bhf')])
                        tt('pool',khf,kd,e4,ALU.mult,[K('kd'),K('e4')],[K('khf')])
                        ev_copy(vbf,zv[cc],[('zv',cc)],[K('vbf')],'act')
                        for src,skey,dstt,dkey in ((bhf,K('bhf'),BH[cc],('BH',cc)),(khf,K('khf'),KH[cc],('KH',cc)),(vbf,K('vbf'),VT[cc],('VT',cc))):
                            pi=tprr.next()
                            for ti in range(NTB):
                                tr(TPB(pi)[:,ti*128:(ti+1)*128],src[:,ti*128:(ti+1)*128],identb,[skey,('c','identb')],[pbk(pi)])
                            ev_copy(dstt,TPB(pi)[:,0:TB].rearrange("p (n c) -> p n c",c=128),[pbk(pi)],[dkey])
                        if d==0:
                            orr=slice(64,128)
                            pi=mmrr.next()
                            mm(PB[pi][:,0:TB],a2s[orr,cc*128:(cc+1)*128],al[orr,:],True,True,[('a2s',),('al',)],[pbk(pi)])
                            act(tA,PB[pi][:,0:TB],AF.Sigmoid,[pbk(pi),('pcol',)],[K('tA')],bias=a0c(1,cc))
                            ts('dve',tA,tA,pcol[:,39+cc:40+cc],omka[:,cc:cc+1],ALU.mult,ALU.add,[K('tA'),('pcol',),('omka',)],[K('tA')])
                            tt('pool',tA,zk[cc],tA,ALU.mult,[('zk',cc),K('tA')],[K('tA')])
                            tt('pool',tA,tA,kd,ALU.add,[K('tA'),K('kd')],[K('tA')])
                            stt('dve',bhf,zr[cc],rkh[:,cc:cc+1],tA,ALU.mult,ALU.mult,[('zr',cc),('rkh',),K('tA'),K('bhf')],[K('bhf')])
                            pi=mmrr.next()
                            mm(PB[pi][:,0:TB],bonesb,bhf,True,True,[('c','bonesb'),K('bhf')],[pbk(pi)])
                            tt('dve',bonus[cc],PB[pi][:,0:TB],zv[cc],ALU.mult,[pbk(pi),('zv',cc)],[('bonus',cc)])
                            pi=mmrr.next()
                            mm(PB[pi][:,0:TB],g2s[:,cc*128:(cc+1)*128],gs,True,True,[('g2s',),('gs',)],[pbk(pi)])
                            ev_copy(gg[cc],PB[pi][:,0:TB],[pbk(pi)],[('gg',cc)],'act')
                        insts=[(ti,hh) for ti in range(NTB) for hh in range(2)]
                        cur={}
                        for ii,(ti,hh) in enumerate(insts):
                            pr=slice(hh*64,(hh+1)*64)
                            am=AM[cc][ti][hh]; kam=('AM',cc,ti,hh)
                            pi=mmrr.next()
                            mm(PB[pi][:,0:256],BTt[cc][pr,ti*128:(ti+1)*128],AR[cc][pr,ti,:,:],True,True,[('BTt',cc),('AR',cc)],[pbk(pi)])
                            mm(PB[pi][:,256:512],KTt[cc][pr,ti*128:(ti+1)*128],AR[cc][pr,ti,:,:],True,True,[('KTt',cc),('AR',cc)],[pbk(pi)])
                            tt('dve',am,PB[pi][:,:],maskA,ALU.mult,[pbk(pi),('c','maskf'),('c','maskb')],[kam])
                            pi=mmrr.next()
                            mm(PB[pi][:,0:128],AR[cc][pr,ti,0,:],BTt[cc][pr,ti*128:(ti+1)*128],True,True,[('BTt',cc),('AR',cc)],[pbk(pi)])
                            tt('dve',ppb[ii][0][:,0:128],PB[pi][:,0:128],maskN,ALU.mult,[pbk(pi),('c','maskf'),('c','maskb')],[('pp',ii,0)])
                            ev_copy(ppb[ii][0][:,128:256],am[:,0:128],[kam],[('pp',ii,0)],'pool')
                            tt('pool',ttb[ii][0],identb,am[:,0:128],ALU.add,[('c','identb'),kam],[('ttx',ii,0)])
                            cur[ii]=0
                        for k in range(1,6):
                            for ii,(ti,hh) in enumerate(insts):
                                c_=cur[ii]; src=ppb[ii][c_]; ks=('pp',ii,c_)
                                mm(PB[ii][:,0:128],src[:,128:256],src[:,0:128],True,True,[ks],[pbk(ii)])
                                if k<5:
                                    mm(PB[ii][:,128:256],src[:,0:128],src[:,128:256],True,True,[ks],[pbk(ii)])
                            for ii,(ti,hh) in enumerate(insts):
                                c_=cur[ii]; dst=ppb[ii][1-c_]; kd_=('pp',ii,1-c_)
                                w_=256 if k<5 else 128
                                ev_copy(dst[:,0:w_],PB[ii][:,0:w_],[pbk(ii)],[kd_],'act')
                            for ii,(ti,hh) in enumerate(insts):
                                c_=cur[ii]; dst=ppb[ii][1-c_]; kd_=('pp',ii,1-c_)
                                mm(PB[ii][:,256:384],dst[:,0:128],ttb[ii][c_],True,True,[kd_,('ttx',ii,c_)],[pbk(ii)])
                            for ii,(ti,hh) in enumerate(insts):
                                c_=cur[ii]
                                outt=TT[cc][ti][hh] if k==5 else ttb[ii][1-c_]
                                okey=('TT',cc,ti,hh) if k==5 else ('ttx',ii,1-c_)
                                tt('dve',outt,PB[ii][:,256:384],ttb[ii][c_],ALU.add,[pbk(ii),('ttx',ii,c_)],[okey])
                                cur[ii]=1-c_
                    chunks=range(NCK) if d==0 else range(NCK-1,-1,-1)
                    bX=(5,1); bU=(6,2); bY=(7,3); bS=(4,0)
                    for ch in chunks:
                        ti=ch//2; half=ch%2
                        tb=slice(half*64,(half+1)*64); cs=slice(half*64,(half+1)*64)
                        for cc in range(4):
                            b=bX[cc%2]
                            mm(PB[b][tb,0:128],AR[cc][:,ti,0,cs],Sbf[cc],True,False,[('AR',cc),('Sbf',cc)],[pbk(b)])
                            for hh in range(2):
                                hs=slice(hh*64,(hh+1)*64)
                                mm(PB[b][tb,hs],AM[cc][ti][hh][tb,256+half*64:256+half*64+64],VT[cc][tb,ti,hs],False,hh==1,[('AM',cc,ti,hh),('VT',cc)],[pbk(b)])
                            ev_copy(Xsb[cc][tb,:],PB[b][tb,0:128],[pbk(b)],[('Xsb',cc)],'act' if cc%2==0 else 'dve')
                        for cc in range(4):
                            b=bU[cc%2]
                            for hh in range(2):
                                hs=slice(hh*64,(hh+1)*64)
                                mm(PB[b][tb,hs],TT[cc][ti][hh][tb,cs],Xsb[cc][tb,hs],True,True,[('TT',cc,ti,hh),('Xsb',cc)],[pbk(b)])
                            ev_copy(Usb[cc][tb,:],PB[b][tb,0:128],[pbk(b)],[('Usb',cc)],'dve' if cc%2==0 else 'act')
                        for cc in range(4):
                            b=bY[cc%2]
                            mm(PB[b][:,0:64],Sbf[cc],AR[cc][:,ti,1,cs],True,False,[('Sbf',cc),('AR',cc)],[pbk(b)])
                            for hh in range(2):
                                hs=slice(hh*64,(hh+1)*64)
                                mm(PB[b][hs,0:64],Usb[cc][tb,hs],AM[cc][ti][hh][tb,128+half*64:128+half*64+64],False,False,[('Usb',cc),('AM',cc,ti,hh)],[pbk(b)])
                                mm(PB[b][hs,0:64],VT[cc][tb,ti,hs],AM[cc][ti][hh][tb,384+half*64:384+half*64+64],False,hh==1,[('VT',cc),('AM',cc,ti,hh)],[pbk(b)])
                            ev_copy(YT[cc][:,ch*64:(ch+1)*64],PB[b][:,0:64],[pbk(b)],[('YT',cc)],'act')
                            b=bS[cc%2]
                            mm(PB[b][:,0:128],BH[cc][tb,ti,:],Usb[cc][tb,:],True,False,[('BH',cc),('Usb',cc)],[pbk(b)])
                            mm(PB[b][:,0:128],KH[cc][tb,ti,:],VT[cc][tb,ti,:],False,True,[('KH',cc),('VT',cc)],[pbk(b)])
                            for hh in range(2):
                                hs=slice(hh*64,(hh+1)*64)
                                stt('dve',S32[cc][hs,hs],S32[cc][hs,hs],gL[cc][hs,ch:ch+1],PB[b][hs,hs],ALU.mult,ALU.add,[('S32',cc),('gL',cc),pbk(b)],[('S32',cc)])
                            ev_copy(Sbf[cc],S32[cc],[('S32',cc)],[('Sbf',cc)],'act')
                    for cc in range(4):
                        if d==1:
                            dma(YB[cc][:,t0:t0+TB],YT[cc],[('YT',cc)],[('YB',)])
                        else:
                            yb,yc,sq,ob=pyb,pyc,psq,pob
                            K=lambda n:('po',n)
                            dma(yb,YB[cc][:,t0:t0+TB],[('YB',)],[K('yb')])
                            tt('pool',yb,yb,YT[cc],ALU.add,[K('yb'),('YT',cc)],[K('yb')])
                            pi=mmrr.next()
                            mm(PB[pi][:,0:TB],bonesf,yb,True,True,[('c','bones'),K('yb')],[pbk(pi)])
                            stt('dve',yc,PB[pi][:,0:TB],-1.0/64,yb,ALU.mult,ALU.add,[pbk(pi),K('yb')],[K('yc')])
                            tt('pool',sq,yc,yc,ALU.mult,[K('yc')],[K('sq')])
                            pi=mmrr.next()
                            mm(PB[pi][:,0:TB],bonesf,sq,True,True,[('c','bones'),K('sq')],[pbk(pi)])
                            act(sq,PB[pi][:,0:TB],AF.Sqrt,[pbk(pi)],[K('sq')],bias=GN_EPS,scale=1.0/64)
                            P.op('dve',lambda e,sq=sq:e.reciprocal(out=sq,in_=sq),r=[K('sq')],w=[K('sq')])
                            tt('pool',yc,yc,sq,ALU.mult,[K('yc'),K('sq')],[K('yc')])
                            ts('dve',yc,yc,pcol[:,47+cc:48+cc],pcol[:,51+cc:52+cc],ALU.mult,ALU.add,[K('yc'),('pcol',)],[K('yc')])
                            tt('pool',yc,yc,bonus[cc],ALU.add,[K('yc'),('bonus',cc)],[K('yc')])
                            tt('dve',ob,yc,gg[cc],ALU.mult,[K('yc'),('gg',cc)],[K('ob')])
                            dma(BOT[:,cc,t0:t0+TB],ob,[K('ob')],[('BOT',)])
                    P.barrier()

            TC=512
            for cg in (range(S//TC) if 'C' in PHASES else []):
                A.top=ltop; t0=cg*TC
                X1=A.f32(4*1024).rearrange("p (n d) -> p n d",n=4)
                X1T=A.bf(8*TC).rearrange("p (k t) -> p k t",k=8)
                c1top=A.top
                MT=A.bf(8*TC).rearrange("p (k t) -> p k t",k=8)
                XO=[A.bf(4*TC).rearrange("p (k t) -> p k t",k=4) for _ in range(3)]
                wbrs=A.bf(12288).rearrange("p (b k d) -> p b k d",b=3,k=4)
                wos=A.bf(8192).rearrange("p (k d) -> p k d",k=8)
                gt=[A.bf(TC) for _ in range(3)]
                acc=A.f32(TC); tmp=A.f32(TC)
                xs=[A.f32(1024) for _ in range(2)]; ypre=A.f32(1024); xbb=A.bf(1024)
                st1=A.f32(4); junk=A.f32(1024)
                for b,src in enumerate((AOT,BOT,COT)):
                    dma(XO[b],src[:,:,t0:t0+TC],[('AOT',),('BOT',),('COT',)],[('XO',b)])
                dma(wbrs.rearrange("p b k d -> p (b k d)"),wb['wbr'][l],[('wb',)],[('wbrs',)])
                dma(wos.rearrange("p k d -> p (k d)"),wb['wo'][l],[('wb',)],[('wos',)])
                for m in range(8):
                    for b in range(3):
                        dma(gt[b],GT[b*8+m][:,t0:t0+TC],[('GT',)],[('gt',b)])
                        pi=mmrr.next()
                        for k in range(4):
                            mm(PB[pi][:,:],wbrs[:,b,k,m*128:(m+1)*128],XO[b][:,k,:],k==0,k==3,[('wbrs',),('XO',b)],[pbk(pi)])
                        if b==0:
                            tt('dve',acc,PB[pi][:,:],gt[b],ALU.mult,[pbk(pi),('gt',b)],[('acc',)])
                        else:
                            tt('dve',tmp,PB[pi][:,:],gt[b],ALU.mult,[pbk(pi),('gt',b)],[('tmp',)])
                            if b==1: tt('pool',acc,acc,tmp,ALU.add,[('acc',),('tmp',)],[('acc',)])
                            else: tt('pool',MT[:,m,:],acc,tmp,ALU.add,[('acc',),('tmp',)],[('MT',)])
                def layer_norm(gi,out_ap,okey):
                    red(st1[:,0:1],ypre,[('ypre',)],[('st1',)])
                    ts('dve',st1[:,1:2],st1[:,0:1],-1.0/1024,0.0,ALU.mult,ALU.add,[('st1',)],[('st1',)])
                    ts('dve',ypre,ypre,st1[:,1:2],0.0,ALU.add,ALU.add,[('ypre',),('st1',)],[('ypre',)])
                    tt('pool',junk,ypre,ypre,ALU.mult,[('ypre',)],[('junk',)])
                    red(st1[:,2:3],junk,[('junk',)],[('st1',)])
                    act(st1[:,3:4],st1[:,2:3],AF.Sqrt,[('st1',)],[('st1',)],bias=LN_EPS,scale=1.0/1024)
                    recip(st1[:,3:4],st1[:,3:4],[('st1',)],[('st1',)])
                    stt('dve',ypre,ypre,st1[:,3:4],lnrow[gi],ALU.mult,ALU.mult,[('ypre',),('st1',),('lnrow',gi)],[('ypre',)])
                    tt('pool',out_ap,ypre,lnrow[gi+1],ALU.add,[('ypre',),('lnrow',gi+1)],[okey])
                for ti in range(TC//128):
                    i=ti%2
                    dma(xs[i],xsrc[t0+ti*128:t0+(ti+1)*128,:],[('xsrc',)],[('xs',i)])
                    for half in range(2):
                        pi=mmrr.next()
                        for k in range(8):
                            mm(PB[pi][:,:],MT[:,k,ti*128:(ti+1)*128],wos[:,k,half*512:(half+1)*512],k==0,k==7,[('MT',),('wos',)],[pbk(pi)])
                        stt('dve',ypre[:,half*512:(half+1)*512],xs[i][:,half*512:(half+1)*512],ALPHA,PB[pi][:,:],ALU.mult,ALU.add,[('xs',i),pbk(pi)],[('ypre',)])
                    layer_norm(0,X1[:,ti,:],('X1',))
                    ev_copy(xbb,X1[:,ti,:],[('X1',)],[('xbb',)],'act')
                    pi=tprr.next()
                    for k in range(8):
                        tr(TPB(pi)[:,k*128:(k+1)*128],xbb[:,k*128:(k+1)*128],identb,[('xbb',),('c','identb')],[pbk(pi)])
                    ev_copy(X1T[:,:,ti*128:(ti+1)*128],TPB(pi).rearrange("p (k t) -> p k t",k=8),[pbk(pi)],[('X1T',)],'dve')
                P.barrier()
                A.top=c1top
                H=A.bf(32*TC).rearrange("p (k t) -> p k t",k=32)
                w2f=A.bf(32768).rearrange("p (k d) -> p k d",k=32)
                w1c=[A.bf(1024) for _ in range(3)]
                rl=[A.f32(TC) for _ in range(2)]
                ypre=A.f32(1024); st1=A.f32(4); junk=A.f32(1024); ot=[A.f32(1024) for _ in range(2)]
                dma(w2f.rearrange("p k d -> p (k d)"),wb['ff2'][l],[('wb',)],[('w2f',)])
                for m in range(32):
                    i=m%3; ri=m%2
                    dma(w1c[i],wb['ff1'][l,m],[('wb',)],[('w1c',i)])
                    pi=mmrr.next()
                    for k in range(8):
                        mm(PB[pi][:,:],w1c[i][:,k*128:(k+1)*128],X1T[:,k,:],k==0,k==7,[('w1c',i),('X1T',)],[pbk(pi)])
                    act(rl[ri],PB[pi][:,:],AF.Relu,[pbk(pi)],[('rl',ri)])
                    tt('pool' if m%2 else 'dve',H[:,m,:],rl[ri],rl[ri],ALU.mult,[('rl',ri)],[('H',)])
                for ti in range(TC//128):
                    for half in range(2):
                        pi=mmrr.next()
                        for k in range(32):
                            mm(PB[pi][:,:],H[:,k,ti*128:(ti+1)*128],w2f[:,k,half*512:(half+1)*512],k==0,k==31,[('H',),('w2f',)],[pbk(pi)])
                        stt('dve',ypre[:,half*512:(half+1)*512],X1[:,ti,half*512:(half+1)*512],ALPHA,PB[pi][:,:],ALU.mult,ALU.add,[('X1',),pbk(pi)],[('ypre',)])
                    layer_norm(2,ot[ti%2],('ot',ti%2))
                    dma(xdst[t0+ti*128:t0+(ti+1)*128,:],ot[ti%2],[('ot',ti%2)],[('xdst',)])
                P.barrier()
    P.run()
    for cm in reversed(pbcm): cm.__exit__(None,None,None)
    return nc

N_CORES=8
_CACHE={}
def _run(job_S, depth, per_core_x, inp, debug=False):
    key=(tuple(job_S),depth)
    nc=build(job_S,depth,debug)
    consts=host_consts(np.asarray(inp['rel_bias'],np.float32))
    wts=host_weights({k:np.asarray(v,np.float32) for k,v in inp.items() if k not in ('x_prompt','x_sample')},depth)
    in_maps=[]
    for c in range(len(per_core_x)):
        m={}
        m.update(consts); m.update(wts)
        for j,x in enumerate(per_core_x[c]): m[f"x{j}"]=np.ascontiguousarray(x,dtype=np.float32)
        in_maps.append(m)
    res=run_bass_kernel_spmd(nc,in_maps,core_ids=list(range(len(per_core_x))))
    return res.results

def kernel(**inputs):
    xp=np.asarray(inputs['x_prompt'],np.float32); xs=np.asarray(inputs['x_sample'],np.float32)
    job_S=[2048,2048,16384]
    per_core=[]
    zero=np.zeros((16384,D),np.float32)
    for c in range(N_CORES):
        per_core.append([xp[2*c],xp[2*c+1],xs[c] if c<2 else zero])
    res=_run(job_S,4,per_core,inputs)
    yp=np.stack([res[c][f"y{j}"] for c in range(N_CORES) for j in range(2)],0).astype(np.float32)
    ys=np.stack([res[c]["y2"] for c in range(2)],0).astype(np.float32)
    return (yp,ys)
```

```python
import numpy as np
import contextlib, math, os
PHASES=os.environ.get('KPHASES','APTRC')
import concourse.bass as bass
import concourse.mybir as mybir
from concourse.bass_utils import run_bass_kernel_spmd
F32=mybir.dt.float32; BF16=mybir.dt.bfloat16
AF=mybir.ActivationFunctionType; ALU=mybir.AluOpType; AX=mybir.AxisListType
ENG=['pe','act','dve','pool','sp']
NROT=8
SAME_ENG_SYNC=True
D=1024; C=512; NIN=6272; DFF=4096
POOL_WINDOWS=(2,4,8,16)
ALPHA=(2*4)**0.25
LN_EPS=1e-5; GN_EPS=64e-5
NCOL=64

class Prog:
    def __init__(self, nc):
        self.nc=nc
        self.ops={e:[] for e in ENG}
        self.segs=[]
        self.lastw={}; self.readers={}
        self.dma_cnt={e:0 for e in ENG}
        self.last_rec={}
    def semid(self, rec):
        return ('d',rec['eng'],rec['dk']%NROT) if rec['dma'] else ('c',rec['eng'])
    def op(self, eng, fn, r=(), w=(), dma=False):
        rec=dict(eng=eng, fn=fn, waits=[], signal=bool(dma), dma=bool(dma))
        if dma:
            rec['dk']=self.dma_cnt[eng]; self.dma_cnt[eng]+=1
        def need(p):
            if p is None or p is rec: return
            if (not p['dma']) and (not dma) and p['eng']==eng and (eng=='pe' or not SAME_ENG_SYNC): return
            p['signal']=True; rec['waits'].append(p)
        for key in r: need(self.lastw.get(key))
        for key in w:
            need(self.lastw.get(key))
            for p in self.readers.get(key,{}).values(): need(p)
        sid=self.semid(rec)
        for key in r: self.readers.setdefault(key,{})[sid]=rec
        for key in w:
            self.lastw[key]=rec; self.readers[key]={}
        self.ops[eng].append(rec); self.last_rec[sid]=rec
        return rec
    def _setup(self):
        if getattr(self,'sems',None) is not None: return
        nc=self.nc
        self.stack=contextlib.ExitStack()
        self.sems={}
        for e in ENG:
            self.sems[('c',e)]=self.stack.enter_context(nc.semaphore(f"c_{e}"))
        for i in range(NROT):
            self.sems[('d','sp',i)]=self.stack.enter_context(nc.semaphore(f"d_sp_{i}"))
        self.emitted={e:0 for e in ENG}; self.ccnt={e:0 for e in ENG}; self.waited={e:{} for e in ENG}
    def barrier(self):
        self._setup()
        lasts=list(self.last_rec.values())
        for e in ENG:
            rec=dict(eng=e, fn=None, waits=[], signal=False, dma=False)
            for p in lasts:
                if p['eng']==e and not p['dma'] and e=='pe': continue
                p['signal']=True; rec['waits'].append(p)
            self.ops[e].append(rec)
        self.lastw={}; self.readers={}
        self.emit_segment()
    def emit_segment(self):
        nc=self.nc; sems=self.sems
        for e in ENG:
            for rec in self.ops[e][self.emitted[e]:]:
                if rec['dma']:
                    k=rec['dk']; rec['sv']=(('d',e,k%NROT),16*(k//NROT+1))
                elif rec['signal']:
                    self.ccnt[e]+=1; rec['sv']=(('c',e),self.ccnt[e])
        nosync=bool(os.environ.get('KNOSYNC'))
        with nc.Block() as block:
            def replay(ename):
                def f(e):
                    wd=self.waited[ename]
                    for rec in self.ops[ename][self.emitted[ename]:]:
                        ws={}
                        for p in rec['waits']:
                            s,v=p['sv']
                            if v>ws.get(s,0): ws[s]=v
                        for s,v in ws.items():
                            if wd.get(s,0)>=v: continue
                            if not nosync: e.wait_ge(sems[s],v)
                            wd[s]=v
                        if rec['fn'] is None: continue
                        ins=rec['fn'](e)
                        if rec['signal'] and not nosync:
                            ins.then_inc(sems[rec['sv'][0]],16 if rec['dma'] else 1)
                        rec['fn']=None
                return f
            block.tensor(replay('pe')); block.scalar(replay('act')); block.vector(replay('dve'))
            block.gpsimd(replay('pool')); block.sync(replay('sp'))
        for e in ENG: self.emitted[e]=len(self.ops[e])
    def run(self):
        self._setup()
        lasts=list(self.last_rec.values())
        rec=dict(eng='sp', fn=None, waits=[], signal=False, dma=False)
        for p in lasts:
            p['signal']=True; rec['waits'].append(p)
        self.ops['sp'].append(rec)
        self.emit_segment()
        self.stack.close()

class Arena:
    def __init__(self, nc, words):
        self.nc=nc
        self.t=nc.alloc_sbuf_tensor("arena",[128,words],F32); self.words=words; self.top=0; self.n=0
        self.base=nc.lookup_mloc(self.t).addr
        self.gen=0; self.cache={}
    def _get(self, off_words, n, dt):
        key=(self.gen,off_words,n,str(dt))
        t=self.cache.get(key)
        if t is None:
            t=self.nc.alloc_sbuf_tensor_at(f"a{len(self.cache)}",[128,n],dt,offset=self.base+4*off_words)
            self.cache[key]=t
        return t
    def f32(self, n0):
        n=(n0+7)//8*8
        assert self.top+n<=self.words, (self.top,n,self.words)
        t=self._get(self.top,n,F32); self.top+=n
        return t[:,0:n0]
    def bf(self, n):
        w=(n+15)//16*8
        assert self.top+w<=self.words, (self.top,w,self.words)
        t=self._get(self.top,2*w,BF16); self.top+=w
        return t[:,0:n]

def _t5_bucket(rel):
    half=16; max_exact=8
    n=np.abs(rel)
    large=max_exact+(np.log(np.maximum(n,1)/max_exact)/math.log(128/max_exact)*(half-max_exact)).astype(np.int32)
    large=np.minimum(large,half-1)
    return (rel>0).astype(np.int32)*half+np.where(n<max_exact,n,large)

def host_consts(rel_bias):
    c={}
    c['ident']=np.eye(128,dtype=np.float32)
    p=np.arange(128)[:,None]; f=np.arange(128)[None,:]
    same=(p//64)==(f//64)
    MUs=((p<f)&same).astype(np.float32); MUi=((p<=f)&same).astype(np.float32)
    MLs=((p>f)&same).astype(np.float32); MLi=((p>=f)&same).astype(np.float32)
    c['maskf']=np.concatenate([MUs,MUi,MUs,MUi],1); c['maskb']=np.concatenate([MLs,MLi,MLs,MLi],1)
    rs=np.ones((128,512),np.float32); rs[:,::64]=0.0; c['restart']=rs
    c['bones']=same.astype(np.float32)
    k=np.arange(128)[:,None]; q=np.arange(128)[None,:]
    ab=np.zeros((128,3,2,4,128),np.float32)
    for kbi,kb in enumerate((-1,0,1)):
        rel=kb*128+k-q
        bk=_t5_bucket(rel); ok=np.abs(rel)<=128
        for kvh in range(2):
            for g in range(4):
                ab[:,kbi,kvh,g,:]=np.where(ok,rel_bias[bk,kvh*4+g],np.float32(-30000.0))
    c['attb']=ab.reshape(128,3072)
    op=np.zeros((128,2,128),np.float32); op[:,0,0:64]=1; op[:,1,64:128]=1; c['onespad']=op.reshape(128,256)
    pe=np.ones((128,4,2,8),np.float32)
    for g,w in enumerate(POOL_WINDOWS):
        h=w//2
        for t in range(h): pe[:,g,0,t]=w/(t+h)
        for i in range(h-1): pe[:,g,1,7-i]=w/(i+1+h)
    c['pedge']=pe.reshape(128,64)
    return c

def host_weights(inp, depth):
    L=depth; w={}
    w_in=inp['w_in'][:L]
    cols=list(range(0,C+1920))
    oq=C+1920
    for j in range(4):
        cols+=list(range(oq+j*64,oq+j*64+64))+list(range(oq+(4+j)*64,oq+(4+j)*64+64))
    cols+=list(range(oq+512,NIN))
    wp=w_in[:,:,cols]
    w['win']=np.ascontiguousarray(wp.reshape(L,8,128,49,128).transpose(0,3,2,1,4).reshape(L,49,128,1024))
    w['poolw']=np.ascontiguousarray(inp['pool_w'][:L].transpose(0,2,1,3).reshape(L,128,512))
    w['w2']=np.ascontiguousarray(inp['rwkv_w2'][:L].reshape(L,128,512))
    w['a2']=np.ascontiguousarray(inp['rwkv_a2'][:L].reshape(L,128,512))
    w['g2']=np.ascontiguousarray(inp['rwkv_g2'][:L])
    wb=inp['w_branch'][:L].copy()
    perm=[kvh*256+g*64+d for g in range(4) for kvh in range(2) for d in range(64)]
    wb[:,2]=wb[:,2][:,perm]
    w['wbr']=np.ascontiguousarray(wb.reshape(L,3,4,128,1024).transpose(0,3,1,2,4).reshape(L,128,12*1024))
    w['wo']=np.ascontiguousarray(inp['w_o'][:L].reshape(L,8,128,1024).transpose(0,2,1,3).reshape(L,128,8*1024))
    w['ff1']=np.ascontiguousarray(inp['w_ff1'][:L].reshape(L,8,128,32,128).transpose(0,3,2,1,4).reshape(L,32,128,1024))
    w['ff2']=np.ascontiguousarray(inp['w_ff2'][:L].reshape(L,32,128,1024).transpose(0,2,1,3).reshape(L,128,32*1024))
    pc=np.zeros((L,128,NCOL),np.float32)
    pc[:,:,0:15]=inp['rwkv_mu'][:L].reshape(L,15,128).transpose(0,2,1)
    pc[:,:,15:19]=inp['pool_scale'][:L].reshape(L,4,128).transpose(0,2,1)
    pc[:,:,19:27]=inp['rwkv_w0'][:L].reshape(L,8,128).transpose(0,2,1)
    pc[:,:,27:35]=inp['rwkv_a0'][:L].reshape(L,8,128).transpose(0,2,1)
    pc[:,:,35:39]=inp['rwkv_k_k'][:L].reshape(L,4,128).transpose(0,2,1)
    pc[:,:,39:43]=inp['rwkv_k_a'][:L].reshape(L,4,128).transpose(0,2,1)
    pc[:,:,43:47]=inp['rwkv_r_k'][:L].reshape(L,4,128).transpose(0,2,1)
    pc[:,:,47:51]=inp['rwkv_gn_g'][:L].reshape(L,4,128).transpose(0,2,1)
    pc[:,:,51:55]=inp['rwkv_gn_b'][:L].reshape(L,4,128).transpose(0,2,1)
    sk=inp['attn_sink'][:L].reshape(L,2,1,4)
    pc[:,:,55:59]=np.broadcast_to(sk,(L,2,64,4)).reshape(L,128,4)
    w['pcol']=pc
    w['lnrow']=np.ascontiguousarray(np.broadcast_to(
        np.stack([inp['ln1_g'][:L],inp['ln1_b'][:L],inp['ln2_g'][:L],inp['ln2_b'][:L]],1)[:,:,None,:],(L,4,128,1024)))
    return w

WSHAPES=lambda L:{'win':[L,49,128,1024],'poolw':[L,128,512],'w2':[L,128,512],'a2':[L,128,512],'g2':[L,128,512],
        'wbr':[L,128,12288],'wo':[L,128,8192],'ff1':[L,32,128,1024],'ff2':[L,128,32768],'pcol':[L,128,NCOL],'lnrow':[L,4,128,1024]}
CSHAPES={'ident':[128,128],'maskf':[128,512],'maskb':[128,512],'restart':[128,512],'bones':[128,128],'attb':[128,3072],
         'onespad':[128,256],'pedge':[128,64]}
BFW=['win','poolw','w2','a2','g2','wbr','wo','ff1','ff2']

def build(job_S, depth, debug=False):
    IK='ExternalOutput' if debug else 'Internal'
    nc=bass.Bass("TRN2", target_bir_lowering=False)
    P=Prog(nc); L=depth
    SMAX=max(job_S)
    din={}
    for n,s in WSHAPES(L).items(): din[n]=nc.dram_tensor(n,s,F32,kind="ExternalInput").ap()
    for n,s in CSHAPES.items(): din[n]=nc.dram_tensor(n,s,F32,kind="ExternalInput").ap()
    xin=[nc.dram_tensor(f"x{j}",[S,D],F32,kind="ExternalInput").ap() for j,S in enumerate(job_S)]
    yout=[nc.dram_tensor(f"y{j}",[S,D],F32,kind="ExternalOutput").ap() for j,S in enumerate(job_S)]
    wb={n:nc.dram_tensor(n+"_b",WSHAPES(L)[n],BF16,kind="Internal").ap() for n in BFW}
    ZP=nc.dram_tensor("ZP",[4,128,SMAX],F32,kind=IK).ap()
    ZR=nc.dram_tensor("ZR",[15,128,SMAX],F32,kind=IK).ap()
    QT=nc.dram_tensor("QT",[128,4,SMAX],BF16,kind=IK).ap()
    KT=nc.dram_tensor("KT",[128,SMAX],BF16,kind=IK).ap()
    VTOK=nc.dram_tensor("VTOK",[SMAX,128],BF16,kind=IK).ap()
    GT=nc.dram_tensor("GT",[24,128,SMAX],BF16,kind=IK).ap()
    AOT=nc.dram_tensor("AOT",[128,4,SMAX],BF16,kind=IK).ap()
    BOT=nc.dram_tensor("BOT",[128,4,SMAX],BF16,kind=IK).ap()
    COT=nc.dram_tensor("COT",[128,4,SMAX],BF16,kind=IK).ap()
    YB=nc.dram_tensor("YB",[4,128,SMAX],F32,kind=IK).ap()
    XTOK=nc.dram_tensor("XTOK",[SMAX,D],F32,kind=IK).ap()
    A=Arena(nc,49152)
    PB=[None]*8; PBB=[None]*8; pbcm=[]
    pbk=lambda i:('pb',i)
    class RR:
        def __init__(s,ids): s.ids=ids; s.i=0
        def next(s):
            i=s.ids[s.i%len(s.ids)]; s.i+=1; return i
    mmrr=RR([0,1,2]); tprr=RR([3,4])
    def TPB(i): return PB[i][:, :].bitcast(BF16)
    def new_psum(tag):
        for cm in reversed(pbcm): cm.__exit__(None,None,None)
        pbcm.clear()
        for i in range(8):
            cm=nc.psum_tensor(f"pb{i}_{tag}",[128,512],F32); PB[i]=cm.__enter__(); pbcm.append(cm)
    new_psum('init')
    def dma(out,in_,r,w): P.op('sp',lambda e:e.dma_start(out=out,in_=in_),r=r,w=w,dma=True)
    ccnt=[0]
    def ev_copy(out,in_,r,w,eng=None):
        if eng is None:
            eng='act' if ccnt[0]%2==0 else 'dve'; ccnt[0]+=1
        if eng=='act': P.op('act',lambda e:e.activation(out=out,in_=in_,func=AF.Copy),r=r,w=w)
        elif eng=='dve': P.op('dve',lambda e:e.tensor_copy(out=out,in_=in_),r=r,w=w)
        else: P.op('pool',lambda e:e.tensor_copy(out=out,in_=in_),r=r,w=w)
    def tt(eng,out,a,b,op,r,w): P.op(eng,lambda e:e.tensor_tensor(out=out,in0=a,in1=b,op=op),r=r,w=w)
    def ts(eng,out,a,s1,s2,op0,op1,r,w): P.op(eng,lambda e:e.tensor_scalar(out=out,in0=a,scalar1=s1,scalar2=s2,op0=op0,op1=op1),r=r,w=w)
    def stt(eng,out,a,s,b,op0,op1,r,w): P.op(eng,lambda e:e.scalar_tensor_tensor(out=out,in0=a,scalar=s,in1=b,op0=op0,op1=op1),r=r,w=w)
    def act(out,in_,func,r,w,bias=None,scale=None):
        kw={}
        if bias is not None: kw['bias']=bias
        if scale is not None: kw['scale']=scale
        P.op('act',lambda e:e.activation(out=out,in_=in_,func=func,**kw),r=r,w=w)
    def red(out,in_,r,w): P.op('dve',lambda e:e.tensor_reduce(out=out,in_=in_,axis=AX.X,op=ALU.add),r=r,w=w)
    def recip(out,in_,r,w): P.op('dve',lambda e:e.reciprocal(out=out,in_=in_),r=r,w=w)
    def mset(ap,val,w): P.op('pool',lambda e:e.memset(ap,val),w=w)
    def mm(out,lhsT,rhs,start,stop,r,w): P.op('pe',lambda e:e.matmul(out,lhsT,rhs,start=start,stop=stop),r=r,w=w)
    def tr(out,in_,idn,r,w): P.op('pe',lambda e:e.transpose(out,in_,idn),r=r,w=w)

    identf=A.f32(128); maskf=A.f32(512); maskb=A.f32(512); restart=A.f32(512); bonesf=A.f32(128)
    attb=A.f32(3072); pedge=A.f32(64); tmpc=A.f32(256)
    identb=A.bf(128); bonesb=A.bf(128); onespad=A.bf(256)
    pcol=A.f32(NCOL); omm=A.f32(16); hmu=A.f32(16); omka=A.f32(4); rkh=A.f32(4); esink=A.f32(512); esk4=A.f32(4)
    lnrow=[A.f32(1024) for _ in range(4)]
    for n,t in [('ident',identf),('maskf',maskf),('maskb',maskb),('restart',restart),('bones',bonesf),('attb',attb),('pedge',pedge),('onespad',tmpc)]:
        dma(t,din[n][:,:],[],[('c',n)])
    ev_copy(identb,identf,[('c','ident')],[('c','identb')],'dve')
    ev_copy(bonesb,bonesf,[('c','bones')],[('c','bonesb')],'dve')
    ev_copy(onespad,tmpc,[('c','onespad')],[('c','onespadb')],'dve')
    P.barrier()
    base_top=A.top
    def cast_rows(src,dst,ncols):
        CH=4096
        for c0 in range(0,ncols,CH):
            cw=min(CH,ncols-c0)
            i=castrr.next()
            dma(cst[i][:,0:cw],src[:,c0:c0+cw],[],[('cst',i)])
            ev_copy(csb[i][:,0:cw],cst[i][:,0:cw],[('cst',i)],[('csb',i)])
            dma(dst[:,c0:c0+cw],csb[i][:,0:cw],[('csb',i)],[('wb',)])
    cst=[A.f32(4096) for _ in range(3)]; csb=[A.bf(4096) for _ in range(3)]; castrr=RR([0,1,2])
    for l in range(L):
        for n in BFW:
            s=WSHAPES(L)[n]
            if len(s)==4:
                for c in range(s[1]): cast_rows(din[n][l,c],wb[n][l,c],s[3])
            else:
                cast_rows(din[n][l],wb[n][l],s[2])
    P.barrier()

    for j,S in enumerate(job_S):
        NT=S//128
        for l in range(L):
            xsrc=xin[j] if l==0 else XTOK
            xdst=yout[j] if l==L-1 else XTOK
            A.top=base_top; A.gen+=1; new_psum(f'{j}_{l}')
            dma(pcol,din['pcol'][l],[],[('pcol',)])
            for i in range(4): dma(lnrow[i],din['lnrow'][l,i],[],[('lnrow',i)])
            ts('dve',omm[:,0:15],pcol[:,0:15],-1.0,1.0,ALU.mult,ALU.add,[('pcol',)],[('omm',)])
            ts('dve',hmu[:,0:15],pcol[:,0:15],0.5,0.0,ALU.mult,ALU.add,[('pcol',)],[('hmu',)])
            ts('dve',omka,pcol[:,39:43],-1.0,1.0,ALU.mult,ALU.add,[('pcol',)],[('omka',)])
            ts('dve',rkh,pcol[:,43:47],0.5,0.0,ALU.mult,ALU.add,[('pcol',)],[('rkh',)])
            act(esk4,pcol[:,55:59],AF.Exp,[('pcol',)],[('esk4',)])
            for g in range(4):
                ts('dve',esink[:,g*128:(g+1)*128],identf,0.0,esk4[:,g:g+1],ALU.mult,ALU.add,[('esk4',),('c','ident')],[('esink',)])
            P.barrier()
            ltop=A.top
            w0c=lambda d,cc: pcol[:,19+d*4+cc:20+d*4+cc]
            a0c=lambda d,cc: pcol[:,27+d*4+cc:28+d*4+cc]

            TA=min(S,2048)
            for sg in (range(S//TA) if 'A' in PHASES else []):
                A.top=ltop; t0=sg*TA
                XT=A.bf(8*TA).rearrange("p (k t) -> p k t",k=8)
                xs=[A.f32(1024) for _ in range(2)]; xb=[A.bf(1024) for _ in range(2)]
                wch=[A.bf(1024) for _ in range(3)]
                stg=[]; stgb=[]
                for _ in range(2):
                    o=A.top; stg.append(A.f32(TA)); o2=A.top; A.top=o; stgb.append(A.bf(TA)); A.top=o2
                for ti in range(TA//128):
                    i=ti%2
                    dma(xs[i],xsrc[t0+ti*128:t0+(ti+1)*128,:],[('xsrc',)],[('xs',i)])
                    ev_copy(xb[i],xs[i],[('xs',i)],[('xb',i)],'act')
                    pi=tprr.next()
                    for k in range(8):
                        tr(TPB(pi)[:,k*128:(k+1)*128],xb[i][:,k*128:(k+1)*128],identb,[('xb',i),('c','identb')],[pbk(pi)])
                    ev_copy(XT[:,:,ti*128:(ti+1)*128],TPB(pi).rearrange("p (k t) -> p k t",k=8),[pbk(pi)],[('XT',)],'dve')
                wi=0
                for c in range(49):
                    if c==24: continue
                    i=wi%3; si=wi%2; wi+=1
                    dma(wch[i],wb['win'][l,c],[('wb',)],[('wch',i)])
                    isbf = c>=19
                    for g in range(TA//512):
                        pi=mmrr.next()
                        for k in range(8):
                            mm(PB[pi][:,:],wch[i][:,k*128:(k+1)*128],XT[:,k,g*512:(g+1)*512],k==0,k==7,[('wch',i),('XT',)],[pbk(pi)])
                        if c>=25:
                            act(stgb[si][:,g*512:(g+1)*512],PB[pi][:,:],AF.Sigmoid,[pbk(pi)],[('stg',si)])
                        elif isbf:
                            ev_copy(stgb[si][:,g*512:(g+1)*512],PB[pi][:,:],[pbk(pi)],[('stg',si)])
                        else:
                            ev_copy(stg[si][:,g*512:(g+1)*512],PB[pi][:,:],[pbk(pi)],[('stg',si)])
                    if c<4: dst=ZP[c][:,t0:t0+TA]; src=stg[si]; key=('ZP',)
                    elif c<19: dst=ZR[c-4][:,t0:t0+TA]; src=stg[si]; key=('ZR',)
                    elif c<23: dst=QT[:,c-19,t0:t0+TA]; src=stgb[si][:,0:TA]; key=('QT',)
                    elif c==23: dst=KT[:,t0:t0+TA]; src=stgb[si][:,0:TA]; key=('KT',)
                    else: dst=GT[c-25][:,t0:t0+TA]; src=stgb[si][:,0:TA]; key=('GT',)
                    dma(dst,src,[('stg',si)],[key])
                dma(wch[0],wb['win'][l,24],[('wb',)],[('wch',0)])
                vst=stgb[0][:,0:TA].rearrange("p (n c) -> p n c",c=128)
                for ti in range(TA//128):
                    pi=mmrr.next()
                    for k in range(8):
                        mm(PB[pi][:,0:128],XT[:,k,ti*128:(ti+1)*128],wch[0][:,k*128:(k+1)*128],k==0,k==7,[('wch',0),('XT',)],[pbk(pi)])
                    ev_copy(vst[:,ti,:],PB[pi][:,0:128],[pbk(pi)],[('stg',0)])
                for n0 in range(0,TA//128,4):
                    dma(VTOK[t0+n0*128:t0+(n0+4)*128,:].rearrange("(n p) c -> p n c",p=128),vst[:,n0:n0+4,:],[('stg',0)],[('VTOK',)])
                P.barrier()

            A.top=ltop
            plw=A.bf(512)
            dma(plw,wb['poolw'][l],[('wb',)],[('plw',)])
            ptop=A.top
            TP=min(S,2048)
            for sg in (range(S//TP) if 'P' in PHASES else []):
                t0=sg*TP; n=TP+16
                for g in range(4):
                    A.top=ptop
                    u=A.f32(n); p0=A.f32(n); p1=A.f32(n); pab=A.bf(TP); ostg=A.bf(TP)
                    lo=max(t0-8,0); hi=min(t0+TP+8,S)
                    if t0-8<0: P.op('pool',lambda e,u=u:e.memset(u[:,0:8],0.0),w=[('u',)])
                    if t0+TP+8>S: P.op('pool',lambda e,u=u,n=n:e.memset(u[:,n-8:n],0.0),w=[('u',)])
                    dma(u[:,lo-(t0-8):hi-(t0-8)],ZP[g][:,lo:hi],[('ZP',)],[('u',)])
                    tt('pool',p0[:,1:n],u[:,0:n-1],u[:,1:n],ALU.add,[('u',)],[('p0',)])
                    cur,nxt,ck,nk=p0,p1,('p0',),('p1',)
                    for k in range(1,g+1):
                        d_=2**(k-1)
                        tt('pool',nxt[:,d_:n-d_],cur[:,0:n-2*d_],cur[:,2*d_:n],ALU.add,[ck],[nk])
                        cur,nxt,ck,nk=nxt,cur,nk,ck
                    wdw=POOL_WINDOWS[g]
                    act(nxt[:,8:8+TP],cur[:,8:8+TP],AF.Copy,[ck],[nk],scale=1.0/wdw)
                    if sg==0:
                        tt('dve',nxt[:,8:16],nxt[:,8:16],pedge[:,g*16:g*16+8],ALU.mult,[nk,('c','pedge')],[nk])
                    if sg==S//TP-1:
                        tt('dve',nxt[:,TP:TP+8],nxt[:,TP:TP+8],pedge[:,g*16+8:g*16+16],ALU.mult,[nk,('c','pedge')],[nk])
                    tt('dve',pab,nxt[:,8:8+TP],u[:,8:8+TP],ALU.subtract,[nk,('u',)],[('pab',)])
                    for tg in range(TP//512):
                        pi=mmrr.next()
                        mm(PB[pi][:,:],plw[:,g*128:(g+1)*128],pab[:,tg*512:(tg+1)*512],True,True,[('plw',),('pab',)],[pbk(pi)])
                        act(ostg[:,tg*512:(tg+1)*512],PB[pi][:,:],AF.Copy,[pbk(pi)],[('ostg',)],scale=pcol[:,15+g:16+g])
                    dma(AOT[:,g,t0:t0+TP],ostg,[('ostg',)],[('AOT',)])
            P.barrier()

            TQ=min(S,2048)
            attb5=attb.rearrange("p (a b c) -> p a b c",a=3,b=2)
            for sg in (range(S//TQ) if 'T' in PHASES else []):
                A.top=ltop; t0=sg*TQ; nqb=TQ//128; nkb=nqb+2
                q=A.bf(4*TQ).rearrange("p (g t) -> p g t",g=4)
                kt=A.bf(TQ+256)
                vp=[A.bf(nkb*128).rearrange("p (n c) -> p n c",c=128) for _ in range(2)]
                cstg=A.bf(4*TQ).rearrange("p (g t) -> p g t",g=4)
                sb=[A.f32(512) for _ in range(2)]; pT=[A.bf(512) for _ in range(2)]
                den=A.f32(512); rec=A.f32(512)
                dma(q,QT[:,:,t0:t0+TQ],[('QT',)],[('q',)])
                lo=max(t0-128,0); hi=min(t0+TQ+128,S)
                dma(kt[:,lo-(t0-128):hi-(t0-128)],KT[:,lo:hi],[('KT',)],[('kt',)])
                for kvh in range(2):
                    P.op('pool',lambda e,v=vp[kvh]:e.memset(v,0.0),w=[('vp',kvh)])
                    b0=(lo-(t0-128))//128; nb=(hi-lo)//128
                    for n0 in range(0,nb,4):
                        n1=min(n0+4,nb)
                        dma(vp[kvh][:,b0+n0:b0+n1,kvh*64:(kvh+1)*64],VTOK[lo+n0*128:lo+n1*128,kvh*64:(kvh+1)*64].rearrange("(n p) c -> p n c",p=128),[('VTOK',)],[('vp',kvh)])
                it=0
                for qb in range(nqb):
                    gq=sg*nqb+qb
                    lst=[(kvh,kbi) for kvh in range(2) for kbi in range(3) if 0<=gq+kbi-1<NT]
                    for idx,(kvh,kbi) in enumerate(lst):
                        pr=slice(kvh*64,(kvh+1)*64)
                        pi=mmrr.next(); si=it%2; it+=1
                        kb=qb+kbi
                        mm(PB[pi][:,:],kt[pr,kb*128:(kb+1)*128],q[pr,:,qb*128:(qb+1)*128],True,True,[('kt',),('q',)],[pbk(pi)])
                        stt('dve',sb[si],PB[pi][:,:],0.125,attb5[:,kbi,kvh,:],ALU.mult,ALU.add,[pbk(pi),('c','attb')],[('sb',si)])
                        act(pT[si],sb[si],AF.Exp,[('sb',si)],[('pT',si)])
                        mm(PB[5][:,:],vp[kvh][:,kb,:],pT[si],idx==0,idx==len(lst)-1,[('vp',kvh),('pT',si)],[pbk(5)])
                        mm(PB[6][:,:],onespad[:,kvh*128:(kvh+1)*128],pT[si],idx==0,idx==len(lst)-1,[('c','onespadb'),('pT',si)],[pbk(6)])
                    tt('dve',den,PB[6][:,:],esink,ALU.add,[pbk(6),('esink',)],[('den',)])
                    recip(rec,den,[('den',)],[('rec',)])
                    tt('dve',cstg[:,:,qb*128:(qb+1)*128],PB[5][:,:].rearrange("p (g t) -> p g t",g=4),rec.rearrange("p (g t) -> p g t",g=4),ALU.mult,[pbk(5),('rec',)],[('cstg',)])
                dma(COT[:,:,t0:t0+TQ],cstg,[('cstg',)],[('COT',)])
                P.barrier()

            TB=256; NBLK=S//TB; NTB=TB//128; NCK=TB//64
            NEG=-math.exp(-0.5)
            for d in ((1,0) if 'R' in PHASES else ()):
                A.top=ltop
                w2s=A.bf(512); a2s=A.bf(512); g2s=A.bf(512)
                dma(w2s,wb['w2'][l],[('wb',)],[('w2s',)]); dma(a2s,wb['a2'][l],[('wb',)],[('a2s',)]); dma(g2s,wb['g2'][l],[('wb',)],[('g2s',)])
                S32=[A.f32(128) for _ in range(4)]; Sbf=[A.bf(128) for _ in range(4)]
                for cc in range(4):
                    P.op('pool',lambda e,t=S32[cc]:e.memset(t,0.0),w=[('S32',cc)])
                    P.op('pool',lambda e,t=Sbf[cc]:e.memset(t,0.0),w=[('Sbf',cc)])
                rtop=A.top
                maskA=maskf if d==0 else maskb
                maskN=maskb[:,0:128] if d==0 else maskf[:,0:128]
                blks=range(NBLK) if d==0 else range(NBLK-1,-1,-1)
                for blk in blks:
                    A.top=rtop; t0=blk*TB
                    zraw=A.f32(TB+2); t1=A.f32(TB); t2=A.f32(TB)
                    th=A.bf(TB); al=A.bf(TB); gs=A.bf(TB)
                    zr=[A.f32(TB) for _ in range(4)]; zk=[A.f32(TB) for _ in range(4)]; zv=[A.f32(TB) for _ in range(4)]
                    def shift(c,out,okey):
                        lo=max(t0-1,0); hi=min(t0+TB+1,S)
                        if t0-1<0: mset(zraw[:,0:1],0.0,[('zraw',)])
                        if t0+TB+1>S: mset(zraw[:,TB+1:TB+2],0.0,[('zraw',)])
                        dma(zraw[:,lo-(t0-1):hi-(t0-1)],ZR[c][:,lo:hi],[('ZR',)],[('zraw',)])
                        tt('pool',t1,zraw[:,0:TB],zraw[:,2:TB+2],ALU.add,[('zraw',)],[('t1',)])
                        act(t2,zraw[:,1:TB+1],AF.Copy,[('zraw',),('omm',)],[('t2',)],scale=omm[:,c:c+1])
                        stt('dve',out,t1,hmu[:,c:c+1],t2,ALU.mult,ALU.add,[('t1',),('t2',),('hmu',)],[okey])
                    for cc in range(4):
                        shift(cc,zr[cc],('zr',cc)); shift(4+cc,zk[cc],('zk',cc)); shift(8+cc,zv[cc],('zv',cc))
                    zw=A.f32(TB); za=A.f32(TB); zg=A.f32(TB)
                    shift(12,zw,('zw',)); shift(13,za,('za',))
                    act(th,zw,AF.Tanh,[('zw',)],[('th',)])
                    ev_copy(al,za,[('za',)],[('al',)],'act')
                    if d==0:
                        shift(14,zg,('zg',))
                        act(gs,zg,AF.Sigmoid,[('zg',)],[('gs',)])
                    btop=A.top
                    AR=[A.bf(NTB*2*128).rearrange("p (n a t) -> p n a t",n=NTB,a=2) for _ in range(4)]
                    BTt=[A.bf(TB) for _ in range(4)]; KTt=[A.bf(TB) for _ in range(4)]
                    BH=[A.bf(TB).rearrange("p (n c) -> p n c",c=128) for _ in range(4)]
                    KH=[A.bf(TB).rearrange("p (n c) -> p n c",c=128) for _ in range(4)]
                    VT=[A.bf(TB).rearrange("p (n c) -> p n c",c=128) for _ in range(4)]
                    gL=[A.f32(NCK) for _ in range(4)]
                    YT=[A.f32(TB) for _ in range(4)]
                    if d==0:
                        bonus=[A.f32(TB) for _ in range(4)]; gg=[A.f32(TB) for _ in range(4)]
                    AM=[[[A.bf(512) for _ in range(2)] for _ in range(NTB)] for _ in range(4)]
                    TT=[[[A.bf(128) for _ in range(2)] for _ in range(NTB)] for _ in range(4)]
                    ppb=[[A.bf(256),A.bf(256)] for _ in range(8*NTB)]; ttb=[[A.bf(128),A.bf(128)] for _ in range(8*NTB)]
                    Xsb=[A.bf(128) for _ in range(4)]; Usb=[A.bf(128) for _ in range(4)]
                    pyb=A.f32(TB); pyc=A.f32(TB); psq=A.f32(TB); pob=A.bf(TB)
                    etop=A.top
                    for cc in range(4):
                        A.top=etop
                        lw=A.f32(TB); alp=A.f32(TB); kk=A.f32(TB); tA=A.f32(TB); tB=A.f32(TB); kd=A.f32(TB); bb=A.f32(TB)
                        cumf=A.f32(TB); cbx=A.f32(TB); cumb=A.f32(TB); cfx=A.f32(TB)
                        e1=A.f32(TB); e2=A.f32(TB); e3=A.f32(TB); e4=A.f32(TB)
                        bhf=A.bf(TB); khf=A.bf(TB); vbf=A.bf(TB)
                        K=lambda n:('r1',n)
                        dr=slice(d*64,(d+1)*64)
                        pi=mmrr.next()
                        mm(PB[pi][:,0:TB],w2s[dr,cc*128:(cc+1)*128],th[dr,:],True,True,[('w2s',),('th',)],[pbk(pi)])
                        act(lw,PB[pi][:,0:TB],AF.Sigmoid,[pbk(pi),('pcol',)],[K('lw')],bias=w0c(d,cc))
                        ts('dve',lw,lw,NEG,0.0,ALU.mult,ALU.add,[K('lw')],[K('lw')])
                        pi=mmrr.next()
                        mm(PB[pi][:,0:TB],a2s[dr,cc*128:(cc+1)*128],al[dr,:],True,True,[('a2s',),('al',)],[pbk(pi)])
                        act(alp,PB[pi][:,0:TB],AF.Sigmoid,[pbk(pi),('pcol',)],[K('alp')],bias=a0c(d,cc))
                        act(tA,zk[cc],AF.Copy,[('zk',cc),('pcol',)],[K('tA')],scale=pcol[:,35+cc:36+cc])
                        tt('pool',tB,tA,tA,ALU.mult,[K('tA')],[K('tB')])
                        pi=mmrr.next()
                        mm(PB[pi][:,0:TB],bonesf,tB,True,True,[('c','bones'),K('tB')],[pbk(pi)])
                        act(tB,PB[pi][:,0:TB],AF.Sqrt,[pbk(pi)],[K('tB')])
                        ts('dve',tB,tB,1e-12,0.0,ALU.max,ALU.add,[K('tB')],[K('tB')])
                        P.op('dve',lambda e,tB=tB:e.reciprocal(out=tB,in_=tB),r=[K('tB')],w=[K('tB')])
                        tt('pool',kk,tA,tB,ALU.mult,[K('tA'),K('tB')],[K('kk')])
                        ts('dve',tA,alp,pcol[:,39+cc:40+cc],omka[:,cc:cc+1],ALU.mult,ALU.add,[K('alp'),('pcol',),('omka',)],[K('tA')])
                        tt('pool',kd,zk[cc],tA,ALU.mult,[('zk',cc),K('tA')],[K('kd')])
                        tt('pool',bb,kk,alp,ALU.mult,[K('kk'),K('alp')],[K('bb')])
                        P.op('dve',lambda e,cumf=cumf,lw=lw:e.tensor_tensor_scan(out=cumf,data0=restart[:,0:TB],data1=lw,initial=0.0,op0=ALU.mult,op1=ALU.add),r=[('c','restart'),K('lw')],w=[K('cumf')])
                        c3=lambda t:t.rearrange("p (c l) -> p c l",l=64)
                        tt('dve',c3(cbx),c3(cumf)[:,:,63:64].to_broadcast([128,NCK,64]),c3(cumf),ALU.subtract,[K('cumf')],[K('cbx')])
                        tt('pool',cumb,cbx,lw,ALU.add,[K('cbx'),K('lw')],[K('cumb')])
                        tt('pool',cfx,cumf,lw,ALU.subtract,[K('cumf'),K('lw')],[K('cfx')])
                        act(gL[cc],c3(cumf)[:,:,63],AF.Exp,[K('cumf')],[('gL',cc)])
                        ci,cx,co=(cumf,cfx,cbx) if d==0 else (cumb,cbx,cfx)
                        kci,kcx,kco=(K('cumf'),K('cfx'),K('cbx')) if d==0 else (K('cumb'),K('cbx'),K('cfx'))
                        act(e1,ci,AF.Exp,[kci],[K('e1')])
                        act(e2,ci,AF.Exp,[kci],[K('e2')],scale=-1.0)
                        act(e3,cx,AF.Exp,[kcx],[K('e3')])
                        act(e4,co,AF.Exp,[kco],[K('e4')])
                        v4=lambda t:t.rearrange("p (n t) -> p n t",t=128)
                        stt('dve',AR[cc][:,:,0,:],v4(kk),-1.0,v4(e3),ALU.mult,ALU.mult,[K('kk'),K('e3')],[('AR',cc)])
                        tt('pool',AR[cc][:,:,1,:],v4(zr[cc]),v4(e1),ALU.mult,[('zr',cc),K('e1')],[('AR',cc)])
                        tt('dve',BTt[cc],bb,e2,ALU.mult,[K('bb'),K('e2')],[('BTt',cc)])
                        tt('pool',KTt[cc],kd,e2,ALU.mult,[K('kd'),K('e2')],[('KTt',cc)])
                        tt('dve',bhf,bb,e4,ALU.mult,[K('bb'),K('e4')],[K('bhf')])
                        tt('pool',khf,kd,e4,ALU.mult,[K('kd'),K('e4')],[K('khf')])
                        ev_copy(vbf,zv[cc],[('zv',cc)],[K('vbf')],'act')
                        for src,skey,dstt,dkey in ((bhf,K('bhf'),BH[cc],('BH',cc)),(khf,K('khf'),KH[cc],('KH',cc)),(vbf,K('vbf'),VT[cc],('VT',cc))):
                            pi=tprr.next()
                            for ti in range(NTB):
                                tr(TPB(pi)[:,ti*128:(ti+1)*128],src[:,ti*128:(ti+1)*128],identb,[skey,('c','identb')],[pbk(pi)])
                            ev_copy(dstt,TPB(pi)[:,0:TB].rearrange("p (n c) -> p n c",c=128),[pbk(pi)],[dkey])
                        if d==0:
                            orr=slice(64,128)
                            pi=mmrr.next()
                            mm(PB[pi][:,0:TB],a2s[orr,cc*128:(cc+1)*128],al[orr,:],True,True,[('a2s',),('al',)],[pbk(pi)])
                            act(tA,PB[pi][:,0:TB],AF.Sigmoid,[pbk(pi),('pcol',)],[K('tA')],bias=a0c(1,cc))
                            ts('dve',tA,tA,pcol[:,39+cc:40+cc],omka[:,cc:cc+1],ALU.mult,ALU.add,[K('tA'),('pcol',),('omka',)],[K('tA')])
                            tt('pool',tA,zk[cc],tA,ALU.mult,[('zk',cc),K('tA')],[K('tA')])
                            tt('pool',tA,tA,kd,ALU.add,[K('tA'),K('kd')],[K('tA')])
                            stt('dve',bhf,zr[cc],rkh[:,cc:cc+1],tA,ALU.mult,ALU.mult,[('zr',cc),('rkh',),K('tA'),K('bhf')],[K('bhf')])
                            pi=mmrr.next()
                            mm(PB[pi][:,0:TB],bonesb,bhf,True,True,[('c','bonesb'),K('bhf')],[pbk(pi)])
                            tt('dve',bonus[cc],PB[pi][:,0:TB],zv[cc],ALU.mult,[pbk(pi),('zv',cc)],[('bonus',cc)])
                            pi=mmrr.next()
                            mm(PB[pi][:,0:TB],g2s[:,cc*128:(cc+1)*128],gs,True,True,[('g2s',),('gs',)],[pbk(pi)])
                            ev_copy(gg[cc],PB[pi][:,0:TB],[pbk(pi)],[('gg',cc)],'act')
                    insts=[(cc,ti,hh) for cc in range(4) for ti in range(NTB) for hh in range(2)]
                    cur={}
                    for ii,(cc,ti,hh) in enumerate(insts):
                        pr=slice(hh*64,(hh+1)*64)
                        am=AM[cc][ti][hh]; kam=('AM',cc,ti,hh)
                        pi=mmrr.next()
                        mm(PB[pi][:,0:256],BTt[cc][pr,ti*128:(ti+1)*128],AR[cc][pr,ti,:,:],True,True,[('BTt',cc),('AR',cc)],[pbk(pi)])
                        mm(PB[pi][:,256:512],KTt[cc][pr,ti*128:(ti+1)*128],AR[cc][pr,ti,:,:],True,True,[('KTt',cc),('AR',cc)],[pbk(pi)])
                        tt('dve',am,PB[pi][:,:],maskA,ALU.mult,[pbk(pi),('c','maskf'),('c','maskb')],[kam])
                        pi=mmrr.next()
                        mm(PB[pi][:,0:128],AR[cc][pr,ti,0,:],BTt[cc][pr,ti*128:(ti+1)*128],True,True,[('BTt',cc),('AR',cc)],[pbk(pi)])
                        tt('dve',ppb[ii][0][:,0:128],PB[pi][:,0:128],maskN,ALU.mult,[pbk(pi),('c','maskf'),('c','maskb')],[('pp',ii,0)])
                        ev_copy(ppb[ii][0][:,128:256],am[:,0:128],[kam],[('pp',ii,0)],'pool')
                        tt('pool',ttb[ii][0],identb,am[:,0:128],ALU.add,[('c','identb'),kam],[('ttx',ii,0)])
                        cur[ii]=0
                    for g0 in range(0,len(insts),8):
                        for k in range(1,6):
                            for ii,(cc,ti,hh) in list(enumerate(insts))[g0:g0+8]:
                                c_=cur[ii]; src=ppb[ii][c_]; ks=('pp',ii,c_)
                                mm(PB[ii%8][:,0:128],src[:,128:256],src[:,0:128],True,True,[ks],[pbk(ii%8)])
                                if k<5:
                                    mm(PB[ii%8][:,128:256],src[:,0:128],src[:,128:256],True,True,[ks],[pbk(ii%8)])
                            for ii,(cc,ti,hh) in list(enumerate(insts))[g0:g0+8]:
                                c_=cur[ii]; dst=ppb[ii][1-c_]; kd_=('pp',ii,1-c_)
                                w_=256 if k<5 else 128
                                ev_copy(dst[:,0:w_],PB[ii%8][:,0:w_],[pbk(ii%8)],[kd_],'act')
                            for ii,(cc,ti,hh) in list(enumerate(insts))[g0:g0+8]:
                                c_=cur[ii]; dst=ppb[ii][1-c_]; kd_=('pp',ii,1-c_)
                                mm(PB[ii%8][:,256:384],dst[:,0:128],ttb[ii][c_],True,True,[kd_,('ttx',ii,c_)],[pbk(ii%8)])
                            for ii,(cc,ti,hh) in list(enumerate(insts))[g0:g0+8]:
                                c_=cur[ii]
                                outt=TT[cc][ti][hh] if k==5 else ttb[ii][1-c_]
                                okey=('TT',cc,ti,hh) if k==5 else ('ttx',ii,1-c_)
                                tt('dve',outt,PB[ii%8][:,256:384],ttb[ii][c_],ALU.add,[pbk(ii%8),('ttx',ii,c_)],[okey])
                                cur[ii]=1-c_
                    chunks=range(NCK) if d==0 else range(NCK-1,-1,-1)
                    bX=(5,1); bU=(6,2); bY=(7,3); bS=(4,0)
                    for ch in chunks:
                        ti=ch//2; half=ch%2
                        tb=slice(half*64,(half+1)*64); cs=slice(half*64,(half+1)*64)
                        for cc in range(4):
                            b=bX[cc%2]
                            mm(PB[b][tb,0:128],AR[cc][:,ti,0,cs],Sbf[cc],True,False,[('AR',cc),('Sbf',cc)],[pbk(b)])
                            for hh in range(2):
                                hs=slice(hh*64,(hh+1)*64)
                                mm(PB[b][tb,hs],AM[cc][ti][hh][tb,256+half*64:256+half*64+64],VT[cc][tb,ti,hs],False,hh==1,[('AM',cc,ti,hh),('VT',cc)],[pbk(b)])
                            ev_copy(Xsb[cc][tb,:],PB[b][tb,0:128],[pbk(b)],[('Xsb',cc)],'act' if cc%2==0 else 'dve')
                        for cc in range(4):
                            b=bU[cc%2]
                            for hh in range(2):
                                hs=slice(hh*64,(hh+1)*64)
                                mm(PB[b][tb,hs],TT[cc][ti][hh][tb,cs],Xsb[cc][tb,hs],True,True,[('TT',cc,ti,hh),('Xsb',cc)],[pbk(b)])
                            ev_copy(Usb[cc][tb,:],PB[b][tb,0:128],[pbk(b)],[('Usb',cc)],'dve' if cc%2==0 else 'act')
                        for cc in range(4):
                            b=bY[cc%2]
                            mm(PB[b][:,0:64],Sbf[cc],AR[cc][:,ti,1,cs],True,False,[('Sbf',cc),('AR',cc)],[pbk(b)])
                            for hh in range(2):
                                hs=slice(hh*64,(hh+1)*64)
                                mm(PB[b][hs,0:64],Usb[cc][tb,hs],AM[cc][ti][hh][tb,128+half*64:128+half*64+64],False,False,[('Usb',cc),('AM',cc,ti,hh)],[pbk(b)])
                                mm(PB[b][hs,0:64],VT[cc][tb,ti,hs],AM[cc][ti][hh][tb,384+half*64:384+half*64+64],False,hh==1,[('VT',cc),('AM',cc,ti,hh)],[pbk(b)])
                            ev_copy(YT[cc][:,ch*64:(ch+1)*64],PB[b][:,0:64],[pbk(b)],[('YT',cc)],'act')
                            b=bS[cc%2]
                            mm(PB[b][:,0:128],BH[cc][tb,ti,:],Usb[cc][tb,:],True,False,[('BH',cc),('Usb',cc)],[pbk(b)])
                            mm(PB[b][:,0:128],KH[cc][tb,ti,:],VT[cc][tb,ti,:],False,True,[('KH',cc),('VT',cc)],[pbk(b)])
                            for hh in range(2):
                                hs=slice(hh*64,(hh+1)*64)
                                stt('dve',S32[cc][hs,hs],S32[cc][hs,hs],gL[cc][hs,ch:ch+1],PB[b][hs,hs],ALU.mult,ALU.add,[('S32',cc),('gL',cc),pbk(b)],[('S32',cc)])
                            ev_copy(Sbf[cc],S32[cc],[('S32',cc)],[('Sbf',cc)],'act')
                    for cc in range(4):
                        if d==1:
                            dma(YB[cc][:,t0:t0+TB],YT[cc],[('YT',cc)],[('YB',)])
                        else:
                            yb,yc,sq,ob=pyb,pyc,psq,pob
                            K=lambda n:('po',n)
                            dma(yb,YB[cc][:,t0:t0+TB],[('YB',)],[K('yb')])
                            tt('pool',yb,yb,YT[cc],ALU.add,[K('yb'),('YT',cc)],[K('yb')])
                            pi=mmrr.next()
                            mm(PB[pi][:,0:TB],bonesf,yb,True,True,[('c','bones'),K('yb')],[pbk(pi)])
                            stt('dve',yc,PB[pi][:,0:TB],-1.0/64,yb,ALU.mult,ALU.add,[pbk(pi),K('yb')],[K('yc')])
                            tt('pool',sq,yc,yc,ALU.mult,[K('yc')],[K('sq')])
                            pi=mmrr.next()
                            mm(PB[pi][:,0:TB],bonesf,sq,True,True,[('c','bones'),K('sq')],[pbk(pi)])
                            act(sq,PB[pi][:,0:TB],AF.Sqrt,[pbk(pi)],[K('sq')],bias=GN_EPS,scale=1.0/64)
                            P.op('dve',lambda e,sq=sq:e.reciprocal(out=sq,in_=sq),r=[K('sq')],w=[K('sq')])
                            tt('pool',yc,yc,sq,ALU.mult,[K('yc'),K('sq')],[K('yc')])
                            ts('dve',yc,yc,pcol[:,47+cc:48+cc],pcol[:,51+cc:52+cc],ALU.mult,ALU.add,[K('yc'),('pcol',)],[K('yc')])
                            tt('pool',yc,yc,bonus[cc],ALU.add,[K('yc'),('bonus',cc)],[K('yc')])
                            tt('dve',ob,yc,gg[cc],ALU.mult,[K('yc'),('gg',cc)],[K('ob')])
                            dma(BOT[:,cc,t0:t0+TB],ob,[K('ob')],[('BOT',)])
                    P.barrier()

            TC=512
            for cg in (range(S//TC) if 'C' in PHASES else []):
                A.top=ltop; t0=cg*TC
                X1=A.f32(4*1024).rearrange("p (n d) -> p n d",n=4)
                X1T=A.bf(8*TC).rearrange("p (k t) -> p k t",k=8)
                c1top=A.top
                MT=A.bf(8*TC).rearrange("p (k t) -> p k t",k=8)
                XO=[A.bf(4*TC).rearrange("p (k t) -> p k t",k=4) for _ in range(3)]
                wbrs=A.bf(12288).rearrange("p (b k d) -> p b k d",b=3,k=4)
                wos=A.bf(8192).rearrange("p (k d) -> p k d",k=8)
                gt=[A.bf(TC) for _ in range(3)]
                acc=A.f32(TC); tmp=A.f32(TC)
                xs=[A.f32(1024) for _ in range(2)]; ypre=A.f32(1024); xbb=A.bf(1024)
                st1=A.f32(4); junk=A.f32(1024)
                for b,src in enumerate((AOT,BOT,COT)):
                    dma(XO[b],src[:,:,t0:t0+TC],[('AOT',),('BOT',),('COT',)],[('XO',b)])
                dma(wbrs.rearrange("p b k d -> p (b k d)"),wb['wbr'][l],[('wb',)],[('wbrs',)])
                dma(wos.rearrange("p k d -> p (k d)"),wb['wo'][l],[('wb',)],[('wos',)])
                for m in range(8):
                    for b in range(3):
                        dma(gt[b],GT[b*8+m][:,t0:t0+TC],[('GT',)],[('gt',b)])
                        pi=mmrr.next()
                        for k in range(4):
                            mm(PB[pi][:,:],wbrs[:,b,k,m*128:(m+1)*128],XO[b][:,k,:],k==0,k==3,[('wbrs',),('XO',b)],[pbk(pi)])
                        if b==0:
                            tt('dve',acc,PB[pi][:,:],gt[b],ALU.mult,[pbk(pi),('gt',b)],[('acc',)])
                        else:
                            tt('dve',tmp,PB[pi][:,:],gt[b],ALU.mult,[pbk(pi),('gt',b)],[('tmp',)])
                            if b==1: tt('pool',acc,acc,tmp,ALU.add,[('acc',),('tmp',)],[('acc',)])
                            else: tt('pool',MT[:,m,:],acc,tmp,ALU.add,[('acc',),('tmp',)],[('MT',)])
                def layer_norm(gi,out_ap,okey):
                    red(st1[:,0:1],ypre,[('ypre',)],[('st1',)])
                    ts('dve',st1[:,1:2],st1[:,0:1],-1.0/1024,0.0,ALU.mult,ALU.add,[('st1',)],[('st1',)])
                    ts('dve',ypre,ypre,st1[:,1:2],0.0,ALU.add,ALU.add,[('ypre',),('st1',)],[('ypre',)])
                    tt('pool',junk,ypre,ypre,ALU.mult,[('ypre',)],[('junk',)])
                    red(st1[:,2:3],junk,[('junk',)],[('st1',)])
                    act(st1[:,3:4],st1[:,2:3],AF.Sqrt,[('st1',)],[('st1',)],bias=LN_EPS,scale=1.0/1024)
                    recip(st1[:,3:4],st1[:,3:4],[('st1',)],[('st1',)])
                    stt('dve',ypre,ypre,st1[:,3:4],lnrow[gi],ALU.mult,ALU.mult,[('ypre',),('st1',),('lnrow',gi)],[('ypre',)])
                    tt('pool',out_ap,ypre,lnrow[gi+1],ALU.add,[('ypre',),('lnrow',gi+1)],[okey])
                for ti in range(TC//128):
                    i=ti%2
                    dma(xs[i],xsrc[t0+ti*128:t0+(ti+1)*128,:],[('xsrc',)],[('xs',i)])
                    for half in range(2):
                        pi=mmrr.next()
                        for k in range(8):
                            mm(PB[pi][:,:],MT[:,k,ti*128:(ti+1)*128],wos[:,k,half*512:(half+1)*512],k==0,k==7,[('MT',),('wos',)],[pbk(pi)])
                        stt('dve',ypre[:,half*512:(half+1)*512],xs[i][:,half*512:(half+1)*512],ALPHA,PB[pi][:,:],ALU.mult,ALU.add,[('xs',i),pbk(pi)],[('ypre',)])
                    layer_norm(0,X1[:,ti,:],('X1',))
                    ev_copy(xbb,X1[:,ti,:],[('X1',)],[('xbb',)],'act')
                    pi=tprr.next()
                    for k in range(8):
                        tr(TPB(pi)[:,k*128:(k+1)*128],xbb[:,k*128:(k+1)*128],identb,[('xbb',),('c','identb')],[pbk(pi)])
                    ev_copy(X1T[:,:,ti*128:(ti+1)*128],TPB(pi).rearrange("p (k t) -> p k t",k=8),[pbk(pi)],[('X1T',)],'dve')
                P.barrier()
                A.top=c1top
                H=A.bf(32*TC).rearrange("p (k t) -> p k t",k=32)
                w2f=A.bf(32768).rearrange("p (k d) -> p k d",k=32)
                w1c=[A.bf(1024) for _ in range(3)]
                rl=[A.f32(TC) for _ in range(2)]
                ypre=A.f32(1024); st1=A.f32(4); junk=A.f32(1024); ot=[A.f32(1024) for _ in range(2)]
                dma(w2f.rearrange("p k d -> p (k d)"),wb['ff2'][l],[('wb',)],[('w2f',)])
                for m in range(32):
                    i=m%3; ri=m%2
                    dma(w1c[i],wb['ff1'][l,m],[('wb',)],[('w1c',i)])
                    pi=mmrr.next()
                    for k in range(8):
                        mm(PB[pi][:,:],w1c[i][:,k*128:(k+1)*128],X1T[:,k,:],k==0,k==7,[('w1c',i),('X1T',)],[pbk(pi)])
                    act(rl[ri],PB[pi][:,:],AF.Relu,[pbk(pi)],[('rl',ri)])
                    tt('pool' if m%2 else 'dve',H[:,m,:],rl[ri],rl[ri],ALU.mult,[('rl',ri)],[('H',)])
                for ti in range(TC//128):
                    for half in range(2):
                        pi=mmrr.next()
                        for k in range(32):
                            mm(PB[pi][:,:],H[:,k,ti*128:(ti+1)*128],w2f[:,k,half*512:(half+1)*512],k==0,k==31,[('H',),('w2f',)],[pbk(pi)])
                        stt('dve',ypre[:,half*512:(half+1)*512],X1[:,ti,half*512:(half+1)*512],ALPHA,PB[pi][:,:],ALU.mult,ALU.add,[('X1',),pbk(pi)],[('ypre',)])
                    layer_norm(2,ot[ti%2],('ot',ti%2))
                    dma(xdst[t0+ti*128:t0+(ti+1)*128,:],ot[ti%2],[('ot',ti%2)],[('xdst',)])
                P.barrier()
    P.run()
    for cm in reversed(pbcm): cm.__exit__(None,None,None)
    return nc

N_CORES=8
_CACHE={}
def _run(job_S, depth, per_core_x, inp, debug=False):
    key=(tuple(job_S),depth)
    nc=build(job_S,depth,debug)
    consts=host_consts(np.asarray(inp['rel_bias'],np.float32))
    wts=host_weights({k:np.asarray(v,np.float32) for k,v in inp.items() if k not in ('x_prompt','x_sample')},depth)
    in_maps=[]
    for c in range(len(per_core_x)):
        m={}
        m.update(consts); m.update(wts)
        for j,x in enumerate(per_core_x[c]): m[f"x{j}"]=np.ascontiguousarray(x,dtype=np.float32)
        in_maps.append(m)
    res=run_bass_kernel_spmd(nc,in_maps,core_ids=list(range(len(per_core_x))))
    return res.results

def kernel(**inputs):
    xp=np.asarray(inputs['x_prompt'],np.float32); xs=np.asarray(inputs['x_sample'],np.float32)
    job_S=[2048,2048,16384]
    per_core=[]
    zero=np.zeros((16384,D),np.float32)
    for c in range(N_CORES):
        per_core.append([xp[2*c],xp[2*c+1],xs[c] if c<2 else zero])
    res=_run(job_S,4,per_core,inputs)
    yp=np.stack([res[c][f"y{j}"] for c in range(N_CORES) for j in range(2)],0).astype(np.float32)
    ys=np.stack([res[c]["y2"] for c in range(2)],0).astype(np.float32)
    return (yp,ys)
```

```python
import numpy as np
import contextlib, math, os
PHASES=os.environ.get('KPHASES','APTRC')
import concourse.bass as bass
import concourse.mybir as mybir
from concourse.bass_utils import run_bass_kernel_spmd
F32=mybir.dt.float32; BF16=mybir.dt.bfloat16
AF=mybir.ActivationFunctionType; ALU=mybir.AluOpType; AX=mybir.AxisListType
ENG=['pe','act','dve','pool','sp']
NROT=8
SAME_ENG_SYNC=True
D=1024; C=512; NIN=6272; DFF=4096
POOL_WINDOWS=(2,4,8,16)
ALPHA=(2*4)**0.25
LN_EPS=1e-5; GN_EPS=64e-5
NCOL=64

class Prog:
    def __init__(self, nc):
        self.nc=nc
        self.ops={e:[] for e in ENG}
        self.segs=[]
        self.lastw={}; self.readers={}
        self.dma_cnt={e:0 for e in ENG}
        self.last_rec={}
    def semid(self, rec):
        return ('d',rec['eng'],rec['dk']%NROT) if rec['dma'] else ('c',rec['eng'])
    def op(self, eng, fn, r=(), w=(), dma=False):
        rec=dict(eng=eng, fn=fn, waits=[], signal=bool(dma), dma=bool(dma))
        if dma:
            rec['dk']=self.dma_cnt[eng]; self.dma_cnt[eng]+=1
        def need(p):
            if p is None or p is rec: return
            if (not p['dma']) and (not dma) and p['eng']==eng and (eng=='pe' or not SAME_ENG_SYNC): return
            p['signal']=True; rec['waits'].append(p)
        for key in r: need(self.lastw.get(key))
        for key in w:
            need(self.lastw.get(key))
            for p in self.readers.get(key,{}).values(): need(p)
        sid=self.semid(rec)
        for key in r: self.readers.setdefault(key,{})[sid]=rec
        for key in w:
            self.lastw[key]=rec; self.readers[key]={}
        self.ops[eng].append(rec); self.last_rec[sid]=rec
        return rec
    def _setup(self):
        if getattr(self,'sems',None) is not None: return
        nc=self.nc
        self.stack=contextlib.ExitStack()
        self.sems={}
        for e in ENG:
            self.sems[('c',e)]=self.stack.enter_context(nc.semaphore(f"c_{e}"))
        for i in range(NROT):
            self.sems[('d','sp',i)]=self.stack.enter_context(nc.semaphore(f"d_sp_{i}"))
        self.emitted={e:0 for e in ENG}; self.ccnt={e:0 for e in ENG}; self.waited={e:{} for e in ENG}
    def barrier(self):
        self._setup()
        lasts=list(self.last_rec.values())
        for e in ENG:
            rec=dict(eng=e, fn=None, waits=[], signal=False, dma=False)
            for p in lasts:
                if p['eng']==e and not p['dma'] and e=='pe': continue
                p['signal']=True; rec['waits'].append(p)
            self.ops[e].append(rec)
        self.lastw={}; self.readers={}
        self.emit_segment()
    def emit_segment(self):
        nc=self.nc; sems=self.sems
        for e in ENG:
            for rec in self.ops[e][self.emitted[e]:]:
                if rec['dma']:
                    k=rec['dk']; rec['sv']=(('d',e,k%NROT),16*(k//NROT+1))
                elif rec['signal']:
                    self.ccnt[e]+=1; rec['sv']=(('c',e),self.ccnt[e])
        nosync=bool(os.environ.get('KNOSYNC'))
        with nc.Block() as block:
            def replay(ename):
                def f(e):
                    wd=self.waited[ename]
                    for rec in self.ops[ename][self.emitted[ename]:]:
                        ws={}
                        for p in rec['waits']:
                            s,v=p['sv']
                            if v>ws.get(s,0): ws[s]=v
                        for s,v in ws.items():
                            if wd.get(s,0)>=v: continue
                            if not nosync: e.wait_ge(sems[s],v)
                            wd[s]=v
                        if rec['fn'] is None: continue
                        ins=rec['fn'](e)
                        if rec['signal'] and not nosync:
                            ins.then_inc(sems[rec['sv'][0]],16 if rec['dma'] else 1)
                        rec['fn']=None
                return f
            block.tensor(replay('pe')); block.scalar(replay('act')); block.vector(replay('dve'))
            block.gpsimd(replay('pool')); block.sync(replay('sp'))
        for e in ENG: self.emitted[e]=len(self.ops[e])
    def run(self):
        self._setup()
        lasts=list(self.last_rec.values())
        rec=dict(eng='sp', fn=None, waits=[], signal=False, dma=False)
        for p in lasts:
            p['signal']=True; rec['waits'].append(p)
        self.ops['sp'].append(rec)
        self.emit_segment()
        self.stack.close()

class Arena:
    def __init__(self, nc, words):
        self.nc=nc
        self.t=nc.alloc_sbuf_tensor("arena",[128,words],F32); self.words=words; self.top=0; self.n=0
        self.base=nc.lookup_mloc(self.t).addr
        self.gen=0; self.cache={}
    def _get(self, off_words, n, dt):
        key=(self.gen,off_words,n,str(dt))
        t=self.cache.get(key)
        if t is None:
            t=self.nc.alloc_sbuf_tensor_at(f"a{len(self.cache)}",[128,n],dt,offset=self.base+4*off_words)
            self.cache[key]=t
        return t
    def f32(self, n0):
        n=(n0+7)//8*8
        assert self.top+n<=self.words, (self.top,n,self.words)
        t=self._get(self.top,n,F32); self.top+=n
        return t[:,0:n0]
    def bf(self, n):
        w=(n+15)//16*8
        assert self.top+w<=self.words, (self.top,w,self.words)
        t=self._get(self.top,2*w,BF16); self.top+=w
        return t[:,0:n]

def _t5_bucket(rel):
    half=16; max_exact=8
    n=np.abs(rel)
    large=max_exact+(np.log(np.maximum(n,1)/max_exact)/math.log(128/max_exact)*(half-max_exact)).astype(np.int32)
    large=np.minimum(large,half-1)
    return (rel>0).astype(np.int32)*half+np.where(n<max_exact,n,large)

def host_consts(rel_bias):
    c={}
    c['ident']=np.eye(128,dtype=np.float32)
    p=np.arange(128)[:,None]; f=np.arange(128)[None,:]
    same=(p//64)==(f//64)
    MUs=((p<f)&same).astype(np.float32); MUi=((p<=f)&same).astype(np.float32)
    MLs=((p>f)&same).astype(np.float32); MLi=((p>=f)&same).astype(np.float32)
    c['maskf']=np.concatenate([MUs,MUi,MUs,MUi],1); c['maskb']=np.concatenate([MLs,MLi,MLs,MLi],1)
    rs=np.ones((128,512),np.float32); rs[:,::64]=0.0; c['restart']=rs
    c['bones']=same.astype(np.float32)
    k=np.arange(128)[:,None]; q=np.arange(128)[None,:]
    ab=np.zeros((128,3,2,4,128),np.float32)
    for kbi,kb in enumerate((-1,0,1)):
        rel=kb*128+k-q
        bk=_t5_bucket(rel); ok=np.abs(rel)<=128
        for kvh in range(2):
            for g in range(4):
                ab[:,kbi,kvh,g,:]=np.where(ok,rel_bias[bk,kvh*4+g],np.float32(-30000.0))
    c['attb']=ab.reshape(128,3072)
    op=np.zeros((128,2,128),np.float32); op[:,0,0:64]=1; op[:,1,64:128]=1; c['onespad']=op.reshape(128,256)
    pe=np.ones((128,4,2,8),np.float32)
    for g,w in enumerate(POOL_WINDOWS):
        h=w//2
        for t in range(h): pe[:,g,0,t]=w/(t+h)
        for i in range(h-1): pe[:,g,1,7-i]=w/(i+1+h)
    c['pedge']=pe.reshape(128,64)
    return c

def host_weights(inp, depth):
    L=depth; w={}
    w_in=inp['w_in'][:L]
    cols=list(range(0,C+1920))
    oq=C+1920
    for j in range(4):
        cols+=list(range(oq+j*64,oq+j*64+64))+list(range(oq+(4+j)*64,oq+(4+j)*64+64))
    cols+=list(range(oq+512,NIN))
    wp=w_in[:,:,cols]
    w['win']=np.ascontiguousarray(wp.reshape(L,8,128,49,128).transpose(0,3,2,1,4).reshape(L,49,128,1024))
    w['poolw']=np.ascontiguousarray(inp['pool_w'][:L].transpose(0,2,1,3).reshape(L,128,512))
    w['w2']=np.ascontiguousarray(inp['rwkv_w2'][:L].reshape(L,128,512))
    w['a2']=np.ascontiguousarray(inp['rwkv_a2'][:L].reshape(L,128,512))
    w['g2']=np.ascontiguousarray(inp['rwkv_g2'][:L])
    wb=inp['w_branch'][:L].copy()
    perm=[kvh*256+g*64+d for g in range(4) for kvh in range(2) for d in range(64)]
    wb[:,2]=wb[:,2][:,perm]
    w['wbr']=np.ascontiguousarray(wb.reshape(L,3,4,128,1024).transpose(0,3,1,2,4).reshape(L,128,12*1024))
    w['wo']=np.ascontiguousarray(inp['w_o'][:L].reshape(L,8,128,1024).transpose(0,2,1,3).reshape(L,128,8*1024))
    w['ff1']=np.ascontiguousarray(inp['w_ff1'][:L].reshape(L,8,128,32,128).transpose(0,3,2,1,4).reshape(L,32,128,1024))
    w['ff2']=np.ascontiguousarray(inp['w_ff2'][:L].reshape(L,32,128,1024).transpose(0,2,1,3).reshape(L,128,32*1024))
    pc=np.zeros((L,128,NCOL),np.float32)
    pc[:,:,0:15]=inp['rwkv_mu'][:L].reshape(L,15,128).transpose(0,2,1)
    pc[:,:,15:19]=inp['pool_scale'][:L].reshape(L,4,128).transpose(0,2,1)
    pc[:,:,19:27]=inp['rwkv_w0'][:L].reshape(L,8,128).transpose(0,2,1)
    pc[:,:,27:35]=inp['rwkv_a0'][:L].reshape(L,8,128).transpose(0,2,1)
    pc[:,:,35:39]=inp['rwkv_k_k'][:L].reshape(L,4,128).transpose(0,2,1)
    pc[:,:,39:43]=inp['rwkv_k_a'][:L].reshape(L,4,128).transpose(0,2,1)
    pc[:,:,43:47]=inp['rwkv_r_k'][:L].reshape(L,4,128).transpose(0,2,1)
    pc[:,:,47:51]=inp['rwkv_gn_g'][:L].reshape(L,4,128).transpose(0,2,1)
    pc[:,:,51:55]=inp['rwkv_gn_b'][:L].reshape(L,4,128).transpose(0,2,1)
    sk=inp['attn_sink'][:L].reshape(L,2,1,4)
    pc[:,:,55:59]=np.broadcast_to(sk,(L,2,64,4)).reshape(L,128,4)
    w['pcol']=pc
    w['lnrow']=np.ascontiguousarray(np.broadcast_to(
        np.stack([inp['ln1_g'][:L],inp['ln1_b'][:L],inp['ln2_g'][:L],inp['ln2_b'][:L]],1)[:,:,None,:],(L,4,128,1024)))
    return w

WSHAPES=lambda L:{'win':[L,49,128,1024],'poolw':[L,128,512],'w2':[L,128,512],'a2':[L,128,512],'g2':[L,128,512],
        'wbr':[L,128,12288],'wo':[L,128,8192],'ff1':[L,32,128,1024],'ff2':[L,128,32768],'pcol':[L,128,NCOL],'lnrow':[L,4,128,1024]}
CSHAPES={'ident':[128,128],'maskf':[128,512],'maskb':[128,512],'restart':[128,512],'bones':[128,128],'attb':[128,3072],
         'onespad':[128,256],'pedge':[128,64]}
BFW=['win','poolw','w2','a2','g2','wbr','wo','ff1','ff2']

def build(job_S, depth, debug=False):
    IK='ExternalOutput' if debug else 'Internal'
    nc=bass.Bass("TRN2", target_bir_lowering=False)
    P=Prog(nc); L=depth
    SMAX=max(job_S)
    din={}
    for n,s in WSHAPES(L).items(): din[n]=nc.dram_tensor(n,s,F32,kind="ExternalInput").ap()
    for n,s in CSHAPES.items(): din[n]=nc.dram_tensor(n,s,F32,kind="ExternalInput").ap()
    xin=[nc.dram_tensor(f"x{j}",[S,D],F32,kind="ExternalInput").ap() for j,S in enumerate(job_S)]
    yout=[nc.dram_tensor(f"y{j}",[S,D],F32,kind="ExternalOutput").ap() for j,S in enumerate(job_S)]
    wb={n:nc.dram_tensor(n+"_b",WSHAPES(L)[n],BF16,kind="Internal").ap() for n in BFW}
    ZP=nc.dram_tensor("ZP",[4,128,SMAX],F32,kind=IK).ap()
    ZR=nc.dram_tensor("ZR",[15,128,SMAX],F32,kind=IK).ap()
    QT=nc.dram_tensor("QT",[128,4,SMAX],BF16,kind=IK).ap()
    KT=nc.dram_tensor("KT",[128,SMAX],BF16,kind=IK).ap()
    VTOK=nc.dram_tensor("VTOK",[SMAX,128],BF16,kind=IK).ap()
    GT=nc.dram_tensor("GT",[24,128,SMAX],BF16,kind=IK).ap()
    AOT=nc.dram_tensor("AOT",[128,4,SMAX],BF16,kind=IK).ap()
    BOT=nc.dram_tensor("BOT",[128,4,SMAX],BF16,kind=IK).ap()
    COT=nc.dram_tensor("COT",[128,4,SMAX],BF16,kind=IK).ap()
    YB=nc.dram_tensor("YB",[4,128,SMAX],F32,kind=IK).ap()
    XTOK=nc.dram_tensor("XTOK",[SMAX,D],F32,kind=IK).ap()
    A=Arena(nc,49152)
    PB=[None]*8; PBB=[None]*8; pbcm=[]
    pbk=lambda i:('pb',i)
    class RR:
        def __init__(s,ids): s.ids=ids; s.i=0
        def next(s):
            i=s.ids[s.i%len(s.ids)]; s.i+=1; return i
    mmrr=RR([0,1,2]); tprr=RR([3,4])
    def TPB(i): return PB[i][:, :].bitcast(BF16)
    def new_psum(tag):
        for cm in reversed(pbcm): cm.__exit__(None,None,None)
        pbcm.clear()
        for i in range(8):
            cm=nc.psum_tensor(f"pb{i}_{tag}",[128,512],F32); PB[i]=cm.__enter__(); pbcm.append(cm)
    new_psum('init')
    def dma(out,in_,r,w): P.op('sp',lambda e:e.dma_start(out=out,in_=in_),r=r,w=w,dma=True)
    ccnt=[0]
    def ev_copy(out,in_,r,w,eng=None):
        if eng is None:
            eng='act' if ccnt[0]%2==0 else 'dve'; ccnt[0]+=1
        if eng=='act': P.op('act',lambda e:e.activation(out=out,in_=in_,func=AF.Copy),r=r,w=w)
        elif eng=='dve': P.op('dve',lambda e:e.tensor_copy(out=out,in_=in_),r=r,w=w)
        else: P.op('pool',lambda e:e.tensor_copy(out=out,in_=in_),r=r,w=w)
    def tt(eng,out,a,b,op,r,w): P.op(eng,lambda e:e.tensor_tensor(out=out,in0=a,in1=b,op=op),r=r,w=w)
    def ts(eng,out,a,s1,s2,op0,op1,r,w): P.op(eng,lambda e:e.tensor_scalar(out=out,in0=a,scalar1=s1,scalar2=s2,op0=op0,op1=op1),r=r,w=w)
    def stt(eng,out,a,s,b,op0,op1,r,w): P.op(eng,lambda e:e.scalar_tensor_tensor(out=out,in0=a,scalar=s,in1=b,op0=op0,op1=op1),r=r,w=w)
    def act(out,in_,func,r,w,bias=None,scale=None):
        kw={}
        if bias is not None: kw['bias']=bias
        if scale is not None: kw['scale']=scale
        P.op('act',lambda e:e.activation(out=out,in_=in_,func=func,**kw),r=r,w=w)
    def red(out,in_,r,w): P.op('dve',lambda e:e.tensor_reduce(out=out,in_=in_,axis=AX.X,op=ALU.add),r=r,w=w)
    def recip(out,in_,r,w): P.op('dve',lambda e:e.reciprocal(out=out,in_=in_),r=r,w=w)
    def mset(ap,val,w): P.op('pool',lambda e:e.memset(ap,val),w=w)
    def mm(out,lhsT,rhs,start,stop,r,w): P.op('pe',lambda e:e.matmul(out,lhsT,rhs,start=start,stop=stop),r=r,w=w)
    def tr(out,in_,idn,r,w): P.op('pe',lambda e:e.transpose(out,in_,idn),r=r,w=w)

    identf=A.f32(128); maskf=A.f32(512); maskb=A.f32(512); restart=A.f32(512); bonesf=A.f32(128)
    attb=A.f32(3072); pedge=A.f32(64); tmpc=A.f32(256)
    identb=A.bf(128); bonesb=A.bf(128); onespad=A.bf(256)
    pcol=A.f32(NCOL); omm=A.f32(16); hmu=A.f32(16); omka=A.f32(4); rkh=A.f32(4); esink=A.f32(512); esk4=A.f32(4)
    lnrow=[A.f32(1024) for _ in range(4)]
    for n,t in [('ident',identf),('maskf',maskf),('maskb',maskb),('restart',restart),('bones',bonesf),('attb',attb),('pedge',pedge),('onespad',tmpc)]:
        dma(t,din[n][:,:],[],[('c',n)])
    ev_copy(identb,identf,[('c','ident')],[('c','identb')],'dve')
    ev_copy(bonesb,bonesf,[('c','bones')],[('c','bonesb')],'dve')
    ev_copy(onespad,tmpc,[('c','onespad')],[('c','onespadb')],'dve')
    P.barrier()
    base_top=A.top
    def cast_rows(src,dst,ncols):
        CH=4096
        for c0 in range(0,ncols,CH):
            cw=min(CH,ncols-c0)
            i=castrr.next()
            dma(cst[i][:,0:cw],src[:,c0:c0+cw],[],[('cst',i)])
            ev_copy(csb[i][:,0:cw],cst[i][:,0:cw],[('cst',i)],[('csb',i)])
            dma(dst[:,c0:c0+cw],csb[i][:,0:cw],[('csb',i)],[('wb',)])
    cst=[A.f32(4096) for _ in range(3)]; csb=[A.bf(4096) for _ in range(3)]; castrr=RR([0,1,2])
    for l in range(L):
        for n in BFW:
            s=WSHAPES(L)[n]
            if len(s)==4:
                for c in range(s[1]): cast_rows(din[n][l,c],wb[n][l,c],s[3])
            else:
                cast_rows(din[n][l],wb[n][l],s[2])
    P.barrier()

    for j,S in enumerate(job_S):
        NT=S//128
        for l in range(L):
            xsrc=xin[j] if l==0 else XTOK
            xdst=yout[j] if l==L-1 else XTOK
            A.top=base_top; A.gen+=1; new_psum(f'{j}_{l}')
            dma(pcol,din['pcol'][l],[],[('pcol',)])
            for i in range(4): dma(lnrow[i],din['lnrow'][l,i],[],[('lnrow',i)])
            ts('dve',omm[:,0:15],pcol[:,0:15],-1.0,1.0,ALU.mult,ALU.add,[('pcol',)],[('omm',)])
            ts('dve',hmu[:,0:15],pcol[:,0:15],0.5,0.0,ALU.mult,ALU.add,[('pcol',)],[('hmu',)])
            ts('dve',omka,pcol[:,39:43],-1.0,1.0,ALU.mult,ALU.add,[('pcol',)],[('omka',)])
            ts('dve',rkh,pcol[:,43:47],0.5,0.0,ALU.mult,ALU.add,[('pcol',)],[('rkh',)])
            act(esk4,pcol[:,55:59],AF.Exp,[('pcol',)],[('esk4',)])
            for g in range(4):
                ts('dve',esink[:,g*128:(g+1)*128],identf,0.0,esk4[:,g:g+1],ALU.mult,ALU.add,[('esk4',),('c','ident')],[('esink',)])
            P.barrier()
            ltop=A.top
            w0c=lambda d,cc: pcol[:,19+d*4+cc:20+d*4+cc]
            a0c=lambda d,cc: pcol[:,27+d*4+cc:28+d*4+cc]

            TA=min(S,2048)
            for sg in (range(S//TA) if 'A' in PHASES else []):
                A.top=ltop; t0=sg*TA
                XT=A.bf(8*TA).rearrange("p (k t) -> p k t",k=8)
                xs=[A.f32(1024) for _ in range(2)]; xb=[A.bf(1024) for _ in range(2)]
                wch=[A.bf(1024) for _ in range(3)]
                stg=[]; stgb=[]
                for _ in range(2):
                    o=A.top; stg.append(A.f32(TA)); o2=A.top; A.top=o; stgb.append(A.bf(TA)); A.top=o2
                for ti in range(TA//128):
                    i=ti%2
                    dma(xs[i],xsrc[t0+ti*128:t0+(ti+1)*128,:],[('xsrc',)],[('xs',i)])
                    ev_copy(xb[i],xs[i],[('xs',i)],[('xb',i)],'act')
                    pi=tprr.next()
                    for k in range(8):
                        tr(TPB(pi)[:,k*128:(k+1)*128],xb[i][:,k*128:(k+1)*128],identb,[('xb',i),('c','identb')],[pbk(pi)])
                    ev_copy(XT[:,:,ti*128:(ti+1)*128],TPB(pi).rearrange("p (k t) -> p k t",k=8),[pbk(pi)],[('XT',)],'dve')
                wi=0
                for c in range(49):
                    if c==24: continue
                    i=wi%3; si=wi%2; wi+=1
                    dma(wch[i],wb['win'][l,c],[('wb',)],[('wch',i)])
                    isbf = c>=19
                    for g in range(TA//512):
                        pi=mmrr.next()
                        for k in range(8):
                            mm(PB[pi][:,:],wch[i][:,k*128:(k+1)*128],XT[:,k,g*512:(g+1)*512],k==0,k==7,[('wch',i),('XT',)],[pbk(pi)])
                        if c>=25:
                            act(stgb[si][:,g*512:(g+1)*512],PB[pi][:,:],AF.Sigmoid,[pbk(pi)],[('stg',si)])
                        elif isbf:
                            ev_copy(stgb[si][:,g*512:(g+1)*512],PB[pi][:,:],[pbk(pi)],[('stg',si)])
                        else:
                            ev_copy(stg[si][:,g*512:(g+1)*512],PB[pi][:,:],[pbk(pi)],[('stg',si)])
                    if c<4: dst=ZP[c][:,t0:t0+TA]; src=stg[si]; key=('ZP',)
                    elif c<19: dst=ZR[c-4][:,t0:t0+TA]; src=stg[si]; key=('ZR',)
                    elif c<23: dst=QT[:,c-19,t0:t0+TA]; src=stgb[si][:,0:TA]; key=('QT',)
                    elif c==23: dst=KT[:,t0:t0+TA]; src=stgb[si][:,0:TA]; key=('KT',)
                    else: dst=GT[c-25][:,t0:t0+TA]; src=stgb[si][:,0:TA]; key=('GT',)
                    dma(dst,src,[('stg',si)],[key])
                dma(wch[0],wb['win'][l,24],[('wb',)],[('wch',0)])
                vst=stgb[0][:,0:TA].rearrange("p (n c) -> p n c",c=128)
                for ti in range(TA//128):
                    pi=mmrr.next()
                    for k in range(8):
                        mm(PB[pi][:,0:128],XT[:,k,ti*128:(ti+1)*128],wch[0][:,k*128:(k+1)*128],k==0,k==7,[('wch',0),('XT',)],[pbk(pi)])
                    ev_copy(vst[:,ti,:],PB[pi][:,0:128],[pbk(pi)],[('stg',0)])
                for n0 in range(0,TA//128,4):
                    dma(VTOK[t0+n0*128:t0+(n0+4)*128,:].rearrange("(n p) c -> p n c",p=128),vst[:,n0:n0+4,:],[('stg',0)],[('VTOK',)])
                P.barrier()

            A.top=ltop
            plw=A.bf(512)
            dma(plw,wb['poolw'][l],[('wb',)],[('plw',)])
            ptop=A.top
            TP=min(S,2048)
            for sg in (range(S//TP) if 'P' in PHASES else []):
                t0=sg*TP; n=TP+16
                for g in range(4):
                    A.top=ptop
                    u=A.f32(n); p0=A.f32(n); p1=A.f32(n); pab=A.bf(TP); ostg=A.bf(TP)
                    lo=max(t0-8,0); hi=min(t0+TP+8,S)
                    if t0-8<0: P.op('pool',lambda e,u=u:e.memset(u[:,0:8],0.0),w=[('u',)])
                    if t0+TP+8>S: P.op('pool',lambda e,u=u,n=n:e.memset(u[:,n-8:n],0.0),w=[('u',)])
                    dma(u[:,lo-(t0-8):hi-(t0-8)],ZP[g][:,lo:hi],[('ZP',)],[('u',)])
                    tt('pool',p0[:,1:n],u[:,0:n-1],u[:,1:n],ALU.add,[('u',)],[('p0',)])
                    cur,nxt,ck,nk=p0,p1,('p0',),('p1',)
                    for k in range(1,g+1):
                        d_=2**(k-1)
                        tt('pool',nxt[:,d_:n-d_],cur[:,0:n-2*d_],cur[:,2*d_:n],ALU.add,[ck],[nk])
                        cur,nxt,ck,nk=nxt,cur,nk,ck
                    wdw=POOL_WINDOWS[g]
                    act(nxt[:,8:8+TP],cur[:,8:8+TP],AF.Copy,[ck],[nk],scale=1.0/wdw)
                    if sg==0:
                        tt('dve',nxt[:,8:16],nxt[:,8:16],pedge[:,g*16:g*16+8],ALU.mult,[nk,('c','pedge')],[nk])
                    if sg==S//TP-1:
                        tt('dve',nxt[:,TP:TP+8],nxt[:,TP:TP+8],pedge[:,g*16+8:g*16+16],ALU.mult,[nk,('c','pedge')],[nk])
                    tt('dve',pab,nxt[:,8:8+TP],u[:,8:8+TP],ALU.subtract,[nk,('u',)],[('pab',)])
                    for tg in range(TP//512):
                        pi=mmrr.next()
                        mm(PB[pi][:,:],plw[:,g*128:(g+1)*128],pab[:,tg*512:(tg+1)*512],True,True,[('plw',),('pab',)],[pbk(pi)])
                        act(ostg[:,tg*512:(tg+1)*512],PB[pi][:,:],AF.Copy,[pbk(pi)],[('ostg',)],scale=pcol[:,15+g:16+g])
                    dma(AOT[:,g,t0:t0+TP],ostg,[('ostg',)],[('AOT',)])
            P.barrier()

            TQ=min(S,2048)
            attb5=attb.rearrange("p (a b c) -> p a b c",a=3,b=2)
            for sg in (range(S//TQ) if 'T' in PHASES else []):
                A.top=ltop; t0=sg*TQ; nqb=TQ//128; nkb=nqb+2
                q=A.bf(4*TQ).rearrange("p (g t) -> p g t",g=4)
                kt=A.bf(TQ+256)
                vp=[A.bf(nkb*128).rearrange("p (n c) -> p n c",c=128) for _ in range(2)]
                cstg=A.bf(4*TQ).rearrange("p (g t) -> p g t",g=4)
                sb=[A.f32(512) for _ in range(2)]; pT=[A.bf(512) for _ in range(2)]
                den=A.f32(512); rec=A.f32(512)
                dma(q,QT[:,:,t0:t0+TQ],[('QT',)],[('q',)])
                lo=max(t0-128,0); hi=min(t0+TQ+128,S)
                dma(kt[:,lo-(t0-128):hi-(t0-128)],KT[:,lo:hi],[('KT',)],[('kt',)])
                for kvh in range(2):
                    P.op('pool',lambda e,v=vp[kvh]:e.memset(v,0.0),w=[('vp',kvh)])
                    b0=(lo-(t0-128))//128; nb=(hi-lo)//128
                    for n0 in range(0,nb,4):
                        n1=min(n0+4,nb)
                        dma(vp[kvh][:,b0+n0:b0+n1,kvh*64:(kvh+1)*64],VTOK[lo+n0*128:lo+n1*128,kvh*64:(kvh+1)*64].rearrange("(n p) c -> p n c",p=128),[('VTOK',)],[('vp',kvh)])
                it=0
                for qb in range(nqb):
                    gq=sg*nqb+qb
                    lst=[(kvh,kbi) for kvh in range(2) for kbi in range(3) if 0<=gq+kbi-1<NT]
                    for idx,(kvh,kbi) in enumerate(lst):
                        pr=slice(kvh*64,(kvh+1)*64)
                        pi=mmrr.next(); si=it%2; it+=1
                        kb=qb+kbi
                        mm(PB[pi][:,:],kt[pr,kb*128:(kb+1)*128],q[pr,:,qb*128:(qb+1)*128],True,True,[('kt',),('q',)],[pbk(pi)])
                        stt('dve',sb[si],PB[pi][:,:],0.125,attb5[:,kbi,kvh,:],ALU.mult,ALU.add,[pbk(pi),('c','attb')],[('sb',si)])
                        act(pT[si],sb[si],AF.Exp,[('sb',si)],[('pT',si)])
                        mm(PB[5][:,:],vp[kvh][:,kb,:],pT[si],idx==0,idx==len(lst)-1,[('vp',kvh),('pT',si)],[pbk(5)])
                        mm(PB[6][:,:],onespad[:,kvh*128:(kvh+1)*128],pT[si],idx==0,idx==len(lst)-1,[('c','onespadb'),('pT',si)],[pbk(6)])
                    tt('dve',den,PB[6][:,:],esink,ALU.add,[pbk(6),('esink',)],[('den',)])
                    recip(rec,den,[('den',)],[('rec',)])
                    tt('dve',cstg[:,:,qb*128:(qb+1)*128],PB[5][:,:].rearrange("p (g t) -> p g t",g=4),rec.rearrange("p (g t) -> p g t",g=4),ALU.mult,[pbk(5),('rec',)],[('cstg',)])
                dma(COT[:,:,t0:t0+TQ],cstg,[('cstg',)],[('COT',)])
                P.barrier()

            TB=256; NBLK=S//TB; NTB=TB//128; NCK=TB//64
            NEG=-math.exp(-0.5)
            for d in ((1,0) if 'R' in PHASES else ()):
                A.top=ltop
                w2s=A.bf(512); a2s=A.bf(512); g2s=A.bf(512)
                dma(w2s,wb['w2'][l],[('wb',)],[('w2s',)]); dma(a2s,wb['a2'][l],[('wb',)],[('a2s',)]); dma(g2s,wb['g2'][l],[('wb',)],[('g2s',)])
                S32=[A.f32(128) for _ in range(4)]; Sbf=[A.bf(128) for _ in range(4)]
                for cc in range(4):
                    P.op('pool',lambda e,t=S32[cc]:e.memset(t,0.0),w=[('S32',cc)])
                    P.op('pool',lambda e,t=Sbf[cc]:e.memset(t,0.0),w=[('Sbf',cc)])
                rtop=A.top
                maskA=maskf if d==0 else maskb
                maskN=maskb[:,0:128] if d==0 else maskf[:,0:128]
                blks=range(NBLK) if d==0 else range(NBLK-1,-1,-1)
                for blk in blks:
                    A.top=rtop; t0=blk*TB
                    zraw=A.f32(TB+2); t1=A.f32(TB); t2=A.f32(TB)
                    th=A.bf(TB); al=A.bf(TB); gs=A.bf(TB)
                    zr=[A.f32(TB) for _ in range(4)]; zk=[A.f32(TB) for _ in range(4)]; zv=[A.f32(TB) for _ in range(4)]
                    def shift(c,out,okey):
                        lo=max(t0-1,0); hi=min(t0+TB+1,S)
                        if t0-1<0: mset(zraw[:,0:1],0.0,[('zraw',)])
                        if t0+TB+1>S: mset(zraw[:,TB+1:TB+2],0.0,[('zraw',)])
                        dma(zraw[:,lo-(t0-1):hi-(t0-1)],ZR[c][:,lo:hi],[('ZR',)],[('zraw',)])
                        tt('pool',t1,zraw[:,0:TB],zraw[:,2:TB+2],ALU.add,[('zraw',)],[('t1',)])
                        act(t2,zraw[:,1:TB+1],AF.Copy,[('zraw',),('omm',)],[('t2',)],scale=omm[:,c:c+1])
                        stt('dve',out,t1,hmu[:,c:c+1],t2,ALU.mult,ALU.add,[('t1',),('t2',),('hmu',)],[okey])
                    for cc in range(4):
                        shift(cc,zr[cc],('zr',cc)); shift(4+cc,zk[cc],('zk',cc)); shift(8+cc,zv[cc],('zv',cc))
                    zw=A.f32(TB); za=A.f32(TB); zg=A.f32(TB)
                    shift(12,zw,('zw',)); shift(13,za,('za',))
                    act(th,zw,AF.Tanh,[('zw',)],[('th',)])
                    ev_copy(al,za,[('za',)],[('al',)],'act')
                    if d==0:
                        shift(14,zg,('zg',))
                        act(gs,zg,AF.Sigmoid,[('zg',)],[('gs',)])
                    btop=A.top
                    AR=[A.bf(NTB*2*128).rearrange("p (n a t) -> p n a t",n=NTB,a=2) for _ in range(4)]
                    BTt=[A.bf(TB) for _ in range(4)]; KTt=[A.bf(TB) for _ in range(4)]
                    BH=[A.bf(TB).rearrange("p (n c) -> p n c",c=128) for _ in range(4)]
                    KH=[A.bf(TB).rearrange("p (n c) -> p n c",c=128) for _ in range(4)]
                    VT=[A.bf(TB).rearrange("p (n c) -> p n c",c=128) for _ in range(4)]
                    gL=[A.f32(NCK) for _ in range(4)]
                    YT=[A.f32(TB) for _ in range(4)]
                    if d==0:
                        bonus=[A.f32(TB) for _ in range(4)]; gg=[A.f32(TB) for _ in range(4)]
                    AM=[[[A.bf(512) for _ in range(2)] for _ in range(NTB)] for _ in range(4)]
                    TT=[[[A.bf(128) for _ in range(2)] for _ in range(NTB)] for _ in range(4)]
                    ppb=[[A.bf(256),A.bf(256)] for _ in range(8*NTB)]; ttb=[[A.bf(128),A.bf(128)] for _ in range(8*NTB)]
                    Xsb=[A.bf(128) for _ in range(4)]; Usb=[A.bf(128) for _ in range(4)]
                    pyb=A.f32(TB); pyc=A.f32(TB); psq=A.f32(TB); pob=A.bf(TB)
                    etop=A.top
                    for cc in range(4):
                        A.top=etop
                        lw=A.f32(TB); alp=A.f32(TB); kk=A.f32(TB); tA=A.f32(TB); tB=A.f32(TB); kd=A.f32(TB); bb=A.f32(TB)
                        cumf=A.f32(TB); cbx=A.f32(TB); cumb=A.f32(TB); cfx=A.f32(TB)
                        e1=A.f32(TB); e2=A.f32(TB); e3=A.f32(TB); e4=A.f32(TB)
                        bhf=A.bf(TB); khf=A.bf(TB); vbf=A.bf(TB)
                        K=lambda n:('r1',n)
                        dr=slice(d*64,(d+1)*64)
                        pi=mmrr.next()
                        mm(PB[pi][:,0:TB],w2s[dr,cc*128:(cc+1)*128],th[dr,:],True,True,[('w2s',),('th',)],[pbk(pi)])
                        act(lw,PB[pi][:,0:TB],AF.Sigmoid,[pbk(pi),('pcol',)],[K('lw')],bias=w0c(d,cc))
                        ts('dve',lw,lw,NEG,0.0,ALU.mult,ALU.add,[K('lw')],[K('lw')])
                        pi=mmrr.next()
                        mm(PB[pi][:,0:TB],a2s[dr,cc*128:(cc+1)*128],al[dr,:],True,True,[('a2s',),('al',)],[pbk(pi)])
                        act(alp,PB[pi][:,0:TB],AF.Sigmoid,[pbk(pi),('pcol',)],[K('alp')],bias=a0c(d,cc))
                        act(tA,zk[cc],AF.Copy,[('zk',cc),('pcol',)],[K('tA')],scale=pcol[:,35+cc:36+cc])
                        tt('pool',tB,tA,tA,ALU.mult,[K('tA')],[K('tB')])
                        pi=mmrr.next()
                        mm(PB[pi][:,0:TB],bonesf,tB,True,True,[('c','bones'),K('tB')],[pbk(pi)])
                        act(tB,PB[pi][:,0:TB],AF.Sqrt,[pbk(pi)],[K('tB')])
                        ts('dve',tB,tB,1e-12,0.0,ALU.max,ALU.add,[K('tB')],[K('tB')])
                        P.op('dve',lambda e,tB=tB:e.reciprocal(out=tB,in_=tB),r=[K('tB')],w=[K('tB')])
                        tt('pool',kk,tA,tB,ALU.mult,[K('tA'),K('tB')],[K('kk')])
                        ts('dve',tA,alp,pcol[:,39+cc:40+cc],omka[:,cc:cc+1],ALU.mult,ALU.add,[K('alp'),('pcol',),('omka',)],[K('tA')])
                        tt('pool',kd,zk[cc],tA,ALU.mult,[('zk',cc),K('tA')],[K('kd')])
                        tt('pool',bb,kk,alp,ALU.mult,[K('kk'),K('alp')],[K('bb')])
                        P.op('dve',lambda e,cumf=cumf,lw=lw:e.tensor_tensor_scan(out=cumf,data0=restart[:,0:TB],data1=lw,initial=0.0,op0=ALU.mult,op1=ALU.add),r=[('c','restart'),K('lw')],w=[K('cumf')])
                        c3=lambda t:t.rearrange("p (c l) -> p c l",l=64)
                        tt('dve',c3(cbx),c3(cumf)[:,:,63:64].to_broadcast([128,NCK,64]),c3(cumf),ALU.subtract,[K('cumf')],[K('cbx')])
                        tt('pool',cumb,cbx,lw,ALU.add,[K('cbx'),K('lw')],[K('cumb')])
                        tt('pool',cfx,cumf,lw,ALU.subtract,[K('cumf'),K('lw')],[K('cfx')])
                        act(gL[cc],c3(cumf)[:,:,63],AF.Exp,[K('cumf')],[('gL',cc)])
                        ci,cx,co=(cumf,cfx,cbx) if d==0 else (cumb,cbx,cfx)
                        kci,kcx,kco=(K('cumf'),K('cfx'),K('cbx')) if d==0 else (K('cumb'),K('cbx'),K('cfx'))
                        act(e1,ci,AF.Exp,[kci],[K('e1')])
                        act(e2,ci,AF.Exp,[kci],[K('e2')],scale=-1.0)
                        act(e3,cx,AF.Exp,[kcx],[K('e3')])
                        act(e4,co,AF.Exp,[kco],[K('e4')])
                        v4=lambda t:t.rearrange("p (n t) -> p n t",t=128)
                        stt('dve',AR[cc][:,:,0,:],v4(kk),-1.0,v4(e3),ALU.mult,ALU.mult,[K('kk'),K('e3')],[('AR',cc)])
                        tt('pool',AR[cc][:,:,1,:],v4(zr[cc]),v4(e1),ALU.mult,[('zr',cc),K('e1')],[('AR',cc)])
                        tt('dve',BTt[cc],bb,e2,ALU.mult,[K('bb'),K('e2')],[('BTt',cc)])
                        tt('pool',KTt[cc],kd,e2,ALU.mult,[K('kd'),K('e2')],[('KTt',cc)])
                        tt('dve',bhf,bb,e4,ALU.mult,[K('bb'),K('e4')],[K('bhf')])
                        tt('pool',khf,kd,e4,ALU.mult,[K('kd'),K('e4')],[K('khf')])
                        ev_copy(vbf,zv[cc],[('zv',cc)],[K('vbf')],'act')
                        for src,skey,dstt,dkey in ((bhf,K('bhf'),BH[cc],('BH',cc)),(khf,K('khf'),KH[cc],('KH',cc)),(vbf,K('vbf'),VT[cc],('VT',cc))):
                            pi=tprr.next()
                            for ti in range(NTB):
                                tr(TPB(pi)[:,ti*128:(ti+1)*128],src[:,ti*128:(ti+1)*128],identb,[skey,('c','identb')],[pbk(pi)])
                            ev_copy(dstt,TPB(pi)[:,0:TB].rearrange("p (n c) -> p n c",c=128),[pbk(pi)],[dkey])
                        if d==0:
                            orr=slice(64,128)
                            pi=mmrr.next()
                            mm(PB[pi][:,0:TB],a2s[orr,cc*128:(cc+1)*128],al[orr,:],True,True,[('a2s',),('al',)],[pbk(pi)])
                            act(tA,PB[pi][:,0:TB],AF.Sigmoid,[pbk(pi),('pcol',)],[K('tA')],bias=a0c(1,cc))
                            ts('dve',tA,tA,pcol[:,39+cc:40+cc],omka[:,cc:cc+1],ALU.mult,ALU.add,[K('tA'),('pcol',),('omka',)],[K('tA')])
                            tt('pool',tA,zk[cc],tA,ALU.mult,[('zk',cc),K('tA')],[K('tA')])
                            tt('pool',tA,tA,kd,ALU.add,[K('tA'),K('kd')],[K('tA')])
                            stt('dve',bhf,zr[cc],rkh[:,cc:cc+1],tA,ALU.mult,ALU.mult,[('zr',cc),('rkh',),K('tA'),K('bhf')],[K('bhf')])
                            pi=mmrr.next()
                            mm(PB[pi][:,0:TB],bonesb,bhf,True,True,[('c','bonesb'),K('bhf')],[pbk(pi)])
                            tt('dve',bonus[cc],PB[pi][:,0:TB],zv[cc],ALU.mult,[pbk(pi),('zv',cc)],[('bonus',cc)])
                            pi=mmrr.next()
                            mm(PB[pi][:,0:TB],g2s[:,cc*128:(cc+1)*128],gs,True,True,[('g2s',),('gs',)],[pbk(pi)])
                            ev_copy(gg[cc],PB[pi][:,0:TB],[pbk(pi)],[('gg',cc)],'act')
                    insts=[(cc,ti,hh) for cc in range(4) for ti in range(NTB) for hh in range(2)]
                    cur={}
                    for ii,(cc,ti,hh) in enumerate(insts):
                        pr=slice(hh*64,(hh+1)*64)
                        am=AM[cc][ti][hh]; kam=('AM',cc,ti,hh)
                        pi=mmrr.next()
                        mm(PB[pi][:,0:256],BTt[cc][pr,ti*128:(ti+1)*128],AR[cc][pr,ti,:,:],True,True,[('BTt',cc),('AR',cc)],[pbk(pi)])
                        mm(PB[pi][:,256:512],KTt[cc][pr,ti*128:(ti+1)*128],AR[cc][pr,ti,:,:],True,True,[('KTt',cc),('AR',cc)],[pbk(pi)])
                        tt('dve',am,PB[pi][:,:],maskA,ALU.mult,[pbk(pi),('c','maskf'),('c','maskb')],[kam])
                        pi=mmrr.next()
                        mm(PB[pi][:,0:128],AR[cc][pr,ti,0,:],BTt[cc][pr,ti*128:(ti+1)*128],True,True,[('BTt',cc),('AR',cc)],[pbk(pi)])
                        tt('dve',ppb[ii][0][:,0:128],PB[pi][:,0:128],maskN,ALU.mult,[pbk(pi),('c','maskf'),('c','maskb')],[('pp',ii,0)])
                        ev_copy(ppb[ii][0][:,128:256],am[:,0:128],[kam],[('pp',ii,0)],'pool')
                        tt('pool',ttb[ii][0],identb,am[:,0:128],ALU.add,[('c','identb'),kam],[('ttx',ii,0)])
                        cur[ii]=0
                    for g0 in range(0,len(insts),8):
                        for k in range(1,6):
                            for ii,(cc,ti,hh) in list(enumerate(insts))[g0:g0+8]:
                                c_=cur[ii]; src=ppb[ii][c_]; ks=('pp',ii,c_)
                                mm(PB[ii%8][:,0:128],src[:,128:256],src[:,0:128],True,True,[ks],[pbk(ii%8)])
                                if k<5:
                                    mm(PB[ii%8][:,128:256],src[:,0:128],src[:,128:256],True,True,[ks],[pbk(ii%8)])
                            for ii,(cc,ti,hh) in list(enumerate(insts))[g0:g0+8]:
                                c_=cur[ii]; dst=ppb[ii][1-c_]; kd_=('pp',ii,1-c_)
                                w_=256 if k<5 else 128
                                ev_copy(dst[:,0:w_],PB[ii%8][:,0:w_],[pbk(ii%8)],[kd_],'act')
                            for ii,(cc,ti,hh) in list(enumerate(insts))[g0:g0+8]:
                                c_=cur[ii]; dst=ppb[ii][1-c_]; kd_=('pp',ii,1-c_)
                                mm(PB[ii%8][:,256:384],dst[:,0:128],ttb[ii][c_],True,True,[kd_,('ttx',ii,c_)],[pbk(ii%8)])
                            for ii,(cc,ti,hh) in list(enumerate(insts))[g0:g0+8]:
                                c_=cur[ii]
                                outt=TT[cc][ti][hh] if k==5 else ttb[ii][1-c_]
                                okey=('TT',cc,ti,hh) if k==5 else ('ttx',ii,1-c_)
                                tt('dve',outt,PB[ii%8][:,256:384],ttb[ii][c_],ALU.add,[pbk(ii%8),('ttx',ii,c_)],[okey])
                                cur[ii]=1-c_
                    chunks=range(NCK) if d==0 else range(NCK-1,-1,-1)
                    bX=(5,1); bU=(6,2); bY=(7,3); bS=(4,0)
                    for ch in chunks:
                        ti=ch//2; half=ch%2
                        tb=slice(half*64,(half+1)*64); cs=slice(half*64,(half+1)*64)
                        for cc in range(4):
                            b=bX[cc%2]
                            mm(PB[b][tb,0:128],AR[cc][:,ti,0,cs],Sbf[cc],True,False,[('AR',cc),('Sbf',cc)],[pbk(b)])
                            for hh in range(2):
                                hs=slice(hh*64,(hh+1)*64)
                                mm(PB[b][tb,hs],AM[cc][ti][hh][tb,256+half*64:256+half*64+64],VT[cc][tb,ti,hs],False,hh==1,[('AM',cc,ti,hh),('VT',cc)],[pbk(b)])
                            ev_copy(Xsb[cc][tb,:],PB[b][tb,0:128],[pbk(b)],[('Xsb',cc)],'act' if cc%2==0 else 'dve')
                        for cc in range(4):
                            b=bU[cc%2]
                            for hh in range(2):
                                hs=slice(hh*64,(hh+1)*64)
                                mm(PB[b][tb,hs],TT[cc][ti][hh][tb,cs],Xsb[cc][tb,hs],True,True,[('TT',cc,ti,hh),('Xsb',cc)],[pbk(b)])
                            ev_copy(Usb[cc][tb,:],PB[b][tb,0:128],[pbk(b)],[('Usb',cc)],'dve' if cc%2==0 else 'act')
                        for cc in range(4):
                            b=bY[cc%2]
                            mm(PB[b][:,0:64],Sbf[cc],AR[cc][:,ti,1,cs],True,False,[('Sbf',cc),('AR',cc)],[pbk(b)])
                            for hh in range(2):
                                hs=slice(hh*64,(hh+1)*64)
                                mm(PB[b][hs,0:64],Usb[cc][tb,hs],AM[cc][ti][hh][tb,128+half*64:128+half*64+64],False,False,[('Usb',cc),('AM',cc,ti,hh)],[pbk(b)])
                                mm(PB[b][hs,0:64],VT[cc][tb,ti,hs],AM[cc][ti][hh][tb,384+half*64:384+half*64+64],False,hh==1,[('VT',cc),('AM',cc,ti,hh)],[pbk(b)])
                            ev_copy(YT[cc][:,ch*64:(ch+1)*64],PB[b][:,0:64],[pbk(b)],[('YT',cc)],'act')
                            b=bS[cc%2]
                            mm(PB[b][:,0:128],BH[cc][tb,ti,:],Usb[cc][tb,:],True,False,[('BH',cc),('Usb',cc)],[pbk(b)])
                            mm(PB[b][:,0:128],KH[cc][tb,ti,:],VT[cc][tb,ti,:],False,True,[('KH',cc),('VT',cc)],[pbk(b)])
                            for hh in range(2):
                                hs=slice(hh*64,(hh+1)*64)
                                stt('dve',S32[cc][hs,hs],S32[cc][hs,hs],gL[cc][hs,ch:ch+1],PB[b][hs,hs],ALU.mult,ALU.add,[('S32',cc),('gL',cc),pbk(b)],[('S32',cc)])
                            ev_copy(Sbf[cc],S32[cc],[('S32',cc)],[('Sbf',cc)],'act')
                    for cc in range(4):
                        if d==1:
                            dma(YB[cc][:,t0:t0+TB],YT[cc],[('YT',cc)],[('YB',)])
                        else:
                            yb,yc,sq,ob=pyb,pyc,psq,pob
                            K=lambda n:('po',n)
                            dma(yb,YB[cc][:,t0:t0+TB],[('YB',)],[K('yb')])
                            tt('pool',yb,yb,YT[cc],ALU.add,[K('yb'),('YT',cc)],[K('yb')])
                            pi=mmrr.next()
                            mm(PB[pi][:,0:TB],bonesf,yb,True,True,[('c','bones'),K('yb')],[pbk(pi)])
                            stt('dve',yc,PB[pi][:,0:TB],-1.0/64,yb,ALU.mult,ALU.add,[pbk(pi),K('yb')],[K('yc')])
                            tt('pool',sq,yc,yc,ALU.mult,[K('yc')],[K('sq')])
                            pi=mmrr.next()
                            mm(PB[pi][:,0:TB],bonesf,sq,True,True,[('c','bones'),K('sq')],[pbk(pi)])
                            act(sq,PB[pi][:,0:TB],AF.Sqrt,[pbk(pi)],[K('sq')],bias=GN_EPS,scale=1.0/64)
                            P.op('dve',lambda e,sq=sq:e.reciprocal(out=sq,in_=sq),r=[K('sq')],w=[K('sq')])
                            tt('pool',yc,yc,sq,ALU.mult,[K('yc'),K('sq')],[K('yc')])
                            ts('dve',yc,yc,pcol[:,47+cc:48+cc],pcol[:,51+cc:52+cc],ALU.mult,ALU.add,[K('yc'),('pcol',)],[K('yc')])
                            tt('pool',yc,yc,bonus[cc],ALU.add,[K('yc'),('bonus',cc)],[K('yc')])
                            tt('dve',ob,yc,gg[cc],ALU.mult,[K('yc'),('gg',cc)],[K('ob')])
                            dma(BOT[:,cc,t0:t0+TB],ob,[K('ob')],[('BOT',)])
                    if blk==blks[-1]: P.barrier()

            TC=512
            for cg in (range(S//TC) if 'C' in PHASES else []):
                A.top=ltop; t0=cg*TC
                X1=A.f32(4*1024).rearrange("p (n d) -> p n d",n=4)
                X1T=A.bf(8*TC).rearrange("p (k t) -> p k t",k=8)
                c1top=A.top
                MT=A.bf(8*TC).rearrange("p (k t) -> p k t",k=8)
                XO=[A.bf(4*TC).rearrange("p (k t) -> p k t",k=4) for _ in range(3)]
                wbrs=A.bf(12288).rearrange("p (b k d) -> p b k d",b=3,k=4)
                wos=A.bf(8192).rearrange("p (k d) -> p k d",k=8)
                gt=[A.bf(TC) for _ in range(3)]
                acc=A.f32(TC); tmp=A.f32(TC)
                xs=[A.f32(1024) for _ in range(2)]; ypre=A.f32(1024); xbb=A.bf(1024)
                st1=A.f32(4); junk=A.f32(1024)
                for b,src in enumerate((AOT,BOT,COT)):
                    dma(XO[b],src[:,:,t0:t0+TC],[('AOT',),('BOT',),('COT',)],[('XO',b)])
                dma(wbrs.rearrange("p b k d -> p (b k d)"),wb['wbr'][l],[('wb',)],[('wbrs',)])
                dma(wos.rearrange("p k d -> p (k d)"),wb['wo'][l],[('wb',)],[('wos',)])
                for m in range(8):
                    for b in range(3):
                        dma(gt[b],GT[b*8+m][:,t0:t0+TC],[('GT',)],[('gt',b)])
                        pi=mmrr.next()
                        for k in range(4):
                            mm(PB[pi][:,:],wbrs[:,b,k,m*128:(m+1)*128],XO[b][:,k,:],k==0,k==3,[('wbrs',),('XO',b)],[pbk(pi)])
                        if b==0:
                            tt('dve',acc,PB[pi][:,:],gt[b],ALU.mult,[pbk(pi),('gt',b)],[('acc',)])
                        else:
                            tt('dve',tmp,PB[pi][:,:],gt[b],ALU.mult,[pbk(pi),('gt',b)],[('tmp',)])
                            if b==1: tt('pool',acc,acc,tmp,ALU.add,[('acc',),('tmp',)],[('acc',)])
                            else: tt('pool',MT[:,m,:],acc,tmp,ALU.add,[('acc',),('tmp',)],[('MT',)])
                def layer_norm(gi,out_ap,okey):
                    red(st1[:,0:1],ypre,[('ypre',)],[('st1',)])
                    ts('dve',st1[:,1:2],st1[:,0:1],-1.0/1024,0.0,ALU.mult,ALU.add,[('st1',)],[('st1',)])
                    ts('dve',ypre,ypre,st1[:,1:2],0.0,ALU.add,ALU.add,[('ypre',),('st1',)],[('ypre',)])
                    tt('pool',junk,ypre,ypre,ALU.mult,[('ypre',)],[('junk',)])
                    red(st1[:,2:3],junk,[('junk',)],[('st1',)])
                    act(st1[:,3:4],st1[:,2:3],AF.Sqrt,[('st1',)],[('st1',)],bias=LN_EPS,scale=1.0/1024)
                    recip(st1[:,3:4],st1[:,3:4],[('st1',)],[('st1',)])
                    stt('dve',ypre,ypre,st1[:,3:4],lnrow[gi],ALU.mult,ALU.mult,[('ypre',),('st1',),('lnrow',gi)],[('ypre',)])
                    tt('pool',out_ap,ypre,lnrow[gi+1],ALU.add,[('ypre',),('lnrow',gi+1)],[okey])
                for ti in range(TC//128):
                    i=ti%2
                    dma(xs[i],xsrc[t0+ti*128:t0+(ti+1)*128,:],[('xsrc',)],[('xs',i)])
                    for half in range(2):
                        pi=mmrr.next()
                        for k in range(8):
                            mm(PB[pi][:,:],MT[:,k,ti*128:(ti+1)*128],wos[:,k,half*512:(half+1)*512],k==0,k==7,[('MT',),('wos',)],[pbk(pi)])
                        stt('dve',ypre[:,half*512:(half+1)*512],xs[i][:,half*512:(half+1)*512],ALPHA,PB[pi][:,:],ALU.mult,ALU.add,[('xs',i),pbk(pi)],[('ypre',)])
                    layer_norm(0,X1[:,ti,:],('X1',))
                    ev_copy(xbb,X1[:,ti,:],[('X1',)],[('xbb',)],'act')
                    pi=tprr.next()
                    for k in range(8):
                        tr(TPB(pi)[:,k*128:(k+1)*128],xbb[:,k*128:(k+1)*128],identb,[('xbb',),('c','identb')],[pbk(pi)])
                    ev_copy(X1T[:,:,ti*128:(ti+1)*128],TPB(pi).rearrange("p (k t) -> p k t",k=8),[pbk(pi)],[('X1T',)],'dve')
                P.barrier()
                A.top=c1top
                H=A.bf(32*TC).rearrange("p (k t) -> p k t",k=32)
                w2f=A.bf(32768).rearrange("p (k d) -> p k d",k=32)
                w1c=[A.bf(1024) for _ in range(3)]
                rl=[A.f32(TC) for _ in range(2)]
                ypre=A.f32(1024); st1=A.f32(4); junk=A.f32(1024); ot=[A.f32(1024) for _ in range(2)]
                dma(w2f.rearrange("p k d -> p (k d)"),wb['ff2'][l],[('wb',)],[('w2f',)])
                for m in range(32):
                    i=m%3; ri=m%2
                    dma(w1c[i],wb['ff1'][l,m],[('wb',)],[('w1c',i)])
                    pi=mmrr.next()
                    for k in range(8):
                        mm(PB[pi][:,:],w1c[i][:,k*128:(k+1)*128],X1T[:,k,:],k==0,k==7,[('w1c',i),('X1T',)],[pbk(pi)])
                    act(rl[ri],PB[pi][:,:],AF.Relu,[pbk(pi)],[('rl',ri)])
                    tt('pool' if m%2 else 'dve',H[:,m,:],rl[ri],rl[ri],ALU.mult,[('rl',ri)],[('H',)])
                for ti in range(TC//128):
                    for half in range(2):
                        pi=mmrr.next()
                        for k in range(32):
                            mm(PB[pi][:,:],H[:,k,ti*128:(ti+1)*128],w2f[:,k,half*512:(half+1)*512],k==0,k==31,[('H',),('w2f',)],[pbk(pi)])
                        stt('dve',ypre[:,half*512:(half+1)*512],X1[:,ti,half*512:(half+1)*512],ALPHA,PB[pi][:,:],ALU.mult,ALU.add,[('X1',),pbk(pi)],[('ypre',)])
                    layer_norm(2,ot[ti%2],('ot',ti%2))
                    dma(xdst[t0+ti*128:t0+(ti+1)*128,:],ot[ti%2],[('ot',ti%2)],[('xdst',)])
                P.barrier()
    P.run()
    for cm in reversed(pbcm): cm.__exit__(None,None,None)
    return nc

N_CORES=8
_CACHE={}
def _run(job_S, depth, per_core_x, inp, debug=False):
    key=(tuple(job_S),depth)
    nc=build(job_S,depth,debug)
    consts=host_consts(np.asarray(inp['rel_bias'],np.float32))
    wts=host_weights({k:np.asarray(v,np.float32) for k,v in inp.items() if k not in ('x_prompt','x_sample')},depth)
    in_maps=[]
    for c in range(len(per_core_x)):
        m={}
        m.update(consts); m.update(wts)
        for j,x in enumerate(per_core_x[c]): m[f"x{j}"]=np.ascontiguousarray(x,dtype=np.float32)
        in_maps.append(m)
    res=run_bass_kernel_spmd(nc,in_maps,core_ids=list(range(len(per_core_x))))
    return res.results

def kernel(**inputs):
    xp=np.asarray(inputs['x_prompt'],np.float32); xs=np.asarray(inputs['x_sample'],np.float32)
    job_S=[2048,2048,16384]
    per_core=[]
    zero=np.zeros((16384,D),np.float32)
    for c in range(N_CORES):
        per_core.append([xp[2*c],xp[2*c+1],xs[c] if c<2 else zero])
    res=_run(job_S,4,per_core,inputs)
    yp=np.stack([res[c][f"y{j}"] for c in range(N_CORES) for j in range(2)],0).astype(np.float32)
    ys=np.stack([res[c]["y2"] for c in range(2)],0).astype(np.float32)
    return (yp,ys)
```
